# Optimizing a Trainium2 kernel written in Bass

```python
import jax
import jax.numpy as jnp
from jax import lax
import numpy as np

D_MODEL = 1024
BATCH = 8
SEQ = 4096
DEPTH = 1

CHUNK = 64
Q_BLOCK = 128
N_MEM = 256
EPS = 1e-6
SB_HD = 128
SB_HEADS = D_MODEL // SB_HD
SB_W = SB_HEADS * SB_HD
ML_HEADS = 4
ML_HD = D_MODEL // ML_HEADS
ML_W = ML_HEADS * ML_HD
X_HEADS = 4
X_HD = D_MODEL // X_HEADS
X_W = X_HEADS * X_HD
CONV_W = 4
D_FF = 4 * D_MODEL
N_BRANCH = 3
IN_SIZES = (SB_W, SB_W, SB_W, ML_W, ML_W, ML_W, ML_W, ML_HEADS, ML_HEADS, X_W, N_BRANCH * D_MODEL)
IN_SPLITS = tuple(int(v) for v in np.cumsum(IN_SIZES)[:-1])
N_IN = int(sum(IN_SIZES))

kernel_name = "hybrid_sb_mlstm_xattn_block"


def _rmsnorm(x, g):
    xf = x.astype(jnp.float32)
    y = xf * lax.rsqrt(jnp.mean(xf * xf, axis=-1, keepdims=True) + EPS)
    return (y * g.astype(jnp.float32)).astype(x.dtype)


def _split_heads(t, n):
    b, s, _ = t.shape
    return t.reshape(b, s, n, -1).transpose(0, 2, 1, 3)


def _merge_heads(t):
    b, h, s, d = t.shape
    return t.transpose(0, 2, 1, 3).reshape(b, s, h * d)


def _causal_conv(u, w, b):
    s = u.shape[1]
    up = jnp.pad(u, ((0, 0), (CONV_W - 1, 0), (0, 0)))
    return sum((up[:, j:j + s, :] * w[j] for j in range(CONV_W)), b)


def _stick_breaking(q, k, v):
    _, _, s_len, dh = q.shape
    scale = dh ** -0.5
    kf = k.astype(jnp.float32)
    vf = v.astype(jnp.float32)
    outs = []
    for blk in range(s_len // Q_BLOCK):
        start, end = blk * Q_BLOCK, (blk + 1) * Q_BLOCK
        qb = q[:, :, start:end].astype(jnp.float32)
        z = jnp.einsum('bhqd,bhkd->bhqk', qb, kf[:, :, :end]) * scale
        q_pos = start + jnp.arange(Q_BLOCK)
        k_pos = jnp.arange(end)
        causal = k_pos[None, :] < q_pos[:, None]
        log_1m = jnp.where(causal, jax.nn.log_sigmoid(-z), 0.0)
        later = lax.cumsum(log_1m, axis=3, reverse=True) - log_1m
        log_a = jnp.where(causal, jax.nn.log_sigmoid(z) + later, -jnp.inf)
        outs.append(jnp.einsum('bhqk,bhkd->bhqd', jnp.exp(log_a), vf[:, :, :end]))
    return jnp.concatenate(outs, axis=2)


def _mlstm(q, k, v, i_pre, f_pre):
    f32 = jnp.float32
    b, h, s_len, dh = q.shape
    nc = s_len // CHUNK
    q = q.astype(f32)
    k = k.astype(f32) * (dh ** -0.5)
    v = v.astype(f32)
    i_pre = i_pre.astype(f32)
    log_f = jax.nn.log_sigmoid(f_pre.astype(f32))

    def chunks(t):
        return jnp.moveaxis(t.reshape(b, h, nc, CHUNK, *t.shape[3:]), 2, 0)

    tri = jnp.tril(jnp.ones((CHUNK, CHUNK), dtype=bool))

    def step(carry, xs):
        c_prev, n_prev, m_prev = carry
        qc, kc, vc, ic, lfc = xs
        bcum = jnp.cumsum(lfc, axis=-1)
        d = jnp.where(tri, bcum[..., :, None] - bcum[..., None, :] + ic[..., None, :], -jnp.inf)
        m_inter = bcum + m_prev[..., None]
        m_t = jnp.maximum(m_inter, jnp.max(d, axis=-1))
        w = jnp.exp(d - m_t[..., None])
        s_inter = jnp.exp(m_inter - m_t)
        sc = jnp.einsum('bhtd,bhsd->bhts', qc, kc) * w
        num = jnp.einsum('bhts,bhsd->bhtd', sc, vc) + s_inter[..., None] * jnp.einsum('bhvk,bhtk->bhtv', c_prev, qc)
        den = jnp.sum(sc, axis=-1) + s_inter * jnp.einsum('bhtk,bhk->bht', qc, n_prev)
        h_c = num / jnp.maximum(jnp.abs(den), jnp.exp(-m_t))[..., None]
        b_end = bcum[..., -1]
        g = b_end[..., None] - bcum + ic
        m_new = jnp.maximum(b_end + m_prev, jnp.max(g, axis=-1))
        decay = jnp.exp(b_end + m_prev - m_new)
        wk = jnp.exp(g - m_new[..., None])
        c_new = decay[..., None, None] * c_prev + jnp.einsum('bhsv,bhsk->bhvk', vc * wk[..., None], kc)
        n_new = decay[..., None] * n_prev + jnp.einsum('bhs,bhsk->bhk', wk, kc)
        return (c_new, n_new, m_new), h_c

    init = (jnp.zeros((b, h, dh, dh), f32), jnp.zeros((b, h, dh), f32), jnp.zeros((b, h), f32))
    _, hs = lax.scan(step, init, (chunks(q), chunks(k), chunks(v), chunks(i_pre), chunks(log_f)))
    return jnp.moveaxis(hs, 0, 2).reshape(b, h, s_len, dh)


def _cross_attention(q, mk, mv, gq, gk):
    dh = q.shape[-1]
    qn = _rmsnorm(q, gq)
    kn = _rmsnorm(mk, gk)
    logits = jnp.einsum('bhsd,bhmd->bhsm', qn, kn).astype(jnp.float32) * (dh ** -0.5)
    p = jax.nn.softmax(logits, axis=-1)
    return jnp.einsum('bhsm,bhmd->bhsd', p.astype(mv.dtype), mv)


def setup_inputs(seed: int = 0) -> dict:
    key = jax.random.key(seed)
    ks = jax.random.split(key, 24)
    f32 = jnp.float32
    L = DEPTH

    def nrm(k, shape, scale):
        return jax.random.normal(k, shape, f32) * scale

    def gain(k, shape):
        return 1.0 + 0.02 * jax.random.normal(k, shape, f32)

    b_i = nrm(ks[4], (L, ML_HEADS), 0.1)
    b_f = jnp.linspace(3.0, 6.0, ML_HEADS, dtype=f32)[None, :] + nrm(ks[5], (L, ML_HEADS), 0.1)
    return {
        "x": nrm(ks[0], (BATCH, SEQ, D_MODEL), 1.0),
        "mem": nrm(ks[1], (BATCH, N_MEM, D_MODEL), 1.0),
        "g_mix": gain(ks[2], (L, D_MODEL)),
        "w_in": nrm(ks[3], (L, D_MODEL, N_IN), D_MODEL ** -0.5),
        "b_if": jnp.concatenate([b_i, b_f], axis=-1),
        "b_gate": nrm(ks[6], (L, N_BRANCH * D_MODEL), 0.02),
        "conv_w": nrm(ks[7], (L, CONV_W, 2 * ML_W), CONV_W ** -0.5),
        "conv_b": nrm(ks[8], (L, 2 * ML_W), 0.02),
        "ml_norm_g": gain(ks[9], (L, ML_W)),
        "g_mem": gain(ks[10], (L, D_MODEL)),
        "w_mem_kv": nrm(ks[11], (L, D_MODEL, 2 * X_W), D_MODEL ** -0.5),
        "q_norm_g": gain(ks[12], (L, X_HD)),
        "k_norm_g": gain(ks[13], (L, X_HD)),
        "w_sb_proj": nrm(ks[14], (L, SB_W, D_MODEL), SB_W ** -0.5),
        "w_ml_proj": nrm(ks[15], (L, ML_W, D_MODEL), ML_W ** -0.5),
        "w_x_proj": nrm(ks[16], (L, X_W, D_MODEL), X_W ** -0.5),
        "w_out": nrm(ks[17], (L, D_MODEL, D_MODEL), D_MODEL ** -0.5),
        "g_mlp": gain(ks[18], (L, D_MODEL)),
        "w_ff1": nrm(ks[19], (L, D_MODEL, D_FF), D_MODEL ** -0.5),
        "w_ff2": nrm(ks[20], (L, D_FF, D_MODEL), D_FF ** -0.5),
    }


def reference(x, mem, g_mix, w_in, b_if, b_gate, conv_w, conv_b, ml_norm_g, g_mem, w_mem_kv,
              q_norm_g, k_norm_g, w_sb_proj, w_ml_proj, w_x_proj, w_out, g_mlp, w_ff1, w_ff2):
    dt = x.dtype
    bsz, seq, _ = x.shape
    for l in range(DEPTH):
        h = _rmsnorm(x, g_mix[l])
        z = h @ w_in[l]
        (sb_q, sb_k, sb_v, ml_q, ml_k, ml_v, ml_o, ml_i, ml_f, x_q, gate_pre) = jnp.split(z, IN_SPLITS, axis=-1)

        y_sb = _merge_heads(_stick_breaking(_split_heads(sb_q, SB_HEADS), _split_heads(sb_k, SB_HEADS),
                                            _split_heads(sb_v, SB_HEADS))).astype(dt)

        qk = jax.nn.silu(_causal_conv(jnp.concatenate([ml_q, ml_k], axis=-1), conv_w[l], conv_b[l]))
        mq, mk = jnp.split(qk, 2, axis=-1)
        i_pre = (ml_i + b_if[l, :ML_HEADS]).transpose(0, 2, 1)
        f_pre = (ml_f + b_if[l, ML_HEADS:]).transpose(0, 2, 1)
        hm = _mlstm(_split_heads(mq, ML_HEADS), _split_heads(mk, ML_HEADS), _split_heads(ml_v, ML_HEADS), i_pre, f_pre)
        hm = _rmsnorm(hm, ml_norm_g[l].reshape(ML_HEADS, 1, ML_HD))
        y_ml = (_merge_heads(hm) * jax.nn.sigmoid(ml_o.astype(jnp.float32))).astype(dt)

        kv = _rmsnorm(mem, g_mem[l]) @ w_mem_kv[l]
        mem_k, mem_v = jnp.split(kv, 2, axis=-1)
        y_x = _merge_heads(_cross_attention(_split_heads(x_q, X_HEADS), _split_heads(mem_k, X_HEADS),
                                            _split_heads(mem_v, X_HEADS), q_norm_g[l], k_norm_g[l])).astype(dt)

        gates = jax.nn.sigmoid(gate_pre + b_gate[l]).reshape(bsz, seq, N_BRANCH, D_MODEL)
        mixed = (gates[:, :, 0] * (y_sb @ w_sb_proj[l])
                 + gates[:, :, 1] * (y_ml @ w_ml_proj[l])
                 + gates[:, :, 2] * (y_x @ w_x_proj[l]))
        x = x + mixed @ w_out[l]

        u = _rmsnorm(x, g_mlp[l]) @ w_ff1[l]
        x = x + jnp.square(jax.nn.relu(u)) @ w_ff2[l]
    return x
```

```python
from contextlib import ExitStack
import numpy as np
import concourse.bass as bass
import concourse.mybir as mybir
from concourse.bass_utils import run_bass_kernel_spmd

F32 = mybir.dt.float32
BF16 = mybir.dt.bfloat16
AF = mybir.ActivationFunctionType
ALU = mybir.AluOpType
AX = mybir.AxisListType

S_LEN = 4096
D = 1024
NTB = 32
NTT = 8
N_IN = 11272
N_MEM = 256
EPS = 1e-6
LN16 = float(np.log(16.0))
NDUMMY = 1


class SemObj:
    def __init__(self, h, name):
        self.h = h
        self.name = name
        self.count = 0
        self.is_dma = name.startswith("d_")


class Buf:
    __slots__ = ("name", "w", "r", "psum")

    def __init__(self, name="", psum=False):
        self.name = name
        self.w = None
        self.r = []
        self.psum = psum


class Eng:
    def __init__(self, name, e, sem, inorder_self):
        self.name = name
        self.e = e
        self.sem = sem
        self.seen = {}
        self.inorder_self = inorder_self
        self.dangling = False


class Sched:
    def __init__(self, nc, stack, self_sync=True):
        self.nc = nc
        self.stack = stack
        self.sems = []
        mk = self.new_sem
        self.pe = Eng("pe", nc.tensor, mk("s_pe"), True)
        self.act = Eng("act", nc.scalar, mk("s_act"), not self_sync)
        self.dve = Eng("dve", nc.vector, mk("s_dve"), not self_sync)
        self.pool = Eng("pool", nc.gpsimd, mk("s_pool"), not self_sync)
        self.sp = Eng("sp", nc.sync, mk("s_sp"), True)
        self.engs = [self.pe, self.act, self.dve, self.pool, self.sp]
        self.out_events = []
        self.nops = 0
        self.trace = {e.name: [] for e in self.engs}

    def new_sem(self, name):
        h = self.stack.enter_context(self.nc.semaphore(name))
        s = SemObj(h, name)
        self.sems.append(s)
        return s

    def _waits(self, eng, reads, writes):
        need = {}

        def add(ev):
            if ev is None:
                return
            s, v = ev
            if need.get(s, 0) < v:
                need[s] = v
        for b in reads:
            add(b.w)
            if b.psum:
                for ev in b.r:
                    if ev[0] is not eng.sem:
                        add(ev)
        for b in writes:
            add(b.w)
            for ev in b.r:
                add(ev)
        for s, v in need.items():
            if s.is_dma:
                v = s.count
            if s is eng.sem and eng.inorder_self:
                continue
            if eng.seen.get(s, 0) >= v:
                continue
            eng.e.wait_ge(s.h, v)
            eng.seen[s] = v
            self.trace[eng.name].append(("w", s.name, v))

    def op(self, eng, emit, reads=(), writes=(), inc=True):
        self._waits(eng, reads, writes)
        ins = emit()
        self.nops += 1
        if inc:
            eng.sem.count += 1
            ins.then_inc(eng.sem.h, 1)
            ev = (eng.sem, eng.sem.count)
            eng.dangling = False
            self.trace[eng.name].append(("i", eng.sem.name, 1))
        else:
            ev = (eng.sem, eng.sem.count + 1)
            eng.dangling = True
        for b in writes:
            b.w = ev
            b.r = []
        for b in reads:
            if b.w is not ev and (not b.r or b.r[-1] != ev):
                b.r.append(ev)
        return ins

    def dma(self, q, sem, out, in_, reads=(), writes=(), is_output=False, **kw):
        self._waits(q, reads, writes)
        ins = q.e.dma_start(out=out, in_=in_, **kw)
        ins.then_inc(sem.h, 16)
        sem.count += 16
        self.trace[q.name].append(("i", sem.name, 16))
        ev = (sem, sem.count)
        for b in writes:
            b.w = ev
            b.r = []
        for b in reads:
            b.r.append(ev)
        if is_output:
            self.out_events.append(ev)
        self.nops += 1
        return ins

    def barrier(self):
        for e in self.engs:
            assert not e.dangling
        for e in self.engs:
            for s in self.sems:
                if s.count == 0:
                    continue
                if s is e.sem:
                    continue
                if e.seen.get(s, 0) >= s.count:
                    continue
                e.e.wait_ge(s.h, s.count)
                e.seen[s] = s.count
                self.trace[e.name].append(("w", s.name, s.count))

    def check_deadlock(self):
        vals = {}
        pos = {n: 0 for n in self.trace}
        progress = True
        while progress:
            progress = False
            for n, tr in self.trace.items():
                while pos[n] < len(tr):
                    kind, sn, v = tr[pos[n]]
                    if kind == "w":
                        if vals.get(sn, 0) < v:
                            break
                    else:
                        vals[sn] = vals.get(sn, 0) + v
                    pos[n] += 1
                    progress = True
        stuck = {n: (pos[n], self.trace[n][pos[n]]) for n in self.trace if pos[n] < len(self.trace[n])}
        return stuck

    def finish(self):
        need = {}
        for s, v in self.out_events:
            need[s] = max(need.get(s, 0), v)
        for s, v in need.items():
            self.sp.e.wait_ge(s.h, v)


PP = {}
_off = 0


def _pp(name, n):
    global _off
    PP[name] = (_off, n)
    _off += n


_pp("ident", 128)
_pp("ntri", 128)
_pp("nones", 128)
_pp("ones", 128)
_pp("mlmask", 128)
_pp("sbmask", 4 * 512)
_pp("sel127", 128)
_pp("gmix", 1024)
_pp("bgate", 24)
_pp("convw", 64)
_pp("convb", 16)
_pp("mlng", 1024)
_pp("gmem", 1024)
_pp("qng", 2)
_pp("kng", 256)
_pp("gmlp", 1024)
_pp("bif", 2)
_pp("sel4", 12)
NPP = _off


def make_pp(inp):
    pp = np.zeros((128, NPP), np.float32)

    def put(name, arr):
        o, n = PP[name]
        pp[:, o:o + n] = np.asarray(arr, np.float32).reshape(128, n)
    p = np.arange(128)
    put("ident", np.eye(128))
    put("ntri", -(p[:, None] >= p[None, :]).astype(np.float32))
    put("nones", -np.ones((128, 128)))
    put("ones", np.ones((128, 128)))
    c = np.arange(512)
    put("sbmask", np.stack([((128 * r + p[:, None]) < c[None, :]) for r in range(4)], 1).astype(np.float32))
    put("mlmask", (p[:, None] <= p[None, :]).astype(np.float32))
    put("sel127", np.repeat((p == 127).astype(np.float32)[:, None], 128, 1))
    put("gmix", np.broadcast_to(inp["g_mix"][0][None, :], (128, 1024)))
    put("bgate", inp["b_gate"][0].reshape(24, 128).T)
    put("convw", inp["conv_w"][0].reshape(4, 16, 128).transpose(2, 1, 0))
    put("convb", inp["conv_b"][0].reshape(16, 128).T)
    put("mlng", np.broadcast_to(inp["ml_norm_g"][0][None, :], (128, 1024)))
    put("gmem", np.broadcast_to(inp["g_mem"][0][None, :], (128, 1024)))
    put("qng", inp["q_norm_g"][0].reshape(2, 128).T)
    put("kng", np.broadcast_to(inp["k_norm_g"][0][None, :], (128, 256)))
    put("gmlp", np.broadcast_to(inp["g_mlp"][0][None, :], (128, 1024)))
    bif = np.zeros((128, 2), np.float32)
    bif[:4, 0] = inp["b_if"][0, :4]
    bif[:4, 1] = inp["b_if"][0, 4:]
    put("bif", bif)
    sel = np.zeros((128, 12), np.float32)
    for q in range(3):
        for h in range(4):
            sel[32 * q + h, 4 * q + h] = 1.0
    put("sel4", sel)
    return pp


C_SBQ, C_SBK, C_SBV = 0, 1024, 2048
C_MLQ, C_MLK, C_MLV, C_MLO = 3072, 4096, 5120, 6144
C_MLI, C_MLF, C_XQ, C_GATE = 7168, 7172, 7176, 8200


class K:
    pass


def build(dbg=(), stages=("s1", "s2")):
    nc = bass.Bass("TRN2", target_bir_lowering=False)
    k = K()
    k.nc = nc
    k.dbg = dbg
    k.ndummy = NDUMMY

    def din(name, shape, dt=F32):
        return nc.dram_tensor(name, shape, dt, kind="ExternalInput").ap()

    def dscr(name, shape, dt):
        kind = "ExternalOutput" if name in dbg else "Internal"
        return nc.dram_tensor(name, shape, dt, kind=kind).ap()

    k.x = din("x", [S_LEN, D])
    k.mem = din("mem", [N_MEM, D])
    k.pp = din("pp", [128, NPP])
    k.w_in = din("w_in", [D, N_IN])
    k.w_mem_kv = din("w_mem_kv", [D, 2048])
    k.w_sb = din("w_sb_proj", [D, D])
    k.w_ml = din("w_ml_proj", [D, D])
    k.w_x = din("w_x_proj", [D, D])
    k.w_out = din("w_out", [D, D])
    k.w_ff1 = din("w_ff1", [D, 4096])
    k.w_ff2 = din("w_ff2", [4096, D])
    k.out = nc.dram_tensor("out", [S_LEN, D], F32, kind="ExternalOutput").ap()

    k.qt_sb = dscr("qt_sb", [8, 128, S_LEN], BF16)
    k.kt_sb = dscr("kt_sb", [8, 128, S_LEN], BF16)
    k.v_sb = dscr("v_sb", [S_LEN, D], BF16)
    k.qt_ml = dscr("qt_ml", [8, 128, S_LEN], BF16)
    k.kt_ml = dscr("kt_ml", [8, 128, S_LEN], BF16)
    k.k_ml = dscr("k_ml", [S_LEN, D], BF16)
    k.v_ml = dscr("v_ml", [S_LEN, D], BF16)
    k.o_ml = dscr("o_ml", [S_LEN, D], BF16)
    k.qt_x = dscr("qt_x", [8, 128, S_LEN], BF16)
    k.gif = dscr("gif", [8, S_LEN], F32)
    k.g_scr = dscr("g_scr", [24, 128, S_LEN], BF16)
    k.yt_sb = dscr("yt_sb", [8, 128, S_LEN], BF16)
    k.yt_ml = dscr("yt_ml", [8, 128, S_LEN], BF16)
    k.yt_x = dscr("yt_x", [8, 128, S_LEN], BF16)

    k.x1_scr = dscr("x1_scr", [S_LEN, D], F32)
    k.h2t_scr = dscr("h2t_scr", [8, 128, S_LEN], BF16)

    with ExitStack() as st:
        S = Sched(nc, st)
        k.S = S
        k.st = st

        def sb(name, shape, dt, stack):
            return stack.enter_context(nc.sbuf_tensor(name, shape, dt))

        def ps(name, shape, dt, stack=st):
            return stack.enter_context(nc.psum_tensor(name, shape, dt))
        k.sb = sb
        k.ps = ps

        k.pf = [ps(f"pf{i}", [128, 512], F32) for i in range(2)]
        k.zz = ps("pzz", [128, 2, 512], F32)
        k.oo = ps("poo", [128, 2, 512], F32)
        k.pf += [k.zz[:, 0, :], k.zz[:, 1, :], k.oo[:, 0, :], k.oo[:, 1, :]]
        k.b_pf = [Buf(f"pf{i}", psum=True) for i in range(6)]
        k.b_zz = [k.b_pf[2], k.b_pf[3]]
        k.b_oo = [k.b_pf[4], k.b_pf[5]]
        k.pb = [ps(f"pb{i}", [128, 1024], BF16) for i in range(2)]
        k.b_pb = [Buf(f"pb{i}", psum=True) for i in range(2)]

        with ExitStack() as stA:
            k.ppt = sb("ppt", [128, NPP], F32, stA)
            k.b_pp = Buf("pp")
            k.sem_c = S.new_sem("d_const")
            S.dma(S.sp, k.sem_c, k.ppt[:], k.pp[:, :], writes=[k.b_pp])

            def ppv(name):
                o, n = PP[name]
                return k.ppt[:, o:o + n]
            k.ppv = ppv
            NCB = 5 * 128 + 2048
            k.cbf = sb("cbf", [128, NCB], BF16, stA)
            k.b_cbf = Buf("cbf")
            o_id = PP["ident"][0]
            S.op(S.dve, lambda: nc.vector.tensor_copy(out=k.cbf[:, :], in_=k.ppt[:, o_id:o_id + NCB]),
                 reads=[k.b_pp], writes=[k.b_cbf])
            k.ident_bf = k.cbf[:, 0:128]
            k.ntri_bf = k.cbf[:, 128:256]
            k.nones_bf = k.cbf[:, 256:384]
            k.ones_bf = k.cbf[:, 384:512]
            k.mlmask_bf = k.cbf[:, 512:640]
            k.sbmask_bf = k.cbf[:, 640:640 + 2048]
            k.eps_t = sb("eps_t", [128, 1], F32, stA)
            k.b_eps = Buf("eps")
            S.op(S.dve, lambda: nc.vector.memset(k.eps_t[:], EPS), writes=[k.b_eps])

            with ExitStack() as st12:
                k.hT = sb("hT", [128, 8, S_LEN], BF16, st12)
                k.b_hT = [Buf(f"hT{i}") for i in range(NTB)]
                with ExitStack() as st1:
                    if "s1" in stages:
                        stage1(k, st1)
                    if "s2" in stages:
                        stage2(k)
                if "hT" in dbg:
                    hT_d = nc.dram_tensor("hT_dbg", [128, 8, S_LEN], BF16, kind="ExternalOutput").ap()
                    sd = S.new_sem("d_dbg")
                    S.dma(S.sp, sd, hT_d, k.hT[:], reads=k.b_hT, is_output=True)
                S.barrier()
            if "mid" in stages:
                stage_mid(k)
            if "sb" in stages:
                stage_sb(k)
            if "ml" in stages:
                stage_ml(k)
            if "xa" in stages:
                stage_xa(k)
            if "4a" in stages:
                stage_4a(k)
            S.barrier()
        if "4b" in stages:
            stage_4b(k)
        S.out_events.extend((s_, s_.count) for s_ in S.sems if s_.name.startswith("d_") and s_.count)
        S.finish()
        stuck = S.check_deadlock()
        assert not stuck, f"deadlock: {stuck}"
        k.S = S
    nc._sched = None
    return nc


def stage1(k, st):
    nc, S = k.nc, k.S
    if True:
        xb = [k.sb(f"xb{i}", [128, D], F32, st) for i in range(3)]
        b_xb = [Buf() for _ in range(3)]
        sem_x = [S.new_sem(f"d_x{i}") for i in range(3)]
        hb = [k.sb(f"hb{i}", [128, D], BF16, st) for i in range(2)]
        b_hb = [Buf() for _ in range(2)]
        junk = k.sb("junk1", [128, D], BF16, st)
        b_junk = Buf()
        stat = k.sb("stat1", [128, 3 * NTB], F32, st)
        gm = k.ppv("gmix")
        for i in range(NTB):
            xi, bx = xb[i % 3], b_xb[i % 3]
            S.dma(S.sp, sem_x[i % 3], xi[:], k.x[i * 128:(i + 1) * 128, :], writes=[bx])
            b_ss, b_rms, b_rs = Buf(), Buf(), Buf()
            ss = stat[:, 3 * i:3 * i + 1]
            rms = stat[:, 3 * i + 1:3 * i + 2]
            rs = stat[:, 3 * i + 2:3 * i + 3]
            S.op(S.act, lambda: nc.scalar.activation(out=junk[:], in_=xi[:], func=AF.Square, accum_out=ss),
                 reads=[bx], writes=[b_junk, b_ss])
            S.op(S.act, lambda: nc.scalar.activation(out=rms, in_=ss, func=AF.Sqrt, bias=k.eps_t[:, 0:1], scale=1.0 / D),
                 reads=[b_ss, k.b_eps], writes=[b_rms])
            S.op(S.dve, lambda: nc.vector.reciprocal(out=rs, in_=rms), reads=[b_rms], writes=[b_rs])
            hi, bh = hb[i % 2], b_hb[i % 2]
            S.op(S.dve, lambda: nc.vector.scalar_tensor_tensor(out=hi[:], in0=xi[:], scalar=rs, in1=gm,
                                                               op0=ALU.mult, op1=ALU.mult),
                 reads=[bx, b_rs, k.b_pp], writes=[bh])
            pb, bpb = k.pb[i % 2], k.b_pb[i % 2]
            for c in range(8):
                S.op(S.pe, lambda: nc.tensor.transpose(out=pb[:, c * 128:(c + 1) * 128], in_=hi[:, c * 128:(c + 1) * 128],
                                                       identity=k.ident_bf),
                     reads=[bh, k.b_cbf], writes=[bpb], inc=(c == 7))
            ev_eng = S.act if i % 2 == 0 else S.dve
            if ev_eng is S.act:
                S.op(S.act, lambda: nc.scalar.copy(out=k.hT[:, :, i * 128:(i + 1) * 128],
                                                   in_=pb[:].rearrange("p (c t) -> p c t", c=8)),
                     reads=[bpb], writes=[k.b_hT[i]])
            else:
                S.op(S.dve, lambda: nc.vector.tensor_copy(out=k.hT[:, :, i * 128:(i + 1) * 128],
                                                          in_=pb[:].rearrange("p (c t) -> p c t", c=8)),
                     reads=[bpb], writes=[k.b_hT[i]])


def stage2(k):
    nc, S = k.nc, k.S
    with ExitStack() as st:
        NW = 3
        wt = [k.sb(f"wt{i}", [128, 8, 512], BF16, st) for i in range(NW)]
        b_wt = [Buf() for _ in range(NW)]
        sem_w = [S.new_sem(f"d_w{i}") for i in range(NW)]
        ob = [k.sb(f"ob{i}", [128, S_LEN], BF16, st) for i in range(2)]
        b_ob = [Buf() for _ in range(2)]
        sem_ob = [S.new_sem(f"d_ob{i}") for i in range(2)]
        ot = [k.sb(f"ot{i}", [128, 4, 512], BF16, st) for i in range(2)]
        b_ot = [Buf() for _ in range(2)]
        sem_ot = [S.new_sem(f"d_ot{i}") for i in range(2)]
        zc = k.sb("zc", [128, 8 + S_LEN], BF16, st)
        b_zc = Buf()
        dg = [k.sb(f"dg{i}", [128, 4, 128], BF16, st) for i in range(2)]
        b_dg = [Buf() for _ in range(2)]
        cnt = {"w": 0, "ob": 0, "ot": 0, "pf": 0, "dg": 0, "ev": 0}
        S.op(S.dve, lambda: nc.vector.memset(zc[:, 0:8], 0.0), writes=[b_zc])

        def load_w(col0, ncols=512):
            i = cnt["w"] % NW
            cnt["w"] += 1
            S.dma(S.pool, sem_w[i], wt[i][:, :, 0:ncols],
                  k.w_in[:, col0:col0 + ncols].rearrange("(c p) n -> p c n", p=128), writes=[b_wt[i]])
            return wt[i], b_wt[i]

        def next_pf():
            i = cnt["pf"] % 4
            cnt["pf"] += 1
            return k.pf[i], k.b_pf[i]

        def evac(out, in_, reads, writes, func=None, scale=1.0, bias=None):
            if func is None and scale == 1.0:
                cnt["ev"] += 1
                if cnt["ev"] % 2 == 0:
                    return S.op(S.dve, lambda: nc.vector.tensor_copy(out=out, in_=in_), reads=reads, writes=writes)
                return S.op(S.act, lambda: nc.scalar.copy(out=out, in_=in_), reads=reads, writes=writes)
            f = func if func is not None else AF.Copy
            if bias is not None:
                return S.op(S.act, lambda: nc.scalar.activation(out=out, in_=in_, func=f, bias=bias, scale=scale),
                            reads=reads, writes=writes)
            return S.op(S.act, lambda: nc.scalar.activation(out=out, in_=in_, func=f, scale=scale),
                        reads=reads, writes=writes)

        def fm_group(w, bw, g, dst_row, scale=1.0):
            i = cnt["ob"] % 2
            cnt["ob"] += 1
            o, bo = ob[i], b_ob[i]
            for tt in range(NTT):
                p, bp = next_pf()
                for c in range(8):
                    S.op(S.pe, lambda: nc.tensor.matmul(p[:], lhsT=w[:, c, g * 128:(g + 1) * 128],
                                                        rhs=k.hT[:, c, tt * 512:(tt + 1) * 512],
                                                        start=(c == 0), stop=(c == 7)),
                         reads=[bw] + k.b_hT[tt * 4:tt * 4 + 4], writes=[bp], inc=(c == 7))
                evac(o[:, tt * 512:(tt + 1) * 512], p[:], [bp], [bo], scale=scale)
            S.dma(S.sp, sem_ob[i], dst_row, o[:], reads=[bo])

        def tm_tile(w, bw, dst, col0, func=None):
            for tq in range(8):
                i = cnt["ot"] % 2
                cnt["ot"] += 1
                o, bo = ot[i], b_ot[i]
                for j in range(4):
                    tb = tq * 4 + j
                    p, bp = next_pf()
                    for c in range(8):
                        S.op(S.pe, lambda: nc.tensor.matmul(p[:], lhsT=k.hT[:, c, tb * 128:(tb + 1) * 128],
                                                            rhs=w[:, c, :], start=(c == 0), stop=(c == 7)),
                             reads=[bw, k.b_hT[tb]], writes=[bp], inc=(c == 7))
                    evac(o[:, j, :], p[:], [bp], [bo], func=func)
                S.dma(S.sp, sem_ot[i], dst[tq * 512:(tq + 1) * 512, col0:col0 + 512].rearrange("(j p) n -> p j n", p=128),
                      o[:], reads=[bo])

        zcs = [zc, k.sb("zc2", [128, 8 + S_LEN], BF16, st)]
        b_zcs = [b_zc, Buf()]
        S.op(S.dve, lambda: nc.vector.memset(zcs[1][:, 0:8], 0.0), writes=[b_zcs[1]])

        def conv_proj(it):
            w, bw, g, zi = it["w"], it["bw"], it["g"], it["zi"]
            for tt in range(NTT):
                p, bp = next_pf()
                for c in range(8):
                    S.op(S.pe, lambda: nc.tensor.matmul(p[:], lhsT=w[:, c, g * 128:(g + 1) * 128],
                                                        rhs=k.hT[:, c, tt * 512:(tt + 1) * 512],
                                                        start=(c == 0), stop=(c == 7)),
                         reads=[bw] + k.b_hT[tt * 4:tt * 4 + 4], writes=[bp], inc=(c == 7))
                evac(zcs[zi][:, 8 + tt * 512:8 + (tt + 1) * 512], p[:], [bp], [b_zcs[zi]])

        def conv_apply(it):
            cg, zi = it["cg"], it["zi"]
            di = cnt["dg"] % 2
            cnt["dg"] += 1
            d, bd = dg[di], b_dg[di]
            o_cw = PP["convw"][0]
            for j in range(4):
                S.op(S.dve, lambda: nc.vector.tensor_scalar(out=d[:, j, :], in0=k.ppv("ident"),
                                                            scalar1=k.ppt[:, o_cw + cg * 4 + j:o_cw + cg * 4 + j + 1],
                                                            scalar2=None, op0=ALU.mult),
                     reads=[k.b_pp], writes=[bd])
            i = cnt["ob"] % 2
            cnt["ob"] += 1
            o, bo = ob[i], b_ob[i]
            it["o"], it["bo"] = o, bo
            o_cb = PP["convb"][0]
            for tt in range(NTT):
                p, bp = next_pf()
                for j in range(4):
                    S.op(S.pe, lambda: nc.tensor.matmul(p[:], lhsT=d[:, j, :],
                                                        rhs=zcs[zi][:, 5 + j + tt * 512:5 + j + (tt + 1) * 512],
                                                        start=(j == 0), stop=(j == 3)),
                         reads=[bd, b_zcs[zi]], writes=[bp], inc=(j == 3))
                evac(o[:, tt * 512:(tt + 1) * 512], p[:], [bp], [bo], func=AF.Silu,
                     bias=k.ppt[:, o_cb + cg:o_cb + cg + 1])
            S.dma(S.sp, sem_ob[i], it["dst_row"], o[:], reads=[bo])

        def conv_ktrans(it):
            if it["kdst"] is None:
                return
            o, bo, kdst, kcol = it["o"], it["bo"], it["kdst"], it["kcol"]
            for tq in range(8):
                pb, bpb = k.pb[tq % 2], k.b_pb[tq % 2]
                for j in range(4):
                    tb = tq * 4 + j
                    S.op(S.pe, lambda: nc.tensor.transpose(out=pb[:, j * 128:(j + 1) * 128],
                                                           in_=o[:, tb * 128:(tb + 1) * 128], identity=k.ident_bf),
                         reads=[bo, k.b_cbf], writes=[bpb], inc=(j == 3))
                ii = cnt["ot"] % 2
                cnt["ot"] += 1
                t_, bt = ot[ii], b_ot[ii]
                evac(t_[:, :, 0:128], pb[:, 0:512].rearrange("p (j n) -> p j n", j=4), [bpb], [bt])
                S.dma(S.sp, sem_ot[ii],
                      kdst[tq * 512:(tq + 1) * 512, kcol:kcol + 128].rearrange("(j p) n -> p j n", p=128),
                      t_[:, :, 0:128], reads=[bt])

        for half in range(2):
            w, bw = load_w(C_SBV + half * 512)
            tm_tile(w, bw, k.v_sb, half * 512)
        for half in range(2):
            w, bw = load_w(C_MLV + half * 512)
            tm_tile(w, bw, k.v_ml, half * 512)
        for half in range(2):
            w, bw = load_w(C_SBQ + half * 512)
            for g in range(4):
                fm_group(w, bw, g, k.qt_sb[half * 4 + g], scale=128.0 ** -0.5)
        for half in range(2):
            w, bw = load_w(C_SBK + half * 512)
            for g in range(4):
                fm_group(w, bw, g, k.kt_sb[half * 4 + g])
        for half in range(2):
            w, bw = load_w(C_XQ + half * 512)
            for g in range(4):
                fm_group(w, bw, g, k.qt_x[half * 4 + g])
        for half in range(2):
            w, bw = load_w(C_MLO + half * 512)
            tm_tile(w, bw, k.o_ml, half * 512, func=AF.Sigmoid)
        items = []
        for which, c0 in ((0, C_MLQ), (1, C_MLK)):
            for half in range(2):
                for g in range(4):
                    cg = half * 4 + g
                    items.append({"col0": c0 + half * 512, "g": g, "cg": which * 8 + cg, "zi": len(items) % 2,
                                  "dst_row": (k.qt_ml if which == 0 else k.kt_ml)[cg],
                                  "kdst": k.k_ml if which == 1 else None, "kcol": cg * 128})
        wcur = {}

        def get_w(it):
            if it["col0"] not in wcur:
                wcur.clear()
                wcur[it["col0"]] = load_w(it["col0"])
            it["w"], it["bw"] = wcur[it["col0"]]
        get_w(items[0])
        conv_proj(items[0])
        for n, it in enumerate(items):
            if n + 1 < len(items):
                get_w(items[n + 1])
                conv_proj(items[n + 1])
            conv_apply(it)
            if n > 0:
                conv_ktrans(items[n - 1])
        conv_ktrans(items[-1])
        o_bg = PP["bgate"][0]
        for t6 in range(6):
            w, bw = load_w(C_GATE + t6 * 512)
            for g in range(4):
                gg = t6 * 4 + g
                i = cnt["ob"] % 2
                cnt["ob"] += 1
                o, bo = ob[i], b_ob[i]
                for tt in range(NTT):
                    p, bp = next_pf()
                    for c in range(8):
                        S.op(S.pe, lambda: nc.tensor.matmul(p[:], lhsT=w[:, c, g * 128:(g + 1) * 128],
                                                            rhs=k.hT[:, c, tt * 512:(tt + 1) * 512],
                                                            start=(c == 0), stop=(c == 7)),
                             reads=[bw] + k.b_hT[tt * 4:tt * 4 + 4], writes=[bp], inc=(c == 7))
                    evac(o[:, tt * 512:(tt + 1) * 512], p[:], [bp, k.b_pp], [bo], func=AF.Sigmoid,
                         bias=k.ppt[:, o_bg + gg:o_bg + gg + 1])
                S.dma(S.sp, sem_ob[i], k.g_scr[gg], o[:], reads=[bo])
        w, bw = load_w(C_MLI, 8)
        gi = [k.sb(f"gi_rows{i}", [4, 2, 512], F32, st) for i in range(2)]
        b_gi = [Buf(), Buf()]
        sem_g = [S.new_sem("d_gif0"), S.new_sem("d_gif1")]
        o_bif = PP["bif"][0]
        for tt in range(NTT):
            gt, bg = gi[tt % 2], b_gi[tt % 2]
            for which in range(2):
                p, bp = next_pf()
                for c in range(8):
                    S.op(S.pe, lambda: nc.tensor.matmul(p[0:4, :], lhsT=w[:, c, which * 4:which * 4 + 4],
                                                        rhs=k.hT[:, c, tt * 512:(tt + 1) * 512],
                                                        start=(c == 0), stop=(c == 7)),
                         reads=[bw] + k.b_hT[tt * 4:tt * 4 + 4], writes=[bp], inc=(c == 7))
                S.op(S.act, lambda: nc.scalar.activation(out=gt[:, which, :], in_=p[0:4, :],
                                                         func=AF.Identity,
                                                         bias=k.ppt[0:4, o_bif + which:o_bif + which + 1], scale=1.0),
                     reads=[bp, k.b_pp], writes=[bg])
            S.dma(S.sp, sem_g[tt % 2], k.gif[:, tt * 512:(tt + 1) * 512].rearrange("(w h) t -> h w t", w=2), gt[:], reads=[bg])
        S.barrier()


def stage_sb(k):
    nc, S = k.nc, k.S
    NS = 3
    tiles_of = [[7, 3], [6, 4], [5, 2, 1, 0]]
    with ExitStack() as st:
        qT = [k.sb(f"sb_q{i}", [128, S_LEN], BF16, st) for i in range(2)]
        kT = [k.sb(f"sb_k{i}", [128, S_LEN], BF16, st) for i in range(2)]
        vv = [k.sb(f"sb_v{i}", [128, NTB, 128], BF16, st) for i in range(2)]
        b_q = [Buf() for _ in range(2)]
        b_k = [Buf() for _ in range(2)]
        b_v = [Buf() for _ in range(2)]
        sem_in = [S.new_sem(f"d_sbin{i}") for i in range(2)]
        yT = [k.sb(f"sb_y{i}", [128, S_LEN], BF16, st) for i in range(2)]
        b_y = [Buf() for _ in range(2)]
        sem_y = [S.new_sem(f"d_sby{i}") for i in range(2)]
        e_sb = [k.sb(f"sb_e{s}", [128, 512], F32, st) for s in range(NS)]
        b_e = [Buf() for _ in range(NS)]
        nl = [[k.sb(f"sb_nl{s}_{j}", [128, 512], BF16, st) for j in range(2)] for s in range(NS)]
        b_nl = [[Buf() for _ in range(2)] for _ in range(NS)]
        at = [[k.sb(f"sb_at{s}_{j}", [128, 512], BF16, st) for j in range(2)] for s in range(NS)]
        b_at = [[Buf() for _ in range(2)] for _ in range(NS)]
        sacc = [k.sb(f"sb_sacc{s}", [128, 512], BF16, st) for s in range(NS)]
        b_sacc = [Buf() for _ in range(NS)]
        zb = [k.pf[s] for s in range(NS)]
        b_zb = [k.b_pf[s] for s in range(NS)]
        ob = [k.pf[NS + s] for s in range(NS)]
        b_ob = [k.b_pf[NS + s] for s in range(NS)]

        def load_head(h):
            i = h % 2
            S.dma(S.sp, sem_in[i], qT[i][:], k.qt_sb[h], writes=[b_q[i]])
            S.dma(S.sp, sem_in[i], kT[i][:], k.kt_sb[h], writes=[b_k[i]])
            S.dma(S.sp, sem_in[i], vv[i][:], k.v_sb[:, h * 128:(h + 1) * 128].rearrange("(j p) n -> p j n", p=128),
                  writes=[b_v[i]])

        def mm1(h, s, qi, kb):
            i = h % 2
            S.op(S.pe, lambda: nc.tensor.matmul(zb[s][:], lhsT=kT[i][:, kb * 128:(kb + 1) * 128],
                                                rhs=qT[i][:, qi * 512:(qi + 1) * 512], start=True, stop=False),
                 reads=[b_k[i], b_q[i]], writes=[b_zb[s]])

        load_head(0)
        for h in range(8):
            if h + 1 < 8:
                load_head(h + 1)
            i = h % 2
            steps = [[(qi, kb) for qi in tiles_of[s] for kb in range(4 * qi + 3, -1, -1)] for s in range(NS)]
            nsteps = len(steps[0])
            assert all(len(x) == nsteps for x in steps)
            for s in range(NS):
                mm1(h, s, *steps[s][0])
            for n in range(nsteps):
                par = n % 2
                info = []
                for s in range(NS):
                    qi, kb = steps[s][n]
                    r = kb - 4 * qi
                    info.append((qi, kb, r, kb == 4 * qi + 3, kb == 0))
                for s in range(NS):
                    S.op(S.act, lambda: nc.scalar.activation(out=e_sb[s][:], in_=zb[s][:], func=AF.Exp),
                         reads=[b_zb[s]], writes=[b_e[s]])
                for s in range(NS):
                    S.op(S.act, lambda: nc.scalar.activation(out=nl[s][par][:], in_=e_sb[s][:], func=AF.Ln, bias=1.0),
                         reads=[b_e[s]], writes=[b_nl[s][par]])
                for s in range(NS):
                    qi, kb, r, first, last = info[s]
                    if r >= 0:
                        S.op(S.dve, lambda: nc.vector.tensor_tensor(out=nl[s][par][:], in0=nl[s][par][:],
                                                                    in1=k.sbmask_bf[:, r * 512:(r + 1) * 512], op=ALU.mult),
                             reads=[b_nl[s][par], k.b_cbf], writes=[b_nl[s][par]])
                for s in range(NS):
                    qi, kb, r, first, last = info[s]
                    S.op(S.pe, lambda: nc.tensor.matmul(zb[s][:], lhsT=k.ntri_bf, rhs=nl[s][par][:], start=False, stop=first),
                         reads=[b_nl[s][par], k.b_cbf], writes=[b_zb[s]], inc=first)
                    if not first:
                        S.op(S.pe, lambda: nc.tensor.matmul(zb[s][:], lhsT=k.nones_bf, rhs=sacc[s][:], start=False, stop=True),
                             reads=[b_sacc[s], k.b_cbf], writes=[b_zb[s]])
                for s in range(NS):
                    qi, kb, r, first, last = info[s]
                    S.op(S.act, lambda: nc.scalar.activation(out=at[s][par][:], in_=zb[s][:], func=AF.Exp),
                         reads=[b_zb[s]], writes=[b_at[s][par]])
                    if not last:
                        if first:
                            S.op(S.pool, lambda: nc.gpsimd.tensor_copy(out=sacc[s][:], in_=nl[s][par][:]),
                                 reads=[b_nl[s][par]], writes=[b_sacc[s]])
                        else:
                            S.op(S.pool, lambda: nc.gpsimd.tensor_tensor(out=sacc[s][:], in0=sacc[s][:], in1=nl[s][par][:],
                                                                        op=ALU.add),
                                 reads=[b_nl[s][par], b_sacc[s]], writes=[b_sacc[s]])
                for s in range(NS):
                    qi, kb, r, first, last = info[s]
                    if r >= 0:
                        S.op(S.dve, lambda: nc.vector.tensor_tensor(out=at[s][par][:], in0=at[s][par][:],
                                                                    in1=k.sbmask_bf[:, r * 512:(r + 1) * 512], op=ALU.mult),
                             reads=[b_at[s][par], k.b_cbf], writes=[b_at[s][par]])
                    S.op(S.pe, lambda: nc.tensor.matmul(ob[s][:], lhsT=vv[i][:, kb, :], rhs=at[s][par][:],
                                                        start=first, stop=last),
                         reads=[b_v[i], b_at[s][par]], writes=[b_ob[s]])
                    if n + 1 < nsteps:
                        mm1(h, s, *steps[s][n + 1])
                    if last:
                        S.op(S.dve, lambda: nc.vector.tensor_copy(out=yT[i][:, qi * 512:(qi + 1) * 512], in_=ob[s][:]),
                             reads=[b_ob[s]], writes=[b_y[i]])
            S.dma(S.sp, sem_y[i], k.yt_sb[h], yT[i][:], reads=[b_y[i]])
        S.barrier()


def stage_ml(k):
    nc, S = k.nc, k.S
    with ExitStack() as st:
        tab = k.sb("ml_tab", [128, 3, 32, 4], F32, st)
        Mb = k.sb("ml_Mb", [128, 33, 4], F32, st)
        uT = k.sb("ml_u", [128, 32, 4], F32, st)
        wT = k.sb("ml_w", [128, 32, 4], F32, st)
        flT = k.sb("ml_fl", [128, 32, 4], F32, st)
        decT = k.sb("ml_dec", [128, 32, 4], F32, st)
        nl16 = k.sb("ml_nl16", [128, 1], F32, st)
        b_tab, b_Mb, b_u, b_w, b_fl, b_dec, b_c = Buf(), Buf(), Buf(), Buf(), Buf(), Buf(), Buf()
        S.op(S.dve, lambda: nc.vector.memset(nl16[:], -LN16), writes=[b_c])
        S.op(S.dve, lambda: nc.vector.memset(Mb[:, 0, :], 0.0), writes=[b_Mb])
        with ExitStack() as st2:
            fp = k.sb("ml_fp", [4, S_LEN], F32, st2)
            ip = k.sb("ml_ip", [4, S_LEN], F32, st2)
            Fn = k.sb("ml_Fn", [4, S_LEN], F32, st2)
            on = k.sb("ml_on", [4, S_LEN], F32, st2)
            b_fp, b_ip, b_Fn, b_on = Buf(), Buf(), Buf(), Buf()
            sg = S.new_sem("d_mlg")
            S.dma(S.sp, sg, fp[:], k.gif[4:8, :], writes=[b_fp])
            S.dma(S.sp, sg, ip[:], k.gif[0:4, :], writes=[b_ip])
            S.op(S.dve, lambda: nc.vector.memset(on[:], 1.0), writes=[b_on])
            S.op(S.act, lambda: nc.scalar.activation(out=fp[:], in_=fp[:], func=AF.Exp, scale=-1.0),
                 reads=[b_fp], writes=[b_fp])
            S.op(S.act, lambda: nc.scalar.activation(out=fp[:], in_=fp[:], func=AF.Ln, bias=1.0),
                 reads=[b_fp], writes=[b_fp])
            S.op(S.dve, lambda: nc.vector.tensor_tensor_scan(out=Fn[:], data0=on[:], data1=fp[:], initial=0.0,
                                                             op0=ALU.mult, op1=ALU.add),
                 reads=[b_on, b_fp], writes=[b_Fn])
            S.op(S.dve, lambda: nc.vector.tensor_tensor(out=ip[:], in0=ip[:], in1=Fn[:], op=ALU.add),
                 reads=[b_ip, b_Fn], writes=[b_ip])
            S.op(S.dve, lambda: nc.vector.tensor_tensor_scan(out=fp[:], data0=ip[:], data1=ip[:], initial=0.0,
                                                             op0=ALU.max, op1=ALU.max),
                 reads=[b_ip], writes=[b_fp])
            pt, bpt = k.pf[0], k.b_pf[0]
            ptv = pt[:, 0:384].rearrange("p (q c h) -> p q c h", q=3, c=32)
            idf = k.ppv("ident")
            for q, (X, bX) in enumerate(((Fn, b_Fn), (ip, b_ip), (fp, b_fp))):
                for c in range(32):
                    S.op(S.pe, lambda: nc.tensor.transpose(out=ptv[:, q, c, :], in_=X[0:4, c * 128:(c + 1) * 128],
                                                           identity=idf[0:4, 0:4]),
                         reads=[bX, k.b_pp], writes=[bpt], inc=(q == 2 and c == 31))
            S.op(S.dve, lambda: nc.vector.tensor_copy(out=tab[:], in_=ptv), reads=[bpt], writes=[b_tab])
            S.barrier()
        pm, bpm = k.pf[1], k.b_pf[1]
        o_sel = PP["sel127"][0]
        S.op(S.pe, lambda: nc.tensor.matmul(pm[:, 0:128], lhsT=k.ppt[:, o_sel:o_sel + 128],
                                            rhs=tab[:, 2].rearrange("p c h -> p (c h)"), start=True, stop=True),
             reads=[b_tab, k.b_pp], writes=[bpm])
        S.op(S.dve, lambda: nc.vector.tensor_copy(out=Mb[:, 1:33, :], in_=pm[:, 0:128].rearrange("p (c h) -> p c h", c=32)),
             reads=[bpm], writes=[b_Mb])
        tmp = k.sb("ml_tmp", [128, 32, 4], F32, st)
        b_tmp = Buf()

        def table(dst, bd, in0, in1, bias):
            S.op(S.dve, lambda: nc.vector.tensor_tensor(out=tmp[:], in0=in0, in1=in1, op=ALU.subtract),
                 reads=[b_tab, b_Mb], writes=[b_tmp])
            if bias:
                S.op(S.act, lambda: nc.scalar.activation(out=dst[:], in_=tmp[:], func=AF.Exp, bias=nl16[:, 0:1]),
                     reads=[b_tmp, b_c], writes=[bd])
            else:
                S.op(S.act, lambda: nc.scalar.activation(out=dst[:], in_=tmp[:], func=AF.Exp),
                     reads=[b_tmp], writes=[bd])
        table(uT, b_u, tab[:, 1], Mb[:, 0:32, :], True)
        table(wT, b_w, tab[:, 1], Mb[:, 1:33, :], True)
        table(flT, b_fl, tab[:, 0], Mb[:, 0:32, :], False)
        table(decT, b_dec, Mb[:, 0:32, :], Mb[:, 1:33, :], False)

        if "ml_tabs" in k.dbg:
            td = nc.dram_tensor("ml_tabs", [128, 7, 128], F32, kind="ExternalOutput").ap()
            sdd = S.new_sem("d_mltab")
            S.dma(S.sp, sdd, td[:, 0:3, :], tab[:].rearrange("p q c h -> p q (c h)"), reads=[b_tab])
            for qi_, (t_, b_) in enumerate(((uT, b_u), (wT, b_w), (flT, b_fl), (decT, b_dec))):
                S.dma(S.sp, sdd, td[:, 3 + qi_, :], t_[:].rearrange("p c h -> p (c h)"), reads=[b_])
        NH = 2
        qTt = [[k.sb(f"ml_q{a}_{i}", [128, 2, 512], BF16, st) for i in range(2)] for a in range(NH)]
        kTt = [[k.sb(f"ml_k{a}_{i}", [128, 2, 512], BF16, st) for i in range(2)] for a in range(NH)]
        ktm = [[k.sb(f"ml_kt{a}_{i}", [128, 4, 256], BF16, st) for i in range(2)] for a in range(NH)]
        vtm = [[k.sb(f"ml_vt{a}_{i}", [128, 4, 256], BF16, st) for i in range(2)] for a in range(NH)]
        otm = [[k.sb(f"ml_ot{a}_{i}", [128, 4, 256], BF16, st) for i in range(2)] for a in range(NH)]
        b_in = [[Buf() for i in range(2)] for a in range(NH)]
        sem_in = [[S.new_sem(f"d_mlin{a}_{i}") for i in range(2)] for a in range(NH)]
        yTt = [[k.sb(f"ml_y{a}_{i}", [128, 2, 512], BF16, st) for i in range(2)] for a in range(NH)]
        b_y = [[Buf() for i in range(2)] for a in range(NH)]
        sem_y = [[S.new_sem(f"d_mly{a}_{i}") for i in range(2)] for a in range(NH)]
        PT = [[k.sb(f"ml_PT{a}_{i}", [128, 128], BF16, st) for i in range(2)] for a in range(NH)]
        vu = [[k.sb(f"ml_vu{a}_{i}", [128, 264], BF16, st) for i in range(2)] for a in range(NH)]
        vw = [[k.sb(f"ml_vw{a}_{i}", [128, 264], BF16, st) for i in range(2)] for a in range(NH)]
        b_PT = [[Buf() for i in range(2)] for a in range(NH)]
        b_vu = [[Buf() for i in range(2)] for a in range(NH)]
        b_vw = [[Buf() for i in range(2)] for a in range(NH)]
        C32 = [k.sb(f"ml_C32_{a}", [128, 2, 264], F32, st) for a in range(NH)]
        Cbf = [[k.sb(f"ml_Cbf{a}_{i}", [128, 2, 264], BF16, st) for i in range(2)] for a in range(NH)]
        b_C32 = [Buf() for a in range(NH)]
        b_Cbf = [[Buf() for i in range(2)] for a in range(NH)]
        hs = [k.sb(f"ml_hs{a}", [128, 256], F32, st) for a in range(NH)]
        t1 = [k.sb(f"ml_t1{a}", [128, 256], F32, st) for a in range(NH)]
        yb = [k.sb(f"ml_yb{a}", [128, 256], BF16, st) for a in range(NH)]
        jk = [k.sb(f"ml_jk{a}", [128, 256], BF16, st) for a in range(NH)]
        sm = [k.sb(f"ml_sm{a}", [128, 8], F32, st) for a in range(NH)]
        b_hs = [Buf() for a in range(NH)]
        b_t1 = [Buf() for a in range(NH)]
        b_yb = [Buf() for a in range(NH)]
        b_jk = [Buf() for a in range(NH)]
        b_sm = [[Buf() for _ in range(8)] for a in range(NH)]
        mlng = k.ppv("mlng")

        def load_group(a, h, g):
            i = g % 2
            sm_, bb = sem_in[a][i], [b_in[a][i]]
            cols = slice(g * 512, (g + 1) * 512)
            S.dma(S.sp, sm_, qTt[a][i][:], k.qt_ml[2 * h:2 * h + 2, :, cols].rearrange("c p t -> p c t"), writes=bb)
            S.dma(S.sp, sm_, kTt[a][i][:], k.kt_ml[2 * h:2 * h + 2, :, cols].rearrange("c p t -> p c t"), writes=bb)
            for dst, src in ((ktm, k.k_ml), (vtm, k.v_ml), (otm, k.o_ml)):
                S.dma(S.sp, sm_, dst[a][i][:], src[cols, h * 256:(h + 1) * 256].rearrange("(j p) n -> p j n", p=128), writes=bb)

        for hp in range(2):
            heads = [2 * hp, 2 * hp + 1]
            for a in range(NH):
                load_group(a, heads[a], 0)
            for g in range(8):
                if g + 1 < 8:
                    for a in range(NH):
                        load_group(a, heads[a], g + 1)
                gi = g % 2
                for j in range(4):
                    c = g * 4 + j
                    par = c % 2
                    for a in range(NH):
                        h = heads[a]
                        col = c * 4 + h
                        bank1, bb1 = k.pf[3 * a], k.b_pf[3 * a]
                        bank2, bb2 = k.pf[3 * a + 1], k.b_pf[3 * a + 1]
                        bank3, bb3 = k.pf[3 * a + 2], k.b_pf[3 * a + 2]
                        bin_ = b_in[a][gi]
                        tsl = slice(j * 128, (j + 1) * 128)
                        uc = uT[:].rearrange("p c h -> p (c h)")[:, col:col + 1]
                        wc = wT[:].rearrange("p c h -> p (c h)")[:, col:col + 1]
                        flc = flT[:].rearrange("p c h -> p (c h)")[:, col:col + 1]
                        dcc = decT[:].rearrange("p c h -> p (c h)")[:, col:col + 1]
                        ps_s = bank1[:, 264:392]
                        for dc in range(2):
                            S.op(S.pe, lambda: nc.tensor.matmul(ps_s, lhsT=kTt[a][gi][:, dc, tsl], rhs=qTt[a][gi][:, dc, tsl],
                                                                start=(dc == 0), stop=(dc == 1)),
                                 reads=[bin_], writes=[bb1], inc=(dc == 1))
                        S.op(S.dve, lambda: nc.vector.tensor_tensor(out=PT[a][par][:], in0=ps_s, in1=k.mlmask_bf, op=ALU.mult),
                             reads=[bb1, k.b_cbf], writes=[b_PT[a][par]])
                        for (dst, bd, sc, bs) in ((vu[a][par], b_vu[a][par], uc, b_u), (vw[a][par], b_vw[a][par], wc, b_w)):
                            S.op(S.act, lambda: nc.scalar.activation(out=dst[:, 0:256], in_=vtm[a][gi][:, j, :], func=AF.Copy, scale=sc),
                                 reads=[bin_, bs], writes=[bd])
                            S.op(S.act, lambda: nc.scalar.copy(out=dst[:, 256:257], in_=sc), reads=[bs], writes=[bd])
                        ps_n = bank3[:, 0:257]
                        S.op(S.pe, lambda: nc.tensor.matmul(ps_n, lhsT=PT[a][par][:], rhs=vu[a][par][:, 0:257],
                                                            start=True, stop=(c == 0)),
                             reads=[b_PT[a][par], b_vu[a][par]], writes=[bb3], inc=(c == 0))
                        if c > 0:
                            cb, bcb = Cbf[a][1 - par], b_Cbf[a][1 - par]
                            for dc in range(2):
                                S.op(S.pe, lambda: nc.tensor.matmul(ps_n, lhsT=qTt[a][gi][:, dc, tsl], rhs=cb[:, dc, 0:257],
                                                                    start=False, stop=(dc == 1)),
                                     reads=[bin_, bcb], writes=[bb3], inc=(dc == 1))
                        ps_c = [bank1[:, 0:257], bank2[:, 0:257]]
                        bbc = [bb1, bb2]
                        for dc in range(2):
                            S.op(S.pe, lambda: nc.tensor.matmul(ps_c[dc], lhsT=ktm[a][gi][:, j, dc * 128:(dc + 1) * 128],
                                                                rhs=vw[a][par][:, 0:257], start=True, stop=True),
                                 reads=[bin_, b_vw[a][par]], writes=[bbc[dc]])
                        for dc in range(2):
                            if c == 0:
                                S.op(S.dve, lambda: nc.vector.tensor_copy(out=C32[a][:, dc, 0:257], in_=ps_c[dc]),
                                     reads=[bbc[dc]], writes=[b_C32[a]])
                            else:
                                S.op(S.dve, lambda: nc.vector.scalar_tensor_tensor(out=C32[a][:, dc, 0:257], in0=C32[a][:, dc, 0:257],
                                                                                   scalar=dcc, in1=ps_c[dc],
                                                                                   op0=ALU.mult, op1=ALU.add),
                                     reads=[bbc[dc], b_C32[a], b_dec], writes=[b_C32[a]])
                        S.op(S.pool, lambda: nc.gpsimd.tensor_copy(out=Cbf[a][par][:, :, 0:257], in_=C32[a][:, :, 0:257]),
                             reads=[b_C32[a]], writes=[b_Cbf[a][par]])
                        smt = sm[a]
                        S.op(S.act, lambda: nc.scalar.activation(out=smt[:, 5:6], in_=bank3[:, 256:257], func=AF.Abs),
                             reads=[bb3], writes=[b_sm[a][5]])
                        S.op(S.dve, lambda: nc.vector.tensor_tensor(out=smt[:, 0:1], in0=smt[:, 5:6], in1=flc, op=ALU.max),
                             reads=[b_sm[a][5], b_fl], writes=[b_sm[a][0]])
                        S.op(S.dve, lambda: nc.vector.reciprocal(out=smt[:, 1:2], in_=smt[:, 0:1]),
                             reads=[b_sm[a][0]], writes=[b_sm[a][1]])
                        S.op(S.act, lambda: nc.scalar.activation(out=hs[a][:], in_=bank3[:, 0:256], func=AF.Copy, scale=smt[:, 1:2]),
                             reads=[bb3, b_sm[a][1]], writes=[b_hs[a]])
                        S.op(S.act, lambda: nc.scalar.activation(out=jk[a][:], in_=hs[a][:], func=AF.Square, accum_out=smt[:, 2:3]),
                             reads=[b_hs[a]], writes=[b_jk[a], b_sm[a][2]])
                        S.op(S.act, lambda: nc.scalar.activation(out=smt[:, 3:4], in_=smt[:, 2:3], func=AF.Sqrt,
                                                                 bias=k.eps_t[:, 0:1], scale=1.0 / 256),
                             reads=[b_sm[a][2], k.b_eps], writes=[b_sm[a][3]])
                        S.op(S.dve, lambda: nc.vector.reciprocal(out=smt[:, 4:5], in_=smt[:, 3:4]),
                             reads=[b_sm[a][3]], writes=[b_sm[a][4]])
                        S.op(S.dve, lambda: nc.vector.scalar_tensor_tensor(out=t1[a][:], in0=hs[a][:], scalar=smt[:, 4:5],
                                                                           in1=mlng[:, h * 256:(h + 1) * 256],
                                                                           op0=ALU.mult, op1=ALU.mult),
                             reads=[b_hs[a], b_sm[a][4], k.b_pp], writes=[b_t1[a]])
                        S.op(S.pool, lambda: nc.gpsimd.tensor_tensor(out=yb[a][:], in0=t1[a][:], in1=otm[a][gi][:, j, :], op=ALU.mult),
                             reads=[b_t1[a], bin_], writes=[b_yb[a]])
                        pb, bpb = k.pb[a], k.b_pb[a]
                        for dc in range(2):
                            S.op(S.pe, lambda: nc.tensor.transpose(out=pb[:, dc * 128:(dc + 1) * 128], in_=yb[a][:, dc * 128:(dc + 1) * 128],
                                                                   identity=k.ident_bf),
                                 reads=[b_yb[a], k.b_cbf], writes=[bpb], inc=(dc == 1))
                        S.op(S.act, lambda: nc.scalar.copy(out=yTt[a][gi][:, :, tsl], in_=pb[:, 0:256].rearrange("p (c t) -> p c t", c=2)),
                             reads=[bpb], writes=[b_y[a][gi]])
                for a in range(NH):
                    h = heads[a]
                    S.dma(S.sp, sem_y[a][gi], k.yt_ml[2 * h:2 * h + 2, :, g * 512:(g + 1) * 512].rearrange("c p t -> p c t"),
                          yTt[a][gi][:], reads=[b_y[a][gi]])
        S.barrier()


def stage_xa(k):
    nc, S = k.nc, k.S
    with ExitStack() as st:
        mt = k.sb("xa_mt", [128, 2, D], F32, st)
        mn = k.sb("xa_mn", [128, D], BF16, st)
        jk = k.sb("xa_jk", [128, D], BF16, st)
        memT = k.sb("xa_memT", [128, 8, N_MEM], BF16, st)
        knT = k.sb("xa_knT", [128, 4, 2, N_MEM], BF16, st)
        vmem = k.sb("xa_vmem", [128, 2, D], BF16, st)
        kn = k.sb("xa_kn", [128, 256], BF16, st)
        sm = k.sb("xa_sm", [128, 64], F32, st)
        wkv = [k.sb(f"xa_w{i}", [128, 8, 512], BF16, st) for i in range(4)]
        b_mt, b_mn, b_jk, b_memT, b_knT, b_vmem, b_kn = Buf(), Buf(), Buf(), Buf(), Buf(), Buf(), Buf()
        b_w = [Buf() for _ in range(4)]
        sem = S.new_sem("d_xa0")
        sem_w = [S.new_sem(f"d_xaw{i}") for i in range(4)]
        S.dma(S.sp, sem, mt[:], k.mem.rearrange("(j p) d -> p j d", p=128), writes=[b_mt])
        for i in range(4):
            S.dma(S.pool, sem_w[i], wkv[i][:], k.w_mem_kv[:, i * 512:(i + 1) * 512].rearrange("(c p) n -> p c n", p=128),
                  writes=[b_w[i]])
        nsm = [0]

        def smcol():
            c = nsm[0]
            nsm[0] += 1
            return sm[:, c:c + 1], Buf()

        def rstd_of(src_ap, src_bufs, n):
            ss, bss = smcol()
            rm, brm = smcol()
            rs, brs = smcol()
            S.op(S.act, lambda: nc.scalar.activation(out=jk[:, 0:n], in_=src_ap, func=AF.Square, accum_out=ss),
                 reads=src_bufs, writes=[b_jk, bss])
            S.op(S.act, lambda: nc.scalar.activation(out=rm, in_=ss, func=AF.Sqrt, bias=k.eps_t[:, 0:1], scale=1.0 / n),
                 reads=[bss, k.b_eps], writes=[brm])
            S.op(S.dve, lambda: nc.vector.reciprocal(out=rs, in_=rm), reads=[brm], writes=[brs])
            return rs, brs

        gmem = k.ppv("gmem")
        for j in range(2):
            rs, brs = rstd_of(mt[:, j, :], [b_mt], D)
            S.op(S.dve, lambda: nc.vector.scalar_tensor_tensor(out=mn[:], in0=mt[:, j, :], scalar=rs, in1=gmem,
                                                               op0=ALU.mult, op1=ALU.mult),
                 reads=[b_mt, brs, k.b_pp], writes=[b_mn])
            pb, bpb = k.pb[j], k.b_pb[j]
            for c in range(8):
                S.op(S.pe, lambda: nc.tensor.transpose(out=pb[:, c * 128:(c + 1) * 128], in_=mn[:, c * 128:(c + 1) * 128],
                                                       identity=k.ident_bf),
                     reads=[b_mn, k.b_cbf], writes=[bpb], inc=(c == 7))
            S.op(S.act, lambda: nc.scalar.copy(out=memT[:, :, j * 128:(j + 1) * 128], in_=pb[:].rearrange("p (c t) -> p c t", c=8)),
                 reads=[bpb], writes=[b_memT])
        kng = k.ppv("kng")
        npf = [0]

        def next_pf():
            i = npf[0] % 6
            npf[0] += 1
            return k.pf[i], k.b_pf[i]
        for j in range(2):
            for t4 in range(4):
                p, bp = next_pf()
                for c in range(8):
                    S.op(S.pe, lambda: nc.tensor.matmul(p[:], lhsT=memT[:, c, j * 128:(j + 1) * 128], rhs=wkv[t4][:, c, :],
                                                        start=(c == 0), stop=(c == 7)),
                         reads=[b_memT, b_w[t4]], writes=[bp], inc=(c == 7))
                if t4 < 2:
                    for hh in range(2):
                        h = t4 * 2 + hh
                        rs, brs = rstd_of(p[:, hh * 256:(hh + 1) * 256], [bp], 256)
                        S.op(S.dve, lambda: nc.vector.scalar_tensor_tensor(out=kn[:], in0=p[:, hh * 256:(hh + 1) * 256], scalar=rs,
                                                                           in1=kng, op0=ALU.mult, op1=ALU.mult),
                             reads=[bp, brs, k.b_pp], writes=[b_kn])
                        pb, bpb = k.pb[hh], k.b_pb[hh]
                        for dc in range(2):
                            S.op(S.pe, lambda: nc.tensor.transpose(out=pb[:, dc * 128:(dc + 1) * 128], in_=kn[:, dc * 128:(dc + 1) * 128],
                                                                   identity=k.ident_bf),
                                 reads=[b_kn, k.b_cbf], writes=[bpb], inc=(dc == 1))
                        S.op(S.act, lambda: nc.scalar.copy(out=knT[:, h, :, j * 128:(j + 1) * 128],
                                                           in_=pb[:, 0:256].rearrange("p (c t) -> p c t", c=2)),
                             reads=[bpb], writes=[b_knT])
                else:
                    S.op(S.dve, lambda: nc.vector.tensor_copy(out=vmem[:, j, (t4 - 2) * 512:(t4 - 1) * 512], in_=p[:]),
                         reads=[bp], writes=[b_vmem])
        qx = [k.sb(f"xa_qx{i}", [128, 2, 512], BF16, st) for i in range(2)]
        b_qx = [Buf(), Buf()]
        sem_q = [S.new_sem("d_xaq0"), S.new_sem("d_xaq1")]
        yx = [k.sb(f"xa_yx{i}", [128, 2, 512], BF16, st) for i in range(2)]
        b_yx = [Buf(), Buf()]
        sem_y = [S.new_sem("d_xay0"), S.new_sem("d_xay1")]
        sq = k.sb("xa_sq", [128, 2, 512], BF16, st)
        rq = k.sb("xa_rq", [128, 512], F32, st)
        rq2 = k.sb("xa_rq2", [128, 512], F32, st)
        qn = k.sb("xa_qn", [128, 2, 512], BF16, st)
        pT = k.sb("xa_pT", [128, 2, 512], BF16, st)
        rden = k.sb("xa_rden", [128, 512], F32, st)
        b_sq, b_rq, b_rq2, b_qn, b_pT, b_rden = Buf(), Buf(), Buf(), Buf(), Buf(), Buf()
        qng = k.ppv("qng")
        it = 0
        for h in range(4):
            for tt in range(NTT):
                i = it % 2
                it += 1
                cols = slice(tt * 512, (tt + 1) * 512)
                S.dma(S.sp, sem_q[i], qx[i][:], k.qt_x[2 * h:2 * h + 2, :, cols].rearrange("c p t -> p c t"), writes=[b_qx[i]])
                S.op(S.act, lambda: nc.scalar.activation(out=sq[:], in_=qx[i][:], func=AF.Square), reads=[b_qx[i]], writes=[b_sq])
                p, bp = next_pf()
                for dc in range(2):
                    S.op(S.pe, lambda: nc.tensor.matmul(p[:], lhsT=k.ones_bf, rhs=sq[:, dc, :], start=(dc == 0), stop=(dc == 1)),
                         reads=[b_sq, k.b_cbf], writes=[bp], inc=(dc == 1))
                S.op(S.act, lambda: nc.scalar.activation(out=rq[:], in_=p[:], func=AF.Sqrt, bias=k.eps_t[:, 0:1], scale=1.0 / 256),
                     reads=[bp, k.b_eps], writes=[b_rq])
                S.op(S.dve, lambda: nc.vector.reciprocal(out=rq2[:], in_=rq[:]), reads=[b_rq], writes=[b_rq2])
                for dc in range(2):
                    S.op(S.dve, lambda: nc.vector.scalar_tensor_tensor(out=qn[:, dc, :], in0=qx[i][:, dc, :], scalar=qng[:, dc:dc + 1],
                                                                       in1=rq2[:], op0=ALU.mult, op1=ALU.mult),
                         reads=[b_qx[i], b_rq2, k.b_pp], writes=[b_qn])
                for mc in range(2):
                    p, bp = next_pf()
                    for dc in range(2):
                        S.op(S.pe, lambda: nc.tensor.matmul(p[:], lhsT=knT[:, h, dc, mc * 128:(mc + 1) * 128], rhs=qn[:, dc, :],
                                                            start=(dc == 0), stop=(dc == 1)),
                             reads=[b_knT, b_qn], writes=[bp], inc=(dc == 1))
                    S.op(S.act, lambda: nc.scalar.activation(out=pT[:, mc, :], in_=p[:], func=AF.Exp, scale=1.0 / 16),
                         reads=[bp], writes=[b_pT])
                p, bp = next_pf()
                for mc in range(2):
                    S.op(S.pe, lambda: nc.tensor.matmul(p[:], lhsT=k.ones_bf, rhs=pT[:, mc, :], start=(mc == 0), stop=(mc == 1)),
                         reads=[b_pT, k.b_cbf], writes=[bp], inc=(mc == 1))
                S.op(S.dve, lambda: nc.vector.reciprocal(out=rden[:], in_=p[:]), reads=[bp], writes=[b_rden])
                for dvc in range(2):
                    p, bp = next_pf()
                    for mc in range(2):
                        S.op(S.pe, lambda: nc.tensor.matmul(p[:], lhsT=vmem[:, mc, h * 256 + dvc * 128:h * 256 + (dvc + 1) * 128],
                                                            rhs=pT[:, mc, :], start=(mc == 0), stop=(mc == 1)),
                             reads=[b_vmem, b_pT], writes=[bp], inc=(mc == 1))
                    S.op(S.dve, lambda: nc.vector.tensor_tensor(out=yx[i][:, dvc, :], in0=p[:], in1=rden[:], op=ALU.mult),
                         reads=[bp, b_rden], writes=[b_yx[i]])
                S.dma(S.sp, sem_y[i], k.yt_x[2 * h:2 * h + 2, :, cols].rearrange("c p t -> p c t"), yx[i][:], reads=[b_yx[i]])
        S.barrier()


def stage_4a(k):
    nc, S = k.nc, k.S
    with ExitStack() as st:
        Wb = [k.sb(f"a_W{b}", [128, 8, D], BF16, st) for b in range(4)]
        b_W = [Buf() for _ in range(4)]
        sem_w = [S.new_sem(f"d_4aw{i}") for i in range(4)]
        for b, src in enumerate((k.w_sb, k.w_ml, k.w_x, k.w_out)):
            for hf in range(2):
                S.dma(S.pool, sem_w[b], Wb[b][:, :, hf * 512:(hf + 1) * 512],
                      src[:, hf * 512:(hf + 1) * 512].rearrange("(c p) n -> p c n", p=128), writes=[b_W[b]])
        yt = [k.sb(f"a_y{i}", [128, 8, 512], BF16, st) for i in range(2)]
        gt = [k.sb(f"a_g{i}", [128, 8, 512], BF16, st) for i in range(2)]
        b_yt = [Buf(), Buf()]
        b_gt = [Buf(), Buf()]
        sem_in = [S.new_sem("d_4ain0"), S.new_sem("d_4ain1")]
        mixed = k.sb("a_mixed", [128, 8, 512], F32, st)
        mixbf = k.sb("a_mixbf", [128, 8, 512], BF16, st)
        tmp = [k.sb(f"a_tmp{i}", [128, 512], F32, st) for i in range(2)]
        b_mixed = [Buf() for _ in range(8)]
        b_mixbf, b_tmp = Buf(), [Buf(), Buf()]
        xt = k.sb("a_xt", [128, 4, D], F32, st)
        b_xt = [Buf() for _ in range(4)]
        sem_x = S.new_sem("d_4ax")
        sem_x1 = S.new_sem("d_4ax1")
        h2 = k.sb("a_h2", [128, D], BF16, st)
        jk = k.sb("a_jk", [128, D], BF16, st)
        h2T = [k.sb(f"a_h2T{i}", [128, 8, 512], BF16, st) for i in range(2)]
        b_h2, b_jk, b_h2T = Buf(), Buf(), [Buf(), Buf()]
        sem_h = [S.new_sem("d_4ah0"), S.new_sem("d_4ah1")]
        sm = k.sb("a_sm", [128, 3 * NTB], F32, st)
        gmlp = k.ppv("gmlp")
        ysrc = (k.yt_sb, k.yt_ml, k.yt_x)
        npf = [0]
        ntmp = [0]
        itc = [0]
        h2b = [h2, k.sb("a_h2b", [128, D], BF16, st)]
        b_h2b = [b_h2, Buf()]

        def branch(tt, b):
            cols = slice(tt * 512, (tt + 1) * 512)
            i = itc[0] % 2
            itc[0] += 1
            S.dma(S.sp, sem_in[i], yt[i][:], ysrc[b][:, :, cols].rearrange("c p t -> p c t"), writes=[b_yt[i]])
            S.dma(S.sp, sem_in[i], gt[i][:], k.g_scr[b * 8:(b + 1) * 8, :, cols].rearrange("c p t -> p c t"), writes=[b_gt[i]])
            for n in range(8):
                p, bp = k.pf[npf[0] % 6], k.b_pf[npf[0] % 6]
                npf[0] += 1
                for fc in range(8):
                    S.op(S.pe, lambda: nc.tensor.matmul(p[:], lhsT=Wb[b][:, fc, n * 128:(n + 1) * 128], rhs=yt[i][:, fc, :],
                                                        start=(fc == 0), stop=(fc == 7)),
                         reads=[b_W[b], b_yt[i]], writes=[bp], inc=(fc == 7))
                if b == 0:
                    S.op(S.dve, lambda: nc.vector.tensor_tensor(out=mixed[:, n, :], in0=p[:], in1=gt[i][:, n, :], op=ALU.mult),
                         reads=[bp, b_gt[i]], writes=[b_mixed[n]])
                else:
                    ti = ntmp[0] % 2
                    ntmp[0] += 1
                    S.op(S.dve, lambda: nc.vector.tensor_tensor(out=tmp[ti][:], in0=p[:], in1=gt[i][:, n, :], op=ALU.mult),
                         reads=[bp, b_gt[i]], writes=[b_tmp[ti]])
                    S.op(S.pool, lambda: nc.gpsimd.tensor_tensor(out=mixed[:, n, :], in0=mixed[:, n, :], in1=tmp[ti][:], op=ALU.add),
                         reads=[b_tmp[ti], b_mixed[n]], writes=[b_mixed[n]])
                    if b == 2:
                        S.op(S.act, lambda: nc.scalar.copy(out=mixbf[:, n, :], in_=mixed[:, n, :]),
                             reads=[b_mixed[n]], writes=[b_mixbf])

        def transposes(tt, j):
            hi = tt % 2
            hb_, bhb = h2b[j % 2], b_h2b[j % 2]
            pb, bpb = k.pb[j % 2], k.b_pb[j % 2]
            for c in range(8):
                S.op(S.pe, lambda: nc.tensor.transpose(out=pb[:, c * 128:(c + 1) * 128], in_=hb_[:, c * 128:(c + 1) * 128],
                                                       identity=k.ident_bf),
                     reads=[bhb, k.b_cbf], writes=[bpb], inc=(c == 7))
            S.op(S.act, lambda: nc.scalar.copy(out=h2T[hi][:, :, j * 128:(j + 1) * 128], in_=pb[:].rearrange("p (c t) -> p c t", c=8)),
                 reads=[bpb], writes=[b_h2T[hi]])

        def outproj(tt):
            cols = slice(tt * 512, (tt + 1) * 512)
            hi = tt % 2
            S.dma(S.sp, sem_x, xt[:], k.x[cols, :].rearrange("(j p) d -> p j d", p=128), writes=b_xt)
            for j in range(4):
                tb = tt * 4 + j
                for hf in range(2):
                    p, bp = k.pf[npf[0] % 6], k.b_pf[npf[0] % 6]
                    npf[0] += 1
                    for fc in range(8):
                        S.op(S.pe, lambda: nc.tensor.matmul(p[:], lhsT=mixbf[:, fc, j * 128:(j + 1) * 128],
                                                            rhs=Wb[3][:, fc, hf * 512:(hf + 1) * 512],
                                                            start=(fc == 0), stop=(fc == 7)),
                             reads=[b_W[3], b_mixbf], writes=[bp], inc=(fc == 7))
                    S.op(S.dve, lambda: nc.vector.tensor_tensor(out=xt[:, j, hf * 512:(hf + 1) * 512], in0=p[:],
                                                                in1=xt[:, j, hf * 512:(hf + 1) * 512], op=ALU.add),
                         reads=[bp, b_xt[j]], writes=[b_xt[j]])
                if j > 0:
                    transposes(tt, j - 1)
                ss, rm, rs = sm[:, 3 * tb:3 * tb + 1], sm[:, 3 * tb + 1:3 * tb + 2], sm[:, 3 * tb + 2:3 * tb + 3]
                b_ss, b_rm, b_rs = Buf(), Buf(), Buf()
                hb_, bhb = h2b[j % 2], b_h2b[j % 2]
                S.op(S.act, lambda: nc.scalar.activation(out=jk[:], in_=xt[:, j, :], func=AF.Square, accum_out=ss),
                     reads=[b_xt[j]], writes=[b_jk, b_ss])
                S.op(S.act, lambda: nc.scalar.activation(out=rm, in_=ss, func=AF.Sqrt, bias=k.eps_t[:, 0:1], scale=1.0 / D),
                     reads=[b_ss, k.b_eps], writes=[b_rm])
                S.op(S.dve, lambda: nc.vector.reciprocal(out=rs, in_=rm), reads=[b_rm], writes=[b_rs])
                S.op(S.dve, lambda: nc.vector.scalar_tensor_tensor(out=hb_[:], in0=xt[:, j, :], scalar=rs, in1=gmlp,
                                                                   op0=ALU.mult, op1=ALU.mult),
                     reads=[b_xt[j], b_rs, k.b_pp], writes=[bhb])
            transposes(tt, 3)
            S.dma(S.pool, sem_x1, k.x1_scr[cols, :].rearrange("(j p) d -> p j d", p=128), xt[:], reads=b_xt)
            S.dma(S.pool, sem_h[hi], k.h2t_scr[:, :, cols].rearrange("c p t -> p c t"), h2T[hi][:], reads=[b_h2T[hi]])

        for b in range(3):
            branch(0, b)
        for tt in range(NTT):
            if tt + 1 < NTT:
                branch(tt + 1, 0)
            outproj(tt)
            if tt + 1 < NTT:
                branch(tt + 1, 1)
                branch(tt + 1, 2)
        S.barrier()


def stage_4b(k):
    nc, S = k.nc, k.S
    with ExitStack() as st:
        W1 = k.sb("b_W1", [128, 8, 4096], BF16, st)
        W2 = k.sb("b_W2", [128, 32, D], BF16, st)
        b_W1 = [Buf() for _ in range(4)]
        b_W2 = [Buf() for _ in range(4)]
        sem_w = [S.new_sem(f"d_4bw{i}") for i in range(8)]
        for q in range(4):
            S.dma(S.pool, sem_w[q], W1[:, :, q * 1024:(q + 1) * 1024],
                  k.w_ff1[:, q * 1024:(q + 1) * 1024].rearrange("(c p) n -> p c n", p=128), writes=[b_W1[q]])
        for q in range(4):
            S.dma(S.pool, sem_w[4 + q], W2[:, q * 8:(q + 1) * 8, :],
                  k.w_ff2[q * 1024:(q + 1) * 1024, :].rearrange("(c p) n -> p c n", p=128), writes=[b_W2[q]])
        h2T = [k.sb(f"b_h2T{i}", [128, 8, 512], BF16, st) for i in range(2)]
        b_h2T = [Buf(), Buf()]
        sem_h = [S.new_sem("d_4bh0"), S.new_sem("d_4bh1")]
        x1b = [k.sb(f"b_x1{i}", [128, D], F32, st) for i in range(2)]
        b_x1 = [Buf(), Buf()]
        sem_x = [S.new_sem("d_4bx0"), S.new_sem("d_4bx1")]
        sem_o = [S.new_sem("d_4bo0"), S.new_sem("d_4bo1")]
        aT = k.sb("b_aT", [128, 32, 512], BF16, st)
        b_aT = [Buf() for _ in range(32)]
        rr = [k.sb(f"b_r{i}", [128, 512], F32, st) for i in range(2)]
        b_rr = [Buf(), Buf()]
        npf = 0
        nx = 0
        for tt in range(NTT):
            cols = slice(tt * 512, (tt + 1) * 512)
            hi = tt % 2
            S.dma(S.sp, sem_h[hi], h2T[hi][:], k.h2t_scr[:, :, cols].rearrange("c p t -> p c t"), writes=[b_h2T[hi]])
            for fc in range(32):
                p, bp = k.pf[npf % 6], k.b_pf[npf % 6]
                npf += 1
                for c in range(8):
                    S.op(S.pe, lambda: nc.tensor.matmul(p[:], lhsT=W1[:, c, fc * 128:(fc + 1) * 128], rhs=h2T[hi][:, c, :],
                                                        start=(c == 0), stop=(c == 7)),
                         reads=[b_W1[fc // 8], b_h2T[hi]], writes=[bp], inc=(c == 7))
                ri = fc % 2
                S.op(S.act, lambda: nc.scalar.activation(out=rr[ri][:], in_=p[:], func=AF.Relu), reads=[bp], writes=[b_rr[ri]])
                S.op(S.dve, lambda: nc.vector.tensor_tensor(out=aT[:, fc, :], in0=p[:], in1=rr[ri][:], op=ALU.mult),
                     reads=[bp, b_rr[ri]], writes=[b_aT[fc]])
            for j in range(4):
                tb = tt * 4 + j
                xi = nx % 2
                nx += 1
                S.dma(S.sp, sem_x[xi], x1b[xi][:], k.x1_scr[tb * 128:(tb + 1) * 128, :], writes=[b_x1[xi]])
                for hf in range(2):
                    p, bp = k.pf[npf % 6], k.b_pf[npf % 6]
                    npf += 1
                    for fc in range(32):
                        S.op(S.pe, lambda: nc.tensor.matmul(p[:], lhsT=aT[:, fc, j * 128:(j + 1) * 128],
                                                            rhs=W2[:, fc, hf * 512:(hf + 1) * 512],
                                                            start=(fc == 0), stop=(fc == 31)),
                             reads=[b_W2[fc // 8], b_aT[fc]], writes=[bp], inc=(fc == 31))
                    S.op(S.dve, lambda: nc.vector.tensor_tensor(out=x1b[xi][:, hf * 512:(hf + 1) * 512], in0=p[:],
                                                                in1=x1b[xi][:, hf * 512:(hf + 1) * 512], op=ALU.add),
                         reads=[bp, b_x1[xi]], writes=[b_x1[xi]])
                S.dma(S.pool, sem_o[xi], k.out[tb * 128:(tb + 1) * 128, :], x1b[xi][:], reads=[b_x1[xi]], is_output=True)
        S.barrier()


def ml_prep(k, st, T=None, stp=None, phase=0):
    nc, S = k.nc, k.S
    if phase == 1:
        return _ml_prep_compute(k, T, stp)
    T = K()
    T.tab = k.sb("ml_tab", [128, 3, 32, 4], F32, st)
    T.Mb = k.sb("ml_Mb", [128, 33, 4], F32, st)
    T.uT = k.sb("ml_u", [128, 128], F32, st)
    T.wT = k.sb("ml_w", [128, 128], F32, st)
    T.flT = k.sb("ml_fl", [128, 128], F32, st)
    T.decT = k.sb("ml_dec", [128, 128], F32, st)
    nl16 = k.sb("ml_nl16", [128, 1], F32, st)
    tmp = k.sb("ml_tmp", [128, 128], F32, st)
    tab, Mb = T.tab, T.Mb
    b_tab, b_Mb, b_c, b_tmp = Buf(), Buf(), Buf(), Buf()
    T.b_u, T.b_w, T.b_fl, T.b_dec = Buf(), Buf(), Buf(), Buf()
    T.nl16, T.tmp, T.b_tab, T.b_Mb, T.b_c, T.b_tmp = nl16, tmp, b_tab, b_Mb, b_c, b_tmp
    return T


def _ml_prep_compute(k, T, st2):
    nc, S = k.nc, k.S
    tab, Mb, nl16, tmp = T.tab, T.Mb, T.nl16, T.tmp
    b_tab, b_Mb, b_c, b_tmp = T.b_tab, T.b_Mb, T.b_c, T.b_tmp
    S.op(S.dve, lambda: nc.vector.memset(nl16[:], -LN16), writes=[b_c])
    S.op(S.dve, lambda: nc.vector.memset(Mb[:, 0, :], 0.0), writes=[b_Mb])
    if True:
        fp = k.sb("ml_fp", [4, S_LEN], F32, st2)
        ip = k.sb("ml_ip", [4, S_LEN], F32, st2)
        Fn = k.sb("ml_Fn", [4, S_LEN], F32, st2)
        on = k.sb("ml_on", [4, S_LEN], F32, st2)
        b_fp, b_ip, b_Fn, b_on = Buf(), Buf(), Buf(), Buf()
        sg = S.new_sem("d_mlg")
        S.dma(S.sp, sg, fp[:], k.gif[4:8, :], writes=[b_fp])
        S.dma(S.sp, sg, ip[:], k.gif[0:4, :], writes=[b_ip])
        S.op(S.dve, lambda: nc.vector.memset(on[:], 1.0), writes=[b_on])
        S.op(S.act, lambda: nc.scalar.activation(out=fp[:], in_=fp[:], func=AF.Exp, scale=-1.0), reads=[b_fp], writes=[b_fp])
        S.op(S.act, lambda: nc.scalar.activation(out=fp[:], in_=fp[:], func=AF.Ln, bias=1.0), reads=[b_fp], writes=[b_fp])
        S.op(S.dve, lambda: nc.vector.tensor_tensor_scan(out=Fn[:], data0=on[:], data1=fp[:], initial=0.0,
                                                         op0=ALU.mult, op1=ALU.add),
             reads=[b_on, b_fp], writes=[b_Fn])
        S.op(S.dve, lambda: nc.vector.tensor_tensor(out=ip[:], in0=ip[:], in1=Fn[:], op=ALU.add),
             reads=[b_ip, b_Fn], writes=[b_ip])
        S.op(S.dve, lambda: nc.vector.tensor_tensor_scan(out=fp[:], data0=ip[:], data1=ip[:], initial=0.0,
                                                         op0=ALU.max, op1=ALU.max),
             reads=[b_ip], writes=[b_fp])
        pt, bpt = k.pf[0], k.b_pf[0]
        ptv = pt[:, 0:384].rearrange("p (q c h) -> p q c h", q=3, c=32)
        idf = k.ppv("ident")
        for q, (X, bX) in enumerate(((Fn, b_Fn), (ip, b_ip), (fp, b_fp))):
            for c in range(32):
                S.op(S.pe, lambda: nc.tensor.transpose(out=ptv[:, q, c, :], in_=X[0:4, c * 128:(c + 1) * 128],
                                                       identity=idf[0:4, 0:4]),
                     reads=[bX, k.b_pp], writes=[bpt], inc=(q == 2 and c == 31))
        S.op(S.dve, lambda: nc.vector.tensor_copy(out=tab[:], in_=ptv), reads=[bpt], writes=[b_tab])
    pm, bpm = k.pf[1], k.b_pf[1]
    o_sel = PP["sel127"][0]
    S.op(S.pe, lambda: nc.tensor.matmul(pm[:, 0:128], lhsT=k.ppt[:, o_sel:o_sel + 128],
                                        rhs=tab[:, 2].rearrange("p c h -> p (c h)"), start=True, stop=True),
         reads=[b_tab, k.b_pp], writes=[bpm])
    S.op(S.dve, lambda: nc.vector.tensor_copy(out=Mb[:, 1:33, :], in_=pm[:, 0:128].rearrange("p (c h) -> p c h", c=32)),
         reads=[bpm], writes=[b_Mb])

    def table(dst, bd, in0, in1, bias):
        S.op(S.dve, lambda: nc.vector.tensor_tensor(out=tmp[:].rearrange("p (c h) -> p c h", c=32), in0=in0, in1=in1, op=ALU.subtract),
             reads=[b_tab, b_Mb], writes=[b_tmp])
        if bias:
            S.op(S.act, lambda: nc.scalar.activation(out=dst[:], in_=tmp[:], func=AF.Exp, bias=nl16[:, 0:1]),
                 reads=[b_tmp, b_c], writes=[bd])
        else:
            S.op(S.act, lambda: nc.scalar.activation(out=dst[:], in_=tmp[:], func=AF.Exp), reads=[b_tmp], writes=[bd])
    table(T.uT, T.b_u, tab[:, 1], Mb[:, 0:32, :], True)
    table(T.wT, T.b_w, tab[:, 1], Mb[:, 1:33, :], True)
    table(T.flT, T.b_fl, tab[:, 0], Mb[:, 0:32, :], False)
    table(T.decT, T.b_dec, Mb[:, 0:32, :], Mb[:, 1:33, :], False)
    return T


def xa_prep(k, st, X=None, st2=None, phase=0):
    nc, S = k.nc, k.S
    if phase == 0:
        X = K()
        X.knT = k.sb("xa_knT", [128, 4, 2, N_MEM], BF16, st)
        X.vmem = k.sb("xa_vmem", [128, 2, D], BF16, st)
        X.b_knT, X.b_vmem = Buf(), Buf()
        return X
    if phase == 2:
        return X.compute()
    knT, vmem = X.knT, X.vmem
    if True:
        mt = k.sb("xa_mt", [128, 2, D], F32, st2)
        mn = k.sb("xa_mn", [128, D], BF16, st2)
        jk = k.sb("xa_jk", [128, D], BF16, st2)
        memT = k.sb("xa_memT", [128, 8, N_MEM], BF16, st2)
        kn = k.sb("xa_kn", [128, 256], BF16, st2)
        sm = k.sb("xa_sm", [128, 64], F32, st2)
        wkv = [k.sb(f"xa_w{i}", [128, 8, 512], BF16, st2) for i in range(4)]
        b_mt, b_mn, b_jk, b_memT, b_kn = Buf(), Buf(), Buf(), Buf(), Buf()
        b_w = [Buf() for _ in range(4)]
        sem = S.new_sem("d_xa0")
        sem_w = [S.new_sem(f"d_xaw{i}") for i in range(4)]
        S.dma(S.sp, sem, mt[:], k.mem.rearrange("(j p) d -> p j d", p=128), writes=[b_mt])
        for i in range(4):
            S.dma(S.pool, sem_w[i], wkv[i][:], k.w_mem_kv[:, i * 512:(i + 1) * 512].rearrange("(c p) n -> p c n", p=128),
                  writes=[b_w[i]])
        nsm = [0]

        def smcol():
            c = nsm[0]
            nsm[0] += 1
            return sm[:, c:c + 1], Buf()

        def compute():
            _xa_compute()
        X.compute = compute

    def _xa_compute():
        def rstd_of(src_ap, src_bufs, n):
            ss, bss = smcol()
            rm, brm = smcol()
            rs, brs = smcol()
            S.op(S.act, lambda: nc.scalar.activation(out=jk[:, 0:n], in_=src_ap, func=AF.Square, accum_out=ss),
                 reads=src_bufs, writes=[b_jk, bss])
            S.op(S.act, lambda: nc.scalar.activation(out=rm, in_=ss, func=AF.Ln, bias=k.eps_t[:, 0:1], scale=1.0 / n),
                 reads=[bss, k.b_eps], writes=[brm])
            S.op(S.act, lambda: nc.scalar.activation(out=rs, in_=rm, func=AF.Exp, scale=-0.5), reads=[brm], writes=[brs])
            return rs, brs

        gmem = k.ppv("gmem")
        for j in range(2):
            rs, brs = rstd_of(mt[:, j, :], [b_mt], D)
            S.op(S.dve, lambda: nc.vector.scalar_tensor_tensor(out=mn[:], in0=mt[:, j, :], scalar=rs, in1=gmem,
                                                               op0=ALU.mult, op1=ALU.mult),
                 reads=[b_mt, brs, k.b_pp], writes=[b_mn])
            pb, bpb = k.pb[j], k.b_pb[j]
            for c in range(8):
                S.op(S.pe, lambda: nc.tensor.transpose(out=pb[:, c * 128:(c + 1) * 128], in_=mn[:, c * 128:(c + 1) * 128],
                                                       identity=k.ident_bf),
                     reads=[b_mn, k.b_cbf], writes=[bpb], inc=(c == 7))
            S.op(S.act, lambda: nc.scalar.copy(out=memT[:, :, j * 128:(j + 1) * 128], in_=pb[:].rearrange("p (c t) -> p c t", c=8)),
                 reads=[bpb], writes=[b_memT])
        kng = k.ppv("kng")
        npf = [0]
        for j in range(2):
            for t4 in range(4):
                p, bp = k.pf[npf[0] % 6], k.b_pf[npf[0] % 6]
                npf[0] += 1
                for c in range(8):
                    S.op(S.pe, lambda: nc.tensor.matmul(p[:], lhsT=memT[:, c, j * 128:(j + 1) * 128], rhs=wkv[t4][:, c, :],
                                                        start=(c == 0), stop=(c == 7)),
                         reads=[b_memT, b_w[t4]], writes=[bp], inc=(c == 7))
                if t4 < 2:
                    for hh in range(2):
                        h = t4 * 2 + hh
                        rs, brs = rstd_of(p[:, hh * 256:(hh + 1) * 256], [bp], 256)
                        S.op(S.dve, lambda: nc.vector.scalar_tensor_tensor(out=kn[:], in0=p[:, hh * 256:(hh + 1) * 256], scalar=rs,
                                                                           in1=kng, op0=ALU.mult, op1=ALU.mult),
                             reads=[bp, brs, k.b_pp], writes=[b_kn])
                        pb, bpb = k.pb[hh], k.b_pb[hh]
                        for dc in range(2):
                            S.op(S.pe, lambda: nc.tensor.transpose(out=pb[:, dc * 128:(dc + 1) * 128], in_=kn[:, dc * 128:(dc + 1) * 128],
                                                                   identity=k.ident_bf),
                                 reads=[b_kn, k.b_cbf], writes=[bpb], inc=(dc == 1))
                        S.op(S.act, lambda: nc.scalar.copy(out=knT[:, h, :, j * 128:(j + 1) * 128],
                                                           in_=pb[:, 0:256].rearrange("p (c t) -> p c t", c=2)),
                             reads=[bpb], writes=[X.b_knT])
                else:
                    S.op(S.dve, lambda: nc.vector.tensor_copy(out=vmem[:, j, (t4 - 2) * 512:(t4 - 1) * 512], in_=p[:]),
                         reads=[bp], writes=[X.b_vmem])
    return X


def gen_sb(k, st, zz, b_z, oo, b_o):
    nc, S = k.nc, k.S
    NS = 2
    tiles_of = [[7, 4, 3, 0], [6, 5, 2, 1]]
    qT = [k.sb(f"sb_q{i}", [128, S_LEN], BF16, st) for i in range(2)]
    kT = [k.sb(f"sb_k{i}", [128, S_LEN], BF16, st) for i in range(2)]
    vv = [k.sb(f"sb_v{i}", [128, NTB, 128], BF16, st) for i in range(2)]
    b_q = [Buf() for _ in range(2)]
    b_k = [Buf() for _ in range(2)]
    b_v = [Buf() for _ in range(2)]
    sem_in = [S.new_sem(f"d_sbin{i}") for i in range(2)]
    yT = [k.sb(f"sb_y{i}", [128, S_LEN], BF16, st) for i in range(2)]
    b_y = [Buf() for _ in range(2)]
    sem_y = [S.new_sem(f"d_sby{i}") for i in range(2)]
    e_sb = k.sb("sb_e", [128, NS, 512], F32, st)
    b_e = [Buf() for s in range(NS)]
    nl = [k.sb(f"sb_nl{j}", [128, NS, 512], BF16, st) for j in range(2)]
    b_nl = [[Buf() for s in range(NS)] for j in range(2)]
    at = [k.sb(f"sb_at{j}", [128, NS, 512], BF16, st) for j in range(2)]
    b_at = [[Buf() for s in range(NS)] for j in range(2)]
    sacc = k.sb("sb_sacc", [128, NS, 512], BF16, st)
    b_sacc = [Buf() for s in range(NS)]

    def load_head(h):
        i = h % 2
        S.dma(S.sp, sem_in[i], qT[i][:], k.qt_sb[h], writes=[b_q[i]])
        S.dma(S.sp, sem_in[i], kT[i][:], k.kt_sb[h], writes=[b_k[i]])
        S.dma(S.sp, sem_in[i], vv[i][:], k.v_sb[:, h * 128:(h + 1) * 128].rearrange("(j p) n -> p j n", p=128),
              writes=[b_v[i]])

    def dummies(n):
        for _ in range(n):
            nc.tensor.matmul(k.dummy_bank[:], lhsT=k.ident_bf, rhs=k.sbmask_bf[:, 0:512], start=True, stop=True)

    def mm1(h, s, qi, kb):
        i = h % 2
        c0 = 128 * max(kb - 4 * qi, 0)
        S.op(S.pe, lambda: nc.tensor.matmul(zz[:, s, c0:512], lhsT=kT[i][:, kb * 128:(kb + 1) * 128],
                                            rhs=qT[i][:, qi * 512 + c0:(qi + 1) * 512], start=True, stop=True),
             reads=[b_k[i], b_q[i]], writes=[b_z[s]])

    load_head(0)
    for h in range(8):
        if h + 1 < 8:
            load_head(h + 1)
        i = h % 2
        steps = [[(qi, kb) for qi in tiles_of[s] for kb in range(4 * qi + 3, -1, -1)] for s in range(NS)]
        nsteps = len(steps[0])
        assert all(len(x) == nsteps for x in steps)
        for s in range(NS):
            mm1(h, s, *steps[s][0])
        for n in range(nsteps):
            par = n % 2
            info = []
            for s in range(NS):
                qi, kb = steps[s][n]
                info.append((qi, kb, kb - 4 * qi, kb == 4 * qi + 3, kb == 0))
            cs = [slice(128 * max(info[s][2], 0), 512) for s in range(NS)]
            wide = all(c.start == 0 for c in cs)
            if wide:
                S.op(S.act, lambda: nc.scalar.activation(out=e_sb[:], in_=zz[:], func=AF.Exp), reads=b_z, writes=b_e)
            for s in range(NS):
                qi, kb, r, first, last = info[s]
                if not wide:
                    S.op(S.act, lambda: nc.scalar.activation(out=e_sb[:, s, cs[s]], in_=zz[:, s, cs[s]], func=AF.Exp),
                         reads=[b_z[s]], writes=[b_e[s]])
                if not first:
                    S.op(S.pe, lambda: nc.tensor.matmul(zz[:, s, cs[s]], lhsT=k.nones_bf, rhs=sacc[:, s, cs[s]], start=False, stop=False,
                                                        skip_group_check=True),
                         reads=[b_sacc[s], k.b_cbf], writes=[b_z[s]], inc=False)
            for s in range(NS):
                S.op(S.act, lambda: nc.scalar.activation(out=nl[par][:, s, cs[s]], in_=e_sb[:, s, cs[s]], func=AF.Ln, bias=1.0),
                     reads=[b_e[s]], writes=[b_nl[par][s]])
            yield
            for s in range(NS):
                qi, kb, r, first, last = info[s]
                if r >= 0:
                    S.op(S.dve, lambda: nc.vector.tensor_tensor(out=nl[par][:, s, cs[s]], in0=nl[par][:, s, cs[s]],
                                                                in1=k.sbmask_bf[:, r * 512 + cs[s].start:(r + 1) * 512], op=ALU.mult),
                         reads=[b_nl[par][s], k.b_cbf], writes=[b_nl[par][s]])
            for s in range(NS):
                qi, kb, r, first, last = info[s]
                S.op(S.pe, lambda: nc.tensor.matmul(zz[:, s, cs[s]], lhsT=k.ntri_bf, rhs=nl[par][:, s, cs[s]], start=False, stop=True,
                                                    skip_group_check=True),
                     reads=[b_nl[par][s], k.b_cbf], writes=[b_z[s]])
                dummies(k.ndummy)
            yield
            for s in range(NS):
                S.op(S.act, lambda: nc.scalar.activation(out=at[par][:, s, cs[s]], in_=zz[:, s, cs[s]], func=AF.Exp),
                     reads=[b_z[s]], writes=[b_at[par][s]])
            for s in range(NS):
                qi, kb, r, first, last = info[s]
                if not last:
                    if first:
                        S.op(S.pool, lambda: nc.gpsimd.memset(sacc[:, s, 0:384], 0.0), writes=[b_sacc[s]])
                        S.op(S.pool, lambda: nc.gpsimd.tensor_copy(out=sacc[:, s, cs[s]], in_=nl[par][:, s, cs[s]]),
                             reads=[b_nl[par][s]], writes=[b_sacc[s]])
                    else:
                        S.op(S.pool, lambda: nc.gpsimd.tensor_tensor(out=sacc[:, s, cs[s]], in0=sacc[:, s, cs[s]], in1=nl[par][:, s, cs[s]],
                                                                    op=ALU.add),
                             reads=[b_nl[par][s], b_sacc[s]], writes=[b_sacc[s]])
            yield
            for s in range(NS):
                qi, kb, r, first, last = info[s]
                if n + 1 < nsteps:
                    mm1(h, s, *steps[s][n + 1])
                if r >= 0:
                    S.op(S.dve, lambda: nc.vector.tensor_tensor(out=at[par][:, s, cs[s]], in0=at[par][:, s, cs[s]],
                                                                in1=k.sbmask_bf[:, r * 512 + cs[s].start:(r + 1) * 512], op=ALU.mult),
                         reads=[b_at[par][s], k.b_cbf], writes=[b_at[par][s]])
                S.op(S.pe, lambda: nc.tensor.matmul(oo[:, s, cs[s]], lhsT=vv[i][:, kb, :], rhs=at[par][:, s, cs[s]],
                                                    start=first, stop=last, skip_group_check=True),
                     reads=[b_v[i], b_at[par][s]], writes=[b_o[s]])
                dummies(k.ndummy + 1)
                if last:
                    S.op(S.dve, lambda: nc.vector.tensor_copy(out=yT[i][:, qi * 512:(qi + 1) * 512], in_=oo[:, s, :]),
                         reads=[b_o[s]], writes=[b_y[i]])
        S.dma(S.sp, sem_y[i], k.yt_sb[h], yT[i][:], reads=[b_y[i]])


def gen_ml(k, st, T, bankA, bbA, bankB, bbB, bankN, bbN, pbT, bpbT):
    nc, S = k.nc, k.S
    qTt = [k.sb(f"ml_q{i}", [128, 2, 512], BF16, st) for i in range(2)]
    kTt = [k.sb(f"ml_k{i}", [128, 2, 512], BF16, st) for i in range(2)]
    ktm = [k.sb(f"ml_kt{i}", [128, 4, 256], BF16, st) for i in range(2)]
    vtm = [k.sb(f"ml_vt{i}", [128, 4, 256], BF16, st) for i in range(2)]
    otm = [k.sb(f"ml_ot{i}", [128, 4, 256], BF16, st) for i in range(2)]
    b_in = [Buf() for i in range(2)]
    sem_in = [S.new_sem(f"d_mlin{i}") for i in range(2)]
    yTt = [k.sb(f"ml_y{i}", [128, 2, 512], BF16, st) for i in range(2)]
    b_y = [Buf() for i in range(2)]
    sem_y = [S.new_sem(f"d_mly{i}") for i in range(2)]
    PT = [k.sb(f"ml_PT{i}", [128, 128], BF16, st) for i in range(2)]
    vu = [k.sb(f"ml_vu{i}", [128, 264], BF16, st) for i in range(2)]
    vw = [k.sb(f"ml_vw{i}", [128, 264], BF16, st) for i in range(2)]
    go = [k.sb(f"ml_go{i}", [128, 256], F32, st) for i in range(2)]
    b_PT = [Buf() for i in range(2)]
    b_vu = [Buf() for i in range(2)]
    b_vw = [Buf() for i in range(2)]
    b_go = [Buf() for i in range(2)]
    C32 = k.sb("ml_C32", [128, 2, 264], F32, st)
    Cbf = [k.sb(f"ml_Cbf{i}", [128, 2, 264], BF16, st) for i in range(2)]
    b_C32 = Buf()
    b_Cbf = [Buf() for i in range(2)]
    yb = [k.sb(f"ml_yb{i}", [128, 256], BF16, st) for i in range(2)]
    jk = k.sb("ml_jk", [128, 256], BF16, st)
    sm = [k.sb(f"ml_sm{i}", [128, 8], F32, st) for i in range(2)]
    b_yb = [Buf() for i in range(2)]
    b_jk = Buf()
    b_sm = [[Buf() for _ in range(8)] for i in range(2)]
    mlng = k.ppv("mlng")

    def load_group(h, g, gi):
        sm_, bb = sem_in[gi], [b_in[gi]]
        cols = slice(g * 512, (g + 1) * 512)
        S.dma(S.sp, sm_, qTt[gi][:], k.qt_ml[2 * h:2 * h + 2, :, cols].rearrange("c p t -> p c t"), writes=bb)
        S.dma(S.sp, sm_, kTt[gi][:], k.kt_ml[2 * h:2 * h + 2, :, cols].rearrange("c p t -> p c t"), writes=bb)
        for dst, src in ((ktm, k.k_ml), (vtm, k.v_ml), (otm, k.o_ml)):
            S.dma(S.sp, sm_, dst[gi][:], src[cols, h * 256:(h + 1) * 256].rearrange("(j p) n -> p j n", p=128), writes=bb)

    ng = 0
    load_group(0, 0, 0)
    for h in range(4):
        for g in range(8):
            gi = ng % 2
            ng += 1
            if g + 1 < 8:
                load_group(h, g + 1, ng % 2)
            elif h + 1 < 4:
                load_group(h + 1, 0, ng % 2)
            for j in range(4):
                c = g * 4 + j
                par = c % 2
                col = c * 4 + h
                bin_ = b_in[gi]
                tsl = slice(j * 128, (j + 1) * 128)
                uc = T.uT[:, col:col + 1]
                wc = T.wT[:, col:col + 1]
                flc = T.flT[:, col:col + 1]
                dcc = T.decT[:, col:col + 1]
                smt = sm[par]
                bsm = b_sm[par]
                ps_s = bankA[:, 264:392]
                ps_c = [bankA[:, 0:257], bankB[:, 0:257]]
                bbc = [bbA, bbB]
                ps_n = bankN[:, 0:257]
                for dc in range(2):
                    S.op(S.pe, lambda: nc.tensor.matmul(ps_s, lhsT=kTt[gi][:, dc, tsl], rhs=qTt[gi][:, dc, tsl],
                                                        start=(dc == 0), stop=(dc == 1)),
                         reads=[bin_], writes=[bbA], inc=(dc == 1))
                for (dst, bd, sc, bs) in ((vu[par], b_vu[par], uc, T.b_u), (vw[par], b_vw[par], wc, T.b_w)):
                    S.op(S.dve, lambda: nc.vector.tensor_scalar(out=dst[:, 0:256], in0=vtm[gi][:, j, :], scalar1=sc, scalar2=None,
                                                                op0=ALU.mult),
                         reads=[bin_, bs], writes=[bd])
                    S.op(S.dve, lambda: nc.vector.tensor_copy(out=dst[:, 256:257], in_=sc), reads=[bs], writes=[bd])
                S.op(S.pool, lambda: nc.gpsimd.tensor_tensor(out=go[par][:], in0=otm[gi][:, j, :], in1=mlng[:, h * 256:(h + 1) * 256],
                                                            op=ALU.mult),
                     reads=[bin_, k.b_pp], writes=[b_go[par]])
                yield
                S.op(S.dve, lambda: nc.vector.tensor_tensor(out=PT[par][:], in0=ps_s, in1=k.mlmask_bf, op=ALU.mult),
                     reads=[bbA, k.b_cbf], writes=[b_PT[par]])
                for dc in range(2):
                    S.op(S.pe, lambda: nc.tensor.matmul(ps_c[dc], lhsT=ktm[gi][:, j, dc * 128:(dc + 1) * 128],
                                                        rhs=vw[par][:, 0:257], start=True, stop=True),
                         reads=[bin_, b_vw[par]], writes=[bbc[dc]])
                yield
                S.op(S.pe, lambda: nc.tensor.matmul(ps_n, lhsT=PT[par][:], rhs=vu[par][:, 0:257], start=True, stop=(c == 0)),
                     reads=[b_PT[par], b_vu[par]], writes=[bbN], inc=(c == 0))
                if c > 0:
                    cb, bcb = Cbf[1 - par], b_Cbf[1 - par]
                    for dc in range(2):
                        S.op(S.pe, lambda: nc.tensor.matmul(ps_n, lhsT=qTt[gi][:, dc, tsl], rhs=cb[:, dc, 0:257],
                                                            start=False, stop=(dc == 1)),
                             reads=[bin_, bcb], writes=[bbN], inc=(dc == 1))
                for dc in range(2):
                    if c == 0:
                        S.op(S.dve, lambda: nc.vector.tensor_copy(out=C32[:, dc, 0:257], in_=ps_c[dc]),
                             reads=[bbc[dc]], writes=[b_C32])
                    else:
                        S.op(S.dve, lambda: nc.vector.scalar_tensor_tensor(out=C32[:, dc, 0:257], in0=C32[:, dc, 0:257],
                                                                           scalar=dcc, in1=ps_c[dc], op0=ALU.mult, op1=ALU.add),
                             reads=[bbc[dc], b_C32, T.b_dec], writes=[b_C32])
                yield
                S.op(S.pool, lambda: nc.gpsimd.tensor_copy(out=Cbf[par][:, :, 0:257], in_=C32[:, :, 0:257]),
                     reads=[b_C32], writes=[b_Cbf[par]])
                S.op(S.act, lambda: nc.scalar.activation(out=smt[:, 0:1], in_=bankN[:, 256:257], func=AF.Abs),
                     reads=[bbN], writes=[bsm[0]])
                S.op(S.act, lambda: nc.scalar.activation(out=jk[:], in_=bankN[:, 0:256], func=AF.Square, accum_out=smt[:, 1:2]),
                     reads=[bbN], writes=[b_jk, bsm[1]])
                yield
                S.op(S.dve, lambda: nc.vector.tensor_tensor(out=smt[:, 2:3], in0=smt[:, 0:1], in1=flc, op=ALU.max),
                     reads=[bsm[0], T.b_fl], writes=[bsm[2]])
                S.op(S.dve, lambda: nc.vector.tensor_scalar(out=smt[:, 3:4], in0=smt[:, 2:3], scalar1=smt[:, 2:3], scalar2=None,
                                                            op0=ALU.mult),
                     reads=[bsm[2]], writes=[bsm[3]])
                yield
                S.op(S.dve, lambda: nc.vector.tensor_scalar(out=smt[:, 3:4], in0=smt[:, 3:4], scalar1=EPS, scalar2=None, op0=ALU.mult),
                     reads=[bsm[3]], writes=[bsm[3]])
                S.op(S.dve, lambda: nc.vector.scalar_tensor_tensor(out=smt[:, 4:5], in0=smt[:, 1:2], scalar=1.0 / 256,
                                                                   in1=smt[:, 3:4], op0=ALU.mult, op1=ALU.add),
                     reads=[bsm[1], bsm[3]], writes=[bsm[4]])
                yield
                S.op(S.act, lambda: nc.scalar.activation(out=smt[:, 5:6], in_=smt[:, 4:5], func=AF.Ln), reads=[bsm[4]], writes=[bsm[5]])
                S.op(S.act, lambda: nc.scalar.activation(out=smt[:, 6:7], in_=smt[:, 5:6], func=AF.Exp, scale=-0.5),
                     reads=[bsm[5]], writes=[bsm[6]])
                yield
                S.op(S.dve, lambda: nc.vector.scalar_tensor_tensor(out=yb[par][:], in0=bankN[:, 0:256], scalar=smt[:, 6:7],
                                                                   in1=go[par][:], op0=ALU.mult, op1=ALU.mult),
                     reads=[bbN, bsm[6], b_go[par]], writes=[b_yb[par]])
                yield
                for dc in range(2):
                    S.op(S.pe, lambda: nc.tensor.transpose(out=pbT[:, dc * 128:(dc + 1) * 128], in_=yb[par][:, dc * 128:(dc + 1) * 128],
                                                           identity=k.ident_bf),
                         reads=[b_yb[par], k.b_cbf], writes=[bpbT], inc=(dc == 1))
                yield
                S.op(S.dve, lambda: nc.vector.tensor_copy(out=yTt[gi][:, :, tsl], in_=pbT[:, 0:256].rearrange("p (c t) -> p c t", c=2)),
                     reads=[bpbT], writes=[b_y[gi]])
                yield
            S.dma(S.sp, sem_y[gi], k.yt_ml[2 * h:2 * h + 2, :, g * 512:(g + 1) * 512].rearrange("c p t -> p c t"),
                  yTt[gi][:], reads=[b_y[gi]])


def gen_xa(k, st, X, banks, bbanks):
    nc, S = k.nc, k.S
    qx = [k.sb(f"xa_qx{i}", [128, 2, 512], BF16, st) for i in range(2)]
    b_qx = [Buf(), Buf()]
    sem_q = [S.new_sem("d_xaq0"), S.new_sem("d_xaq1")]
    yx = [k.sb(f"xa_yx{i}", [128, 2, 512], BF16, st) for i in range(2)]
    b_yx = [Buf(), Buf()]
    sem_y = [S.new_sem("d_xay0"), S.new_sem("d_xay1")]
    sq = k.sb("xa_sq", [128, 2, 512], BF16, st)
    rq = k.sb("xa_rq", [128, 512], F32, st)
    rq2 = k.sb("xa_rq2", [128, 512], F32, st)
    qn = k.sb("xa_qn", [128, 2, 512], BF16, st)
    pT = k.sb("xa_pT", [128, 2, 512], BF16, st)
    rden = k.sb("xa_rden", [128, 512], F32, st)
    b_sq, b_rq, b_rq2, b_qn, b_pT, b_rden = Buf(), Buf(), Buf(), Buf(), Buf(), Buf()
    qng = k.ppv("qng")
    knT, vmem = X.knT, X.vmem
    (pa, pb_, pc), (ba, bb_, bc) = banks, bbanks
    its = [(h, tt) for h in range(4) for tt in range(NTT)]

    def load(n):
        h, tt = its[n]
        S.dma(S.sp, sem_q[n % 2], qx[n % 2][:], k.qt_x[2 * h:2 * h + 2, :, tt * 512:(tt + 1) * 512].rearrange("c p t -> p c t"),
              writes=[b_qx[n % 2]])
    load(0)
    for n, (h, tt) in enumerate(its):
        i = n % 2
        cols = slice(tt * 512, (tt + 1) * 512)
        if n + 1 < len(its):
            load(n + 1)
        S.op(S.dve, lambda: nc.vector.tensor_tensor(out=sq[:], in0=qx[i][:], in1=qx[i][:], op=ALU.mult), reads=[b_qx[i]], writes=[b_sq])
        yield
        for dc in range(2):
            S.op(S.pe, lambda: nc.tensor.matmul(pa[:], lhsT=k.ones_bf, rhs=sq[:, dc, :], start=(dc == 0), stop=(dc == 1)),
                 reads=[b_sq, k.b_cbf], writes=[ba], inc=(dc == 1))
        yield
        S.op(S.act, lambda: nc.scalar.activation(out=rq[:], in_=pa[:], func=AF.Ln, bias=k.eps_t[:, 0:1], scale=1.0 / 256),
             reads=[ba, k.b_eps], writes=[b_rq])
        yield
        S.op(S.act, lambda: nc.scalar.activation(out=rq2[:], in_=rq[:], func=AF.Exp, scale=-0.5), reads=[b_rq], writes=[b_rq2])
        yield
        for dc in range(2):
            S.op(S.dve, lambda: nc.vector.scalar_tensor_tensor(out=qn[:, dc, :], in0=qx[i][:, dc, :], scalar=qng[:, dc:dc + 1],
                                                               in1=rq2[:], op0=ALU.mult, op1=ALU.mult),
                 reads=[b_qx[i], b_rq2, k.b_pp], writes=[b_qn])
        yield
        for mc, (p, bp) in enumerate(((pa, ba), (pb_, bb_))):
            for dc in range(2):
                S.op(S.pe, lambda: nc.tensor.matmul(p[:], lhsT=knT[:, h, dc, mc * 128:(mc + 1) * 128], rhs=qn[:, dc, :],
                                                    start=(dc == 0), stop=(dc == 1)),
                     reads=[X.b_knT, b_qn], writes=[bp], inc=(dc == 1))
        yield
        for mc, (p, bp) in enumerate(((pa, ba), (pb_, bb_))):
            S.op(S.act, lambda: nc.scalar.activation(out=pT[:, mc, :], in_=p[:], func=AF.Exp, scale=1.0 / 16),
                 reads=[bp], writes=[b_pT])
        yield
        for mc in range(2):
            S.op(S.pe, lambda: nc.tensor.matmul(pc[:], lhsT=k.ones_bf, rhs=pT[:, mc, :], start=(mc == 0), stop=(mc == 1)),
                 reads=[b_pT, k.b_cbf], writes=[bc], inc=(mc == 1))
        for dvc, (p, bp) in enumerate(((pa, ba), (pb_, bb_))):
            for mc in range(2):
                S.op(S.pe, lambda: nc.tensor.matmul(p[:], lhsT=vmem[:, mc, h * 256 + dvc * 128:h * 256 + (dvc + 1) * 128],
                                                    rhs=pT[:, mc, :], start=(mc == 0), stop=(mc == 1)),
                     reads=[X.b_vmem, b_pT], writes=[bp], inc=(mc == 1))
        yield
        S.op(S.dve, lambda: nc.vector.reciprocal(out=rden[:], in_=pc[:]), reads=[bc], writes=[b_rden])
        yield
        for dvc, (p, bp) in enumerate(((pa, ba), (pb_, bb_))):
            S.op(S.dve, lambda: nc.vector.tensor_tensor(out=yx[i][:, dvc, :], in0=p[:], in1=rden[:], op=ALU.mult),
                 reads=[bp, b_rden], writes=[b_yx[i]])
        S.dma(S.sp, sem_y[i], k.yt_x[2 * h:2 * h + 2, :, cols].rearrange("c p t -> p c t"), yx[i][:], reads=[b_yx[i]])
        yield


def stage_mid(k, bg_per_yield=1):
    nc, S = k.nc, k.S
    with ExitStack() as st:
        T = ml_prep(k, st)
        X = xa_prep(k, st)
        with ExitStack() as stp:
            xa_prep(k, st, X, stp, phase=1)
            ml_prep(k, st, T, stp, phase=1)
            xa_prep(k, st, X, stp, phase=2)
            S.barrier()
        fg = gen_sb(k, st, k.zz, k.b_zz, k.oo, k.b_oo)
        bankN = k.pb[1][:].bitcast(F32)
        pb0f = k.pb[0][:].bitcast(F32)
        bankB = pb0f[:, 128:512]
        bgs = [gen_ml(k, st, T, k.pf[0], k.b_pf[0], bankB, k.b_pb[0], bankN, k.b_pb[1], k.pb[0], k.b_pb[0]),
               gen_xa(k, st, X, (k.pf[0], pb0f, bankN), (k.b_pf[0], k.b_pb[0], k.b_pb[1]))]
        k.dummy_bank = k.pf[1]
        bi = 0
        for _ in fg:
            for _r in range(bg_per_yield):
                while bi < len(bgs):
                    try:
                        next(bgs[bi])
                        break
                    except StopIteration:
                        bi += 1
        while bi < len(bgs):
            for _ in bgs[bi]:
                pass
            bi += 1
        S.barrier()


ALL_STAGES = ("s1", "s2", "mid", "4a", "4b")
_NC_CACHE = {}


def kernel(**inputs):
    inp = {k_: np.asarray(v) for k_, v in inputs.items()}
    if "nc" not in _NC_CACHE:
        _NC_CACHE["nc"] = build(dbg=(), stages=ALL_STAGES)
    nc = _NC_CACHE["nc"]
    pp = make_pp(inp)
    shared = {"pp": pp, "w_in": np.ascontiguousarray(inp["w_in"][0]), "w_mem_kv": np.ascontiguousarray(inp["w_mem_kv"][0]),
              "w_sb_proj": np.ascontiguousarray(inp["w_sb_proj"][0]), "w_ml_proj": np.ascontiguousarray(inp["w_ml_proj"][0]),
              "w_x_proj": np.ascontiguousarray(inp["w_x_proj"][0]), "w_out": np.ascontiguousarray(inp["w_out"][0]),
              "w_ff1": np.ascontiguousarray(inp["w_ff1"][0]), "w_ff2": np.ascontiguousarray(inp["w_ff2"][0])}
    in_maps = []
    for b in range(8):
        m = dict(shared)
        m["x"] = np.ascontiguousarray(inp["x"][b])
        m["mem"] = np.ascontiguousarray(inp["mem"][b])
        in_maps.append(m)
    res = run_bass_kernel_spmd(nc, in_maps, core_ids=list(range(8)))
    return np.stack([np.asarray(r["out"]) for r in res.results], axis=0).astype(np.float32)
```

```python
from contextlib import ExitStack
import numpy as np
import concourse.bass as bass
import concourse.mybir as mybir
from concourse.bass_utils import run_bass_kernel_spmd

F32 = mybir.dt.float32
BF16 = mybir.dt.bfloat16
AF = mybir.ActivationFunctionType
ALU = mybir.AluOpType
AX = mybir.AxisListType

S_LEN = 4096
D = 1024
NTB = 32
NTT = 8
N_IN = 11272
N_MEM = 256
EPS = 1e-6
LN16 = float(np.log(16.0))
NDUMMY = 1


class SemObj:
    def __init__(self, h, name):
        self.h = h
        self.name = name
        self.count = 0
        self.is_dma = name.startswith("d_")


class Buf:
    __slots__ = ("name", "w", "r", "psum")

    def __init__(self, name="", psum=False):
        self.name = name
        self.w = None
        self.r = []
        self.psum = psum


class Eng:
    def __init__(self, name, e, sem, inorder_self):
        self.name = name
        self.e = e
        self.sem = sem
        self.seen = {}
        self.inorder_self = inorder_self
        self.dangling = False


class Sched:
    def __init__(self, nc, stack, self_sync=True):
        self.nc = nc
        self.stack = stack
        self.sems = []
        mk = self.new_sem
        self.pe = Eng("pe", nc.tensor, mk("s_pe"), True)
        self.act = Eng("act", nc.scalar, mk("s_act"), not self_sync)
        self.dve = Eng("dve", nc.vector, mk("s_dve"), not self_sync)
        self.pool = Eng("pool", nc.gpsimd, mk("s_pool"), not self_sync)
        self.sp = Eng("sp", nc.sync, mk("s_sp"), True)
        self.engs = [self.pe, self.act, self.dve, self.pool, self.sp]
        self.out_events = []
        self.nops = 0
        self.trace = {e.name: [] for e in self.engs}

    def new_sem(self, name):
        h = self.stack.enter_context(self.nc.semaphore(name))
        s = SemObj(h, name)
        self.sems.append(s)
        return s

    def _waits(self, eng, reads, writes):
        need = {}

        def add(ev):
            if ev is None:
                return
            s, v = ev
            if need.get(s, 0) < v:
                need[s] = v
        for b in reads:
            add(b.w)
            if b.psum:
                for ev in b.r:
                    if ev[0] is not eng.sem:
                        add(ev)
        for b in writes:
            add(b.w)
            for ev in b.r:
                add(ev)
        for s, v in need.items():
            if s.is_dma:
                v = s.count
            if s is eng.sem and eng.inorder_self:
                continue
            if eng.seen.get(s, 0) >= v:
                continue
            eng.e.wait_ge(s.h, v)
            eng.seen[s] = v
            self.trace[eng.name].append(("w", s.name, v))

    def op(self, eng, emit, reads=(), writes=(), inc=True):
        self._waits(eng, reads, writes)
        ins = emit()
        self.nops += 1
        if inc:
            eng.sem.count += 1
            ins.then_inc(eng.sem.h, 1)
            ev = (eng.sem, eng.sem.count)
            eng.dangling = False
            self.trace[eng.name].append(("i", eng.sem.name, 1))
        else:
            ev = (eng.sem, eng.sem.count + 1)
            eng.dangling = True
        for b in writes:
            b.w = ev
            b.r = []
        for b in reads:
            if b.w is not ev and (not b.r or b.r[-1] != ev):
                b.r.append(ev)
        return ins

    def dma(self, q, sem, out, in_, reads=(), writes=(), is_output=False, **kw):
        self._waits(q, reads, writes)
        ins = q.e.dma_start(out=out, in_=in_, **kw)
        ins.then_inc(sem.h, 16)
        sem.count += 16
        self.trace[q.name].append(("i", sem.name, 16))
        ev = (sem, sem.count)
        for b in writes:
            b.w = ev
            b.r = []
        for b in reads:
            b.r.append(ev)
        if is_output:
            self.out_events.append(ev)
        self.nops += 1
        return ins

    def barrier(self):
        for e in self.engs:
            assert not e.dangling
        for e in self.engs:
            for s in self.sems:
                if s.count == 0:
                    continue
                if s is e.sem:
                    continue
                if e.seen.get(s, 0) >= s.count:
                    continue
                e.e.wait_ge(s.h, s.count)
                e.seen[s] = s.count
                self.trace[e.name].append(("w", s.name, s.count))

    def check_deadlock(self):
        vals = {}
        pos = {n: 0 for n in self.trace}
        progress = True
        while progress:
            progress = False
            for n, tr in self.trace.items():
                while pos[n] < len(tr):
                    kind, sn, v = tr[pos[n]]
                    if kind == "w":
                        if vals.get(sn, 0) < v:
                            break
                    else:
                        vals[sn] = vals.get(sn, 0) + v
                    pos[n] += 1
                    progress = True
        stuck = {n: (pos[n], self.trace[n][pos[n]]) for n in self.trace if pos[n] < len(self.trace[n])}
        return stuck

    def finish(self):
        need = {}
        for s, v in self.out_events:
            need[s] = max(need.get(s, 0), v)
        for s, v in need.items():
            self.sp.e.wait_ge(s.h, v)


PP = {}
_off = 0


def _pp(name, n):
    global _off
    PP[name] = (_off, n)
    _off += n


_pp("ident", 128)
_pp("ntri", 128)
_pp("nones", 128)
_pp("ones", 128)
_pp("mlmask", 128)
_pp("sbmask", 4 * 512)
_pp("sel127", 128)
_pp("gmix", 1024)
_pp("bgate", 24)
_pp("convw", 64)
_pp("convb", 16)
_pp("mlng", 1024)
_pp("gmem", 1024)
_pp("qng", 2)
_pp("kng", 256)
_pp("gmlp", 1024)
_pp("bif", 2)
_pp("sel4", 12)
NPP = _off


def make_pp(inp):
    pp = np.zeros((128, NPP), np.float32)

    def put(name, arr):
        o, n = PP[name]
        pp[:, o:o + n] = np.asarray(arr, np.float32).reshape(128, n)
    p = np.arange(128)
    put("ident", np.eye(128))
    put("ntri", -(p[:, None] >= p[None, :]).astype(np.float32))
    put("nones", -np.ones((128, 128)))
    put("ones", np.ones((128, 128)))
    c = np.arange(512)
    put("sbmask", np.stack([((128 * r + p[:, None]) < c[None, :]) for r in range(4)], 1).astype(np.float32))
    put("mlmask", (p[:, None] <= p[None, :]).astype(np.float32))
    put("sel127", np.repeat((p == 127).astype(np.float32)[:, None], 128, 1))
    put("gmix", np.broadcast_to(inp["g_mix"][0][None, :], (128, 1024)))
    put("bgate", inp["b_gate"][0].reshape(24, 128).T)
    put("convw", inp["conv_w"][0].reshape(4, 16, 128).transpose(2, 1, 0))
    put("convb", inp["conv_b"][0].reshape(16, 128).T)
    put("mlng", np.broadcast_to(inp["ml_norm_g"][0][None, :], (128, 1024)))
    put("gmem", np.broadcast_to(inp["g_mem"][0][None, :], (128, 1024)))
    put("qng", inp["q_norm_g"][0].reshape(2, 128).T)
    put("kng", np.broadcast_to(inp["k_norm_g"][0][None, :], (128, 256)))
    put("gmlp", np.broadcast_to(inp["g_mlp"][0][None, :], (128, 1024)))
    bif = np.zeros((128, 2), np.float32)
    bif[:4, 0] = inp["b_if"][0, :4]
    bif[:4, 1] = inp["b_if"][0, 4:]
    put("bif", bif)
    sel = np.zeros((128, 12), np.float32)
    for q in range(3):
        for h in range(4):
            sel[32 * q + h, 4 * q + h] = 1.0
    put("sel4", sel)
    return pp


C_SBQ, C_SBK, C_SBV = 0, 1024, 2048
C_MLQ, C_MLK, C_MLV, C_MLO = 3072, 4096, 5120, 6144
C_MLI, C_MLF, C_XQ, C_GATE = 7168, 7172, 7176, 8200


class K:
    pass


def build(dbg=(), stages=("s1", "s2")):
    nc = bass.Bass("TRN2", target_bir_lowering=False)
    k = K()
    k.nc = nc
    k.dbg = dbg
    k.ndummy = NDUMMY

    def din(name, shape, dt=F32):
        return nc.dram_tensor(name, shape, dt, kind="ExternalInput").ap()

    def dscr(name, shape, dt):
        kind = "ExternalOutput" if name in dbg else "Internal"
        return nc.dram_tensor(name, shape, dt, kind=kind).ap()

    k.x = din("x", [S_LEN, D])
    k.mem = din("mem", [N_MEM, D])
    k.pp = din("pp", [128, NPP])
    k.w_in = din("w_in", [D, N_IN])
    k.w_mem_kv = din("w_mem_kv", [D, 2048])
    k.w_sb = din("w_sb_proj", [D, D])
    k.w_ml = din("w_ml_proj", [D, D])
    k.w_x = din("w_x_proj", [D, D])
    k.w_out = din("w_out", [D, D])
    k.w_ff1 = din("w_ff1", [D, 4096])
    k.w_ff2 = din("w_ff2", [4096, D])
    k.out = nc.dram_tensor("out", [S_LEN, D], F32, kind="ExternalOutput").ap()

    k.qt_sb = dscr("qt_sb", [8, 128, S_LEN], BF16)
    k.kt_sb = dscr("kt_sb", [8, 128, S_LEN], BF16)
    k.v_sb = dscr("v_sb", [S_LEN, D], BF16)
    k.qt_ml = dscr("qt_ml", [8, 128, S_LEN], BF16)
    k.kt_ml = dscr("kt_ml", [8, 128, S_LEN], BF16)
    k.k_ml = dscr("k_ml", [S_LEN, D], BF16)
    k.v_ml = dscr("v_ml", [S_LEN, D], BF16)
    k.o_ml = dscr("o_ml", [S_LEN, D], BF16)
    k.qt_x = dscr("qt_x", [8, 128, S_LEN], BF16)
    k.gif = dscr("gif", [8, S_LEN], F32)
    k.g_scr = dscr("g_scr", [24, 128, S_LEN], BF16)
    k.yt_sb = dscr("yt_sb", [8, 128, S_LEN], BF16)
    k.yt_ml = dscr("yt_ml", [8, 128, S_LEN], BF16)
    k.yt_x = dscr("yt_x", [8, 128, S_LEN], BF16)

    k.x1_scr = dscr("x1_scr", [S_LEN, D], F32)
    k.h2t_scr = dscr("h2t_scr", [8, 128, S_LEN], BF16)

    with ExitStack() as st:
        S = Sched(nc, st)
        k.S = S
        k.st = st

        def sb(name, shape, dt, stack):
            return stack.enter_context(nc.sbuf_tensor(name, shape, dt))

        def ps(name, shape, dt, stack=st):
            return stack.enter_context(nc.psum_tensor(name, shape, dt))
        k.sb = sb
        k.ps = ps

        k.pf = [ps(f"pf{i}", [128, 512], F32) for i in range(2)]
        k.zz = ps("pzz", [128, 2, 512], F32)
        k.oo = ps("poo", [128, 2, 512], F32)
        k.pf += [k.zz[:, 0, :], k.zz[:, 1, :], k.oo[:, 0, :], k.oo[:, 1, :]]
        k.b_pf = [Buf(f"pf{i}", psum=True) for i in range(6)]
        k.b_zz = [k.b_pf[2], k.b_pf[3]]
        k.b_oo = [k.b_pf[4], k.b_pf[5]]
        k.pb = [ps(f"pb{i}", [128, 1024], BF16) for i in range(2)]
        k.b_pb = [Buf(f"pb{i}", psum=True) for i in range(2)]

        with ExitStack() as stA:
            k.ppt = sb("ppt", [128, NPP], F32, stA)
            k.b_pp = Buf("pp")
            k.sem_c = S.new_sem("d_const")
            S.dma(S.sp, k.sem_c, k.ppt[:], k.pp[:, :], writes=[k.b_pp])

            def ppv(name):
                o, n = PP[name]
                return k.ppt[:, o:o + n]
            k.ppv = ppv
            NCB = 5 * 128 + 2048
            k.cbf = sb("cbf", [128, NCB], BF16, stA)
            k.b_cbf = Buf("cbf")
            o_id = PP["ident"][0]
            S.op(S.dve, lambda: nc.vector.tensor_copy(out=k.cbf[:, :], in_=k.ppt[:, o_id:o_id + NCB]),
                 reads=[k.b_pp], writes=[k.b_cbf])
            k.ident_bf = k.cbf[:, 0:128]
            k.ntri_bf = k.cbf[:, 128:256]
            k.nones_bf = k.cbf[:, 256:384]
            k.ones_bf = k.cbf[:, 384:512]
            k.mlmask_bf = k.cbf[:, 512:640]
            k.sbmask_bf = k.cbf[:, 640:640 + 2048]
            k.eps_t = sb("eps_t", [128, 1], F32, stA)
            k.b_eps = Buf("eps")
            S.op(S.dve, lambda: nc.vector.memset(k.eps_t[:], EPS), writes=[k.b_eps])

            with ExitStack() as st12:
                k.hT = sb("hT", [128, 8, S_LEN], BF16, st12)
                k.b_hT = [Buf(f"hT{i}") for i in range(NTB)]
                with ExitStack() as st1:
                    if "s1" in stages:
                        stage1(k, st1)
                    if "s2" in stages:
                        stage2(k)
                if "hT" in dbg:
                    hT_d = nc.dram_tensor("hT_dbg", [128, 8, S_LEN], BF16, kind="ExternalOutput").ap()
                    sd = S.new_sem("d_dbg")
                    S.dma(S.sp, sd, hT_d, k.hT[:], reads=k.b_hT, is_output=True)
                S.barrier()
            if "mid" in stages:
                stage_mid(k)
            if "sb" in stages:
                stage_sb(k)
            if "ml" in stages:
                stage_ml(k)
            if "xa" in stages:
                stage_xa(k)
            if "4a" in stages:
                stage_4a(k)
            S.barrier()
        if "4b" in stages:
            stage_4b(k)
        S.out_events.extend((s_, s_.count) for s_ in S.sems if s_.name.startswith("d_") and s_.count)
        S.finish()
        stuck = S.check_deadlock()
        assert not stuck, f"deadlock: {stuck}"
        k.S = S
    nc._sched = None
    return nc


def stage1(k, st):
    nc, S = k.nc, k.S
    if True:
        xb = [k.sb(f"xb{i}", [128, D], F32, st) for i in range(3)]
        b_xb = [Buf() for _ in range(3)]
        sem_x = [S.new_sem(f"d_x{i}") for i in range(3)]
        hb = [k.sb(f"hb{i}", [128, D], BF16, st) for i in range(2)]
        b_hb = [Buf() for _ in range(2)]
        junk = k.sb("junk1", [128, D], BF16, st)
        b_junk = Buf()
        stat = k.sb("stat1", [128, 3 * NTB], F32, st)
        gm = k.ppv("gmix")
        for i in range(NTB):
            xi, bx = xb[i % 3], b_xb[i % 3]
            S.dma(S.sp, sem_x[i % 3], xi[:], k.x[i * 128:(i + 1) * 128, :], writes=[bx])
            b_ss, b_rms, b_rs = Buf(), Buf(), Buf()
            ss = stat[:, 3 * i:3 * i + 1]
            rms = stat[:, 3 * i + 1:3 * i + 2]
            rs = stat[:, 3 * i + 2:3 * i + 3]
            S.op(S.act, lambda: nc.scalar.activation(out=junk[:], in_=xi[:], func=AF.Square, accum_out=ss),
                 reads=[bx], writes=[b_junk, b_ss])
            S.op(S.act, lambda: nc.scalar.activation(out=rms, in_=ss, func=AF.Sqrt, bias=k.eps_t[:, 0:1], scale=1.0 / D),
                 reads=[b_ss, k.b_eps], writes=[b_rms])
            S.op(S.dve, lambda: nc.vector.reciprocal(out=rs, in_=rms), reads=[b_rms], writes=[b_rs])
            hi, bh = hb[i % 2], b_hb[i % 2]
            S.op(S.dve, lambda: nc.vector.scalar_tensor_tensor(out=hi[:], in0=xi[:], scalar=rs, in1=gm,
                                                               op0=ALU.mult, op1=ALU.mult),
                 reads=[bx, b_rs, k.b_pp], writes=[bh])
            pb, bpb = k.pb[i % 2], k.b_pb[i % 2]
            for c in range(8):
                S.op(S.pe, lambda: nc.tensor.transpose(out=pb[:, c * 128:(c + 1) * 128], in_=hi[:, c * 128:(c + 1) * 128],
                                                       identity=k.ident_bf),
                     reads=[bh, k.b_cbf], writes=[bpb], inc=(c == 7))
            ev_eng = S.act if i % 2 == 0 else S.dve
            if ev_eng is S.act:
                S.op(S.act, lambda: nc.scalar.copy(out=k.hT[:, :, i * 128:(i + 1) * 128],
                                                   in_=pb[:].rearrange("p (c t) -> p c t", c=8)),
                     reads=[bpb], writes=[k.b_hT[i]])
            else:
                S.op(S.dve, lambda: nc.vector.tensor_copy(out=k.hT[:, :, i * 128:(i + 1) * 128],
                                                          in_=pb[:].rearrange("p (c t) -> p c t", c=8)),
                     reads=[bpb], writes=[k.b_hT[i]])


def stage2(k):
    nc, S = k.nc, k.S
    with ExitStack() as st:
        NW = 3
        wt = [k.sb(f"wt{i}", [128, 8, 512], BF16, st) for i in range(NW)]
        b_wt = [Buf() for _ in range(NW)]
        sem_w = [S.new_sem(f"d_w{i}") for i in range(NW)]
        ob = [k.sb(f"ob{i}", [128, S_LEN], BF16, st) for i in range(2)]
        b_ob = [Buf() for _ in range(2)]
        sem_ob = [S.new_sem(f"d_ob{i}") for i in range(2)]
        ot = [k.sb(f"ot{i}", [128, 4, 512], BF16, st) for i in range(2)]
        b_ot = [Buf() for _ in range(2)]
        sem_ot = [S.new_sem(f"d_ot{i}") for i in range(2)]
        zc = k.sb("zc", [128, 8 + S_LEN], BF16, st)
        b_zc = Buf()
        dg = [k.sb(f"dg{i}", [128, 4, 128], BF16, st) for i in range(2)]
        b_dg = [Buf() for _ in range(2)]
        cnt = {"w": 0, "ob": 0, "ot": 0, "pf": 0, "dg": 0, "ev": 0}
        S.op(S.dve, lambda: nc.vector.memset(zc[:, 0:8], 0.0), writes=[b_zc])

        def load_w(col0, ncols=512):
            i = cnt["w"] % NW
            cnt["w"] += 1
            S.dma(S.pool, sem_w[i], wt[i][:, :, 0:ncols],
                  k.w_in[:, col0:col0 + ncols].rearrange("(c p) n -> p c n", p=128), writes=[b_wt[i]])
            return wt[i], b_wt[i]

        def next_pf():
            i = cnt["pf"] % 4
            cnt["pf"] += 1
            return k.pf[i], k.b_pf[i]

        def evac(out, in_, reads, writes, func=None, scale=1.0, bias=None):
            if func is None and scale == 1.0:
                cnt["ev"] += 1
                if cnt["ev"] % 2 == 0:
                    return S.op(S.dve, lambda: nc.vector.tensor_copy(out=out, in_=in_), reads=reads, writes=writes)
                return S.op(S.act, lambda: nc.scalar.copy(out=out, in_=in_), reads=reads, writes=writes)
            f = func if func is not None else AF.Copy
            if bias is not None:
                return S.op(S.act, lambda: nc.scalar.activation(out=out, in_=in_, func=f, bias=bias, scale=scale),
                            reads=reads, writes=writes)
            return S.op(S.act, lambda: nc.scalar.activation(out=out, in_=in_, func=f, scale=scale),
                        reads=reads, writes=writes)

        def fm_group(w, bw, g, dst_row, scale=1.0):
            i = cnt["ob"] % 2
            cnt["ob"] += 1
            o, bo = ob[i], b_ob[i]
            for tt in range(NTT):
                p, bp = next_pf()
                for c in range(8):
                    S.op(S.pe, lambda: nc.tensor.matmul(p[:], lhsT=w[:, c, g * 128:(g + 1) * 128],
                                                        rhs=k.hT[:, c, tt * 512:(tt + 1) * 512],
                                                        start=(c == 0), stop=(c == 7)),
                         reads=[bw] + k.b_hT[tt * 4:tt * 4 + 4], writes=[bp], inc=(c == 7))
                evac(o[:, tt * 512:(tt + 1) * 512], p[:], [bp], [bo], scale=scale)
            S.dma(S.sp, sem_ob[i], dst_row, o[:], reads=[bo])

        def tm_tile(w, bw, dst, col0, func=None):
            for tq in range(8):
                i = cnt["ot"] % 2
                cnt["ot"] += 1
                o, bo = ot[i], b_ot[i]
                for j in range(4):
                    tb = tq * 4 + j
                    p, bp = next_pf()
                    for c in range(8):
                        S.op(S.pe, lambda: nc.tensor.matmul(p[:], lhsT=k.hT[:, c, tb * 128:(tb + 1) * 128],
                                                            rhs=w[:, c, :], start=(c == 0), stop=(c == 7)),
                             reads=[bw, k.b_hT[tb]], writes=[bp], inc=(c == 7))
                    evac(o[:, j, :], p[:], [bp], [bo], func=func)
                S.dma(S.sp, sem_ot[i], dst[tq * 512:(tq + 1) * 512, col0:col0 + 512].rearrange("(j p) n -> p j n", p=128),
                      o[:], reads=[bo])

        zcs = [zc, k.sb("zc2", [128, 8 + S_LEN], BF16, st)]
        b_zcs = [b_zc, Buf()]
        S.op(S.dve, lambda: nc.vector.memset(zcs[1][:, 0:8], 0.0), writes=[b_zcs[1]])

        def conv_proj(it):
            w, bw, g, zi = it["w"], it["bw"], it["g"], it["zi"]
            for tt in range(NTT):
                p, bp = next_pf()
                for c in range(8):
                    S.op(S.pe, lambda: nc.tensor.matmul(p[:], lhsT=w[:, c, g * 128:(g + 1) * 128],
                                                        rhs=k.hT[:, c, tt * 512:(tt + 1) * 512],
                                                        start=(c == 0), stop=(c == 7)),
                         reads=[bw] + k.b_hT[tt * 4:tt * 4 + 4], writes=[bp], inc=(c == 7))
                evac(zcs[zi][:, 8 + tt * 512:8 + (tt + 1) * 512], p[:], [bp], [b_zcs[zi]])

        def conv_apply(it):
            cg, zi = it["cg"], it["zi"]
            di = cnt["dg"] % 2
            cnt["dg"] += 1
            d, bd = dg[di], b_dg[di]
            o_cw = PP["convw"][0]
            for j in range(4):
                S.op(S.dve, lambda: nc.vector.tensor_scalar(out=d[:, j, :], in0=k.ppv("ident"),
                                                            scalar1=k.ppt[:, o_cw + cg * 4 + j:o_cw + cg * 4 + j + 1],
                                                            scalar2=None, op0=ALU.mult),
                     reads=[k.b_pp], writes=[bd])
            i = cnt["ob"] % 2
            cnt["ob"] += 1
            o, bo = ob[i], b_ob[i]
            it["o"], it["bo"] = o, bo
            o_cb = PP["convb"][0]
            for tt in range(NTT):
                p, bp = next_pf()
                for j in range(4):
                    S.op(S.pe, lambda: nc.tensor.matmul(p[:], lhsT=d[:, j, :],
                                                        rhs=zcs[zi][:, 5 + j + tt * 512:5 + j + (tt + 1) * 512],
                                                        start=(j == 0), stop=(j == 3)),
                         reads=[bd, b_zcs[zi]], writes=[bp], inc=(j == 3))
                evac(o[:, tt * 512:(tt + 1) * 512], p[:], [bp], [bo], func=AF.Silu,
                     bias=k.ppt[:, o_cb + cg:o_cb + cg + 1])
            S.dma(S.sp, sem_ob[i], it["dst_row"], o[:], reads=[bo])

        def conv_ktrans(it):
            if it["kdst"] is None:
                return
            o, bo, kdst, kcol = it["o"], it["bo"], it["kdst"], it["kcol"]
            for tq in range(8):
                pb, bpb = k.pb[tq % 2], k.b_pb[tq % 2]
                for j in range(4):
                    tb = tq * 4 + j
                    S.op(S.pe, lambda: nc.tensor.transpose(out=pb[:, j * 128:(j + 1) * 128],
                                                           in_=o[:, tb * 128:(tb + 1) * 128], identity=k.ident_bf),
                         reads=[bo, k.b_cbf], writes=[bpb], inc=(j == 3))
                ii = cnt["ot"] % 2
                cnt["ot"] += 1
                t_, bt = ot[ii], b_ot[ii]
                evac(t_[:, :, 0:128], pb[:, 0:512].rearrange("p (j n) -> p j n", j=4), [bpb], [bt])
                S.dma(S.sp, sem_ot[ii],
                      kdst[tq * 512:(tq + 1) * 512, kcol:kcol + 128].rearrange("(j p) n -> p j n", p=128),
                      t_[:, :, 0:128], reads=[bt])

        for half in range(2):
            w, bw = load_w(C_SBV + half * 512)
            tm_tile(w, bw, k.v_sb, half * 512)
        for half in range(2):
            w, bw = load_w(C_MLV + half * 512)
            tm_tile(w, bw, k.v_ml, half * 512)
        for half in range(2):
            w, bw = load_w(C_SBQ + half * 512)
            for g in range(4):
                fm_group(w, bw, g, k.qt_sb[half * 4 + g], scale=128.0 ** -0.5)
        for half in range(2):
            w, bw = load_w(C_SBK + half * 512)
            for g in range(4):
                fm_group(w, bw, g, k.kt_sb[half * 4 + g])
        for half in range(2):
            w, bw = load_w(C_XQ + half * 512)
            for g in range(4):
                fm_group(w, bw, g, k.qt_x[half * 4 + g])
        for half in range(2):
            w, bw = load_w(C_MLO + half * 512)
            tm_tile(w, bw, k.o_ml, half * 512, func=AF.Sigmoid)
        items = []
        for which, c0 in ((0, C_MLQ), (1, C_MLK)):
            for half in range(2):
                for g in range(4):
                    cg = half * 4 + g
                    items.append({"col0": c0 + half * 512, "g": g, "cg": which * 8 + cg, "zi": len(items) % 2,
                                  "dst_row": (k.qt_ml if which == 0 else k.kt_ml)[cg],
                                  "kdst": k.k_ml if which == 1 else None, "kcol": cg * 128})
        wcur = {}

        def get_w(it):
            if it["col0"] not in wcur:
                wcur.clear()
                wcur[it["col0"]] = load_w(it["col0"])
            it["w"], it["bw"] = wcur[it["col0"]]
        get_w(items[0])
        conv_proj(items[0])
        for n, it in enumerate(items):
            if n + 1 < len(items):
                get_w(items[n + 1])
                conv_proj(items[n + 1])
            conv_apply(it)
            if n > 0:
                conv_ktrans(items[n - 1])
        conv_ktrans(items[-1])
        o_bg = PP["bgate"][0]
        for t6 in range(6):
            w, bw = load_w(C_GATE + t6 * 512)
            for g in range(4):
                gg = t6 * 4 + g
                i = cnt["ob"] % 2
                cnt["ob"] += 1
                o, bo = ob[i], b_ob[i]
                for tt in range(NTT):
                    p, bp = next_pf()
                    for c in range(8):
                        S.op(S.pe, lambda: nc.tensor.matmul(p[:], lhsT=w[:, c, g * 128:(g + 1) * 128],
                                                            rhs=k.hT[:, c, tt * 512:(tt + 1) * 512],
                                                            start=(c == 0), stop=(c == 7)),
                             reads=[bw] + k.b_hT[tt * 4:tt * 4 + 4], writes=[bp], inc=(c == 7))
                    evac(o[:, tt * 512:(tt + 1) * 512], p[:], [bp, k.b_pp], [bo], func=AF.Sigmoid,
                         bias=k.ppt[:, o_bg + gg:o_bg + gg + 1])
                S.dma(S.sp, sem_ob[i], k.g_scr[gg], o[:], reads=[bo])
        w, bw = load_w(C_MLI, 8)
        gi = [k.sb(f"gi_rows{i}", [4, 2, 512], F32, st) for i in range(2)]
        b_gi = [Buf(), Buf()]
        sem_g = [S.new_sem("d_gif0"), S.new_sem("d_gif1")]
        o_bif = PP["bif"][0]
        for tt in range(NTT):
            gt, bg = gi[tt % 2], b_gi[tt % 2]
            for which in range(2):
                p, bp = next_pf()
                for c in range(8):
                    S.op(S.pe, lambda: nc.tensor.matmul(p[0:4, :], lhsT=w[:, c, which * 4:which * 4 + 4],
                                                        rhs=k.hT[:, c, tt * 512:(tt + 1) * 512],
                                                        start=(c == 0), stop=(c == 7)),
                         reads=[bw] + k.b_hT[tt * 4:tt * 4 + 4], writes=[bp], inc=(c == 7))
                S.op(S.act, lambda: nc.scalar.activation(out=gt[:, which, :], in_=p[0:4, :],
                                                         func=AF.Identity,
                                                         bias=k.ppt[0:4, o_bif + which:o_bif + which + 1], scale=1.0),
                     reads=[bp, k.b_pp], writes=[bg])
            S.dma(S.sp, sem_g[tt % 2], k.gif[:, tt * 512:(tt + 1) * 512].rearrange("(w h) t -> h w t", w=2), gt[:], reads=[bg])
        S.barrier()


def stage_sb(k):
    nc, S = k.nc, k.S
    NS = 3
    tiles_of = [[7, 3], [6, 4], [5, 2, 1, 0]]
    with ExitStack() as st:
        qT = [k.sb(f"sb_q{i}", [128, S_LEN], BF16, st) for i in range(2)]
        kT = [k.sb(f"sb_k{i}", [128, S_LEN], BF16, st) for i in range(2)]
        vv = [k.sb(f"sb_v{i}", [128, NTB, 128], BF16, st) for i in range(2)]
        b_q = [Buf() for _ in range(2)]
        b_k = [Buf() for _ in range(2)]
        b_v = [Buf() for _ in range(2)]
        sem_in = [S.new_sem(f"d_sbin{i}") for i in range(2)]
        yT = [k.sb(f"sb_y{i}", [128, S_LEN], BF16, st) for i in range(2)]
        b_y = [Buf() for _ in range(2)]
        sem_y = [S.new_sem(f"d_sby{i}") for i in range(2)]
        e_sb = [k.sb(f"sb_e{s}", [128, 512], F32, st) for s in range(NS)]
        b_e = [Buf() for _ in range(NS)]
        nl = [[k.sb(f"sb_nl{s}_{j}", [128, 512], BF16, st) for j in range(2)] for s in range(NS)]
        b_nl = [[Buf() for _ in range(2)] for _ in range(NS)]
        at = [[k.sb(f"sb_at{s}_{j}", [128, 512], BF16, st) for j in range(2)] for s in range(NS)]
        b_at = [[Buf() for _ in range(2)] for _ in range(NS)]
        sacc = [k.sb(f"sb_sacc{s}", [128, 512], BF16, st) for s in range(NS)]
        b_sacc = [Buf() for _ in range(NS)]
        zb = [k.pf[s] for s in range(NS)]
        b_zb = [k.b_pf[s] for s in range(NS)]
        ob = [k.pf[NS + s] for s in range(NS)]
        b_ob = [k.b_pf[NS + s] for s in range(NS)]

        def load_head(h):
            i = h % 2
            S.dma(S.sp, sem_in[i], qT[i][:], k.qt_sb[h], writes=[b_q[i]])
            S.dma(S.sp, sem_in[i], kT[i][:], k.kt_sb[h], writes=[b_k[i]])
            S.dma(S.sp, sem_in[i], vv[i][:], k.v_sb[:, h * 128:(h + 1) * 128].rearrange("(j p) n -> p j n", p=128),
                  writes=[b_v[i]])

        def mm1(h, s, qi, kb):
            i = h % 2
            S.op(S.pe, lambda: nc.tensor.matmul(zb[s][:], lhsT=kT[i][:, kb * 128:(kb + 1) * 128],
                                                rhs=qT[i][:, qi * 512:(qi + 1) * 512], start=True, stop=False),
                 reads=[b_k[i], b_q[i]], writes=[b_zb[s]])

        load_head(0)
        for h in range(8):
            if h + 1 < 8:
                load_head(h + 1)
            i = h % 2
            steps = [[(qi, kb) for qi in tiles_of[s] for kb in range(4 * qi + 3, -1, -1)] for s in range(NS)]
            nsteps = len(steps[0])
            assert all(len(x) == nsteps for x in steps)
            for s in range(NS):
                mm1(h, s, *steps[s][0])
            for n in range(nsteps):
                par = n % 2
                info = []
                for s in range(NS):
                    qi, kb = steps[s][n]
                    r = kb - 4 * qi
                    info.append((qi, kb, r, kb == 4 * qi + 3, kb == 0))
                for s in range(NS):
                    S.op(S.act, lambda: nc.scalar.activation(out=e_sb[s][:], in_=zb[s][:], func=AF.Exp),
                         reads=[b_zb[s]], writes=[b_e[s]])
                for s in range(NS):
                    S.op(S.act, lambda: nc.scalar.activation(out=nl[s][par][:], in_=e_sb[s][:], func=AF.Ln, bias=1.0),
                         reads=[b_e[s]], writes=[b_nl[s][par]])
                for s in range(NS):
                    qi, kb, r, first, last = info[s]
                    if r >= 0:
                        S.op(S.dve, lambda: nc.vector.tensor_tensor(out=nl[s][par][:], in0=nl[s][par][:],
                                                                    in1=k.sbmask_bf[:, r * 512:(r + 1) * 512], op=ALU.mult),
                             reads=[b_nl[s][par], k.b_cbf], writes=[b_nl[s][par]])
                for s in range(NS):
                    qi, kb, r, first, last = info[s]
                    S.op(S.pe, lambda: nc.tensor.matmul(zb[s][:], lhsT=k.ntri_bf, rhs=nl[s][par][:], start=False, stop=first),
                         reads=[b_nl[s][par], k.b_cbf], writes=[b_zb[s]], inc=first)
                    if not first:
                        S.op(S.pe, lambda: nc.tensor.matmul(zb[s][:], lhsT=k.nones_bf, rhs=sacc[s][:], start=False, stop=True),
                             reads=[b_sacc[s], k.b_cbf], writes=[b_zb[s]])
                for s in range(NS):
                    qi, kb, r, first, last = info[s]
                    S.op(S.act, lambda: nc.scalar.activation(out=at[s][par][:], in_=zb[s][:], func=AF.Exp),
                         reads=[b_zb[s]], writes=[b_at[s][par]])
                    if not last:
                        if first:
                            S.op(S.pool, lambda: nc.gpsimd.tensor_copy(out=sacc[s][:], in_=nl[s][par][:]),
                                 reads=[b_nl[s][par]], writes=[b_sacc[s]])
                        else:
                            S.op(S.pool, lambda: nc.gpsimd.tensor_tensor(out=sacc[s][:], in0=sacc[s][:], in1=nl[s][par][:],
                                                                        op=ALU.add),
                                 reads=[b_nl[s][par], b_sacc[s]], writes=[b_sacc[s]])
                for s in range(NS):
                    qi, kb, r, first, last = info[s]
                    if r >= 0:
                        S.op(S.dve, lambda: nc.vector.tensor_tensor(out=at[s][par][:], in0=at[s][par][:],
                                                                    in1=k.sbmask_bf[:, r * 512:(r + 1) * 512], op=ALU.mult),
                             reads=[b_at[s][par], k.b_cbf], writes=[b_at[s][par]])
                    S.op(S.pe, lambda: nc.tensor.matmul(ob[s][:], lhsT=vv[i][:, kb, :], rhs=at[s][par][:],
                                                        start=first, stop=last),
                         reads=[b_v[i], b_at[s][par]], writes=[b_ob[s]])
                    if n + 1 < nsteps:
                        mm1(h, s, *steps[s][n + 1])
                    if last:
                        S.op(S.dve, lambda: nc.vector.tensor_copy(out=yT[i][:, qi * 512:(qi + 1) * 512], in_=ob[s][:]),
                             reads=[b_ob[s]], writes=[b_y[i]])
            S.dma(S.sp, sem_y[i], k.yt_sb[h], yT[i][:], reads=[b_y[i]])
        S.barrier()


def stage_ml(k):
    nc, S = k.nc, k.S
    with ExitStack() as st:
        tab = k.sb("ml_tab", [128, 3, 32, 4], F32, st)
        Mb = k.sb("ml_Mb", [128, 33, 4], F32, st)
        uT = k.sb("ml_u", [128, 32, 4], F32, st)
        wT = k.sb("ml_w", [128, 32, 4], F32, st)
        flT = k.sb("ml_fl", [128, 32, 4], F32, st)
        decT = k.sb("ml_dec", [128, 32, 4], F32, st)
        nl16 = k.sb("ml_nl16", [128, 1], F32, st)
        b_tab, b_Mb, b_u, b_w, b_fl, b_dec, b_c = Buf(), Buf(), Buf(), Buf(), Buf(), Buf(), Buf()
        S.op(S.dve, lambda: nc.vector.memset(nl16[:], -LN16), writes=[b_c])
        S.op(S.dve, lambda: nc.vector.memset(Mb[:, 0, :], 0.0), writes=[b_Mb])
        with ExitStack() as st2:
            fp = k.sb("ml_fp", [4, S_LEN], F32, st2)
            ip = k.sb("ml_ip", [4, S_LEN], F32, st2)
            Fn = k.sb("ml_Fn", [4, S_LEN], F32, st2)
            on = k.sb("ml_on", [4, S_LEN], F32, st2)
            b_fp, b_ip, b_Fn, b_on = Buf(), Buf(), Buf(), Buf()
            sg = S.new_sem("d_mlg")
            S.dma(S.sp, sg, fp[:], k.gif[4:8, :], writes=[b_fp])
            S.dma(S.sp, sg, ip[:], k.gif[0:4, :], writes=[b_ip])
            S.op(S.dve, lambda: nc.vector.memset(on[:], 1.0), writes=[b_on])
            S.op(S.act, lambda: nc.scalar.activation(out=fp[:], in_=fp[:], func=AF.Exp, scale=-1.0),
                 reads=[b_fp], writes=[b_fp])
            S.op(S.act, lambda: nc.scalar.activation(out=fp[:], in_=fp[:], func=AF.Ln, bias=1.0),
                 reads=[b_fp], writes=[b_fp])
            S.op(S.dve, lambda: nc.vector.tensor_tensor_scan(out=Fn[:], data0=on[:], data1=fp[:], initial=0.0,
                                                             op0=ALU.mult, op1=ALU.add),
                 reads=[b_on, b_fp], writes=[b_Fn])
            S.op(S.dve, lambda: nc.vector.tensor_tensor(out=ip[:], in0=ip[:], in1=Fn[:], op=ALU.add),
                 reads=[b_ip, b_Fn], writes=[b_ip])
            S.op(S.dve, lambda: nc.vector.tensor_tensor_scan(out=fp[:], data0=ip[:], data1=ip[:], initial=0.0,
                                                             op0=ALU.max, op1=ALU.max),
                 reads=[b_ip], writes=[b_fp])
            pt, bpt = k.pf[0], k.b_pf[0]
            ptv = pt[:, 0:384].rearrange("p (q c h) -> p q c h", q=3, c=32)
            idf = k.ppv("ident")
            for q, (X, bX) in enumerate(((Fn, b_Fn), (ip, b_ip), (fp, b_fp))):
                for c in range(32):
                    S.op(S.pe, lambda: nc.tensor.transpose(out=ptv[:, q, c, :], in_=X[0:4, c * 128:(c + 1) * 128],
                                                           identity=idf[0:4, 0:4]),
                         reads=[bX, k.b_pp], writes=[bpt], inc=(q == 2 and c == 31))
            S.op(S.dve, lambda: nc.vector.tensor_copy(out=tab[:], in_=ptv), reads=[bpt], writes=[b_tab])
            S.barrier()
        pm, bpm = k.pf[1], k.b_pf[1]
        o_sel = PP["sel127"][0]
        S.op(S.pe, lambda: nc.tensor.matmul(pm[:, 0:128], lhsT=k.ppt[:, o_sel:o_sel + 128],
                                            rhs=tab[:, 2].rearrange("p c h -> p (c h)"), start=True, stop=True),
             reads=[b_tab, k.b_pp], writes=[bpm])
        S.op(S.dve, lambda: nc.vector.tensor_copy(out=Mb[:, 1:33, :], in_=pm[:, 0:128].rearrange("p (c h) -> p c h", c=32)),
             reads=[bpm], writes=[b_Mb])
        tmp = k.sb("ml_tmp", [128, 32, 4], F32, st)
        b_tmp = Buf()

        def table(dst, bd, in0, in1, bias):
            S.op(S.dve, lambda: nc.vector.tensor_tensor(out=tmp[:], in0=in0, in1=in1, op=ALU.subtract),
                 reads=[b_tab, b_Mb], writes=[b_tmp])
            if bias:
                S.op(S.act, lambda: nc.scalar.activation(out=dst[:], in_=tmp[:], func=AF.Exp, bias=nl16[:, 0:1]),
                     reads=[b_tmp, b_c], writes=[bd])
            else:
                S.op(S.act, lambda: nc.scalar.activation(out=dst[:], in_=tmp[:], func=AF.Exp),
                     reads=[b_tmp], writes=[bd])
        table(uT, b_u, tab[:, 1], Mb[:, 0:32, :], True)
        table(wT, b_w, tab[:, 1], Mb[:, 1:33, :], True)
        table(flT, b_fl, tab[:, 0], Mb[:, 0:32, :], False)
        table(decT, b_dec, Mb[:, 0:32, :], Mb[:, 1:33, :], False)

        if "ml_tabs" in k.dbg:
            td = nc.dram_tensor("ml_tabs", [128, 7, 128], F32, kind="ExternalOutput").ap()
            sdd = S.new_sem("d_mltab")
            S.dma(S.sp, sdd, td[:, 0:3, :], tab[:].rearrange("p q c h -> p q (c h)"), reads=[b_tab])
            for qi_, (t_, b_) in enumerate(((uT, b_u), (wT, b_w), (flT, b_fl), (decT, b_dec))):
                S.dma(S.sp, sdd, td[:, 3 + qi_, :], t_[:].rearrange("p c h -> p (c h)"), reads=[b_])
        NH = 2
        qTt = [[k.sb(f"ml_q{a}_{i}", [128, 2, 512], BF16, st) for i in range(2)] for a in range(NH)]
        kTt = [[k.sb(f"ml_k{a}_{i}", [128, 2, 512], BF16, st) for i in range(2)] for a in range(NH)]
        ktm = [[k.sb(f"ml_kt{a}_{i}", [128, 4, 256], BF16, st) for i in range(2)] for a in range(NH)]
        vtm = [[k.sb(f"ml_vt{a}_{i}", [128, 4, 256], BF16, st) for i in range(2)] for a in range(NH)]
        otm = [[k.sb(f"ml_ot{a}_{i}", [128, 4, 256], BF16, st) for i in range(2)] for a in range(NH)]
        b_in = [[Buf() for i in range(2)] for a in range(NH)]
        sem_in = [[S.new_sem(f"d_mlin{a}_{i}") for i in range(2)] for a in range(NH)]
        yTt = [[k.sb(f"ml_y{a}_{i}", [128, 2, 512], BF16, st) for i in range(2)] for a in range(NH)]
        b_y = [[Buf() for i in range(2)] for a in range(NH)]
        sem_y = [[S.new_sem(f"d_mly{a}_{i}") for i in range(2)] for a in range(NH)]
        PT = [[k.sb(f"ml_PT{a}_{i}", [128, 128], BF16, st) for i in range(2)] for a in range(NH)]
        vu = [[k.sb(f"ml_vu{a}_{i}", [128, 264], BF16, st) for i in range(2)] for a in range(NH)]
        vw = [[k.sb(f"ml_vw{a}_{i}", [128, 264], BF16, st) for i in range(2)] for a in range(NH)]
        b_PT = [[Buf() for i in range(2)] for a in range(NH)]
        b_vu = [[Buf() for i in range(2)] for a in range(NH)]
        b_vw = [[Buf() for i in range(2)] for a in range(NH)]
        C32 = [k.sb(f"ml_C32_{a}", [128, 2, 264], F32, st) for a in range(NH)]
        Cbf = [[k.sb(f"ml_Cbf{a}_{i}", [128, 2, 264], BF16, st) for i in range(2)] for a in range(NH)]
        b_C32 = [Buf() for a in range(NH)]
        b_Cbf = [[Buf() for i in range(2)] for a in range(NH)]
        hs = [k.sb(f"ml_hs{a}", [128, 256], F32, st) for a in range(NH)]
        t1 = [k.sb(f"ml_t1{a}", [128, 256], F32, st) for a in range(NH)]
        yb = [k.sb(f"ml_yb{a}", [128, 256], BF16, st) for a in range(NH)]
        jk = [k.sb(f"ml_jk{a}", [128, 256], BF16, st) for a in range(NH)]
        sm = [k.sb(f"ml_sm{a}", [128, 8], F32, st) for a in range(NH)]
        b_hs = [Buf() for a in range(NH)]
        b_t1 = [Buf() for a in range(NH)]
        b_yb = [Buf() for a in range(NH)]
        b_jk = [Buf() for a in range(NH)]
        b_sm = [[Buf() for _ in range(8)] for a in range(NH)]
        mlng = k.ppv("mlng")

        def load_group(a, h, g):
            i = g % 2
            sm_, bb = sem_in[a][i], [b_in[a][i]]
            cols = slice(g * 512, (g + 1) * 512)
            S.dma(S.sp, sm_, qTt[a][i][:], k.qt_ml[2 * h:2 * h + 2, :, cols].rearrange("c p t -> p c t"), writes=bb)
            S.dma(S.sp, sm_, kTt[a][i][:], k.kt_ml[2 * h:2 * h + 2, :, cols].rearrange("c p t -> p c t"), writes=bb)
            for dst, src in ((ktm, k.k_ml), (vtm, k.v_ml), (otm, k.o_ml)):
                S.dma(S.sp, sm_, dst[a][i][:], src[cols, h * 256:(h + 1) * 256].rearrange("(j p) n -> p j n", p=128), writes=bb)

        for hp in range(2):
            heads = [2 * hp, 2 * hp + 1]
            for a in range(NH):
                load_group(a, heads[a], 0)
            for g in range(8):
                if g + 1 < 8:
                    for a in range(NH):
                        load_group(a, heads[a], g + 1)
                gi = g % 2
                for j in range(4):
                    c = g * 4 + j
                    par = c % 2
                    for a in range(NH):
                        h = heads[a]
                        col = c * 4 + h
                        bank1, bb1 = k.pf[3 * a], k.b_pf[3 * a]
                        bank2, bb2 = k.pf[3 * a + 1], k.b_pf[3 * a + 1]
                        bank3, bb3 = k.pf[3 * a + 2], k.b_pf[3 * a + 2]
                        bin_ = b_in[a][gi]
                        tsl = slice(j * 128, (j + 1) * 128)
                        uc = uT[:].rearrange("p c h -> p (c h)")[:, col:col + 1]
                        wc = wT[:].rearrange("p c h -> p (c h)")[:, col:col + 1]
                        flc = flT[:].rearrange("p c h -> p (c h)")[:, col:col + 1]
                        dcc = decT[:].rearrange("p c h -> p (c h)")[:, col:col + 1]
                        ps_s = bank1[:, 264:392]
                        for dc in range(2):
                            S.op(S.pe, lambda: nc.tensor.matmul(ps_s, lhsT=kTt[a][gi][:, dc, tsl], rhs=qTt[a][gi][:, dc, tsl],
                                                                start=(dc == 0), stop=(dc == 1)),
                                 reads=[bin_], writes=[bb1], inc=(dc == 1))
                        S.op(S.dve, lambda: nc.vector.tensor_tensor(out=PT[a][par][:], in0=ps_s, in1=k.mlmask_bf, op=ALU.mult),
                             reads=[bb1, k.b_cbf], writes=[b_PT[a][par]])
                        for (dst, bd, sc, bs) in ((vu[a][par], b_vu[a][par], uc, b_u), (vw[a][par], b_vw[a][par], wc, b_w)):
                            S.op(S.act, lambda: nc.scalar.activation(out=dst[:, 0:256], in_=vtm[a][gi][:, j, :], func=AF.Copy, scale=sc),
                                 reads=[bin_, bs], writes=[bd])
                            S.op(S.act, lambda: nc.scalar.copy(out=dst[:, 256:257], in_=sc), reads=[bs], writes=[bd])
                        ps_n = bank3[:, 0:257]
                        S.op(S.pe, lambda: nc.tensor.matmul(ps_n, lhsT=PT[a][par][:], rhs=vu[a][par][:, 0:257],
                                                            start=True, stop=(c == 0)),
                             reads=[b_PT[a][par], b_vu[a][par]], writes=[bb3], inc=(c == 0))
                        if c > 0:
                            cb, bcb = Cbf[a][1 - par], b_Cbf[a][1 - par]
                            for dc in range(2):
                                S.op(S.pe, lambda: nc.tensor.matmul(ps_n, lhsT=qTt[a][gi][:, dc, tsl], rhs=cb[:, dc, 0:257],
                                                                    start=False, stop=(dc == 1)),
                                     reads=[bin_, bcb], writes=[bb3], inc=(dc == 1))
                        ps_c = [bank1[:, 0:257], bank2[:, 0:257]]
                        bbc = [bb1, bb2]
                        for dc in range(2):
                            S.op(S.pe, lambda: nc.tensor.matmul(ps_c[dc], lhsT=ktm[a][gi][:, j, dc * 128:(dc + 1) * 128],
                                                                rhs=vw[a][par][:, 0:257], start=True, stop=True),
                                 reads=[bin_, b_vw[a][par]], writes=[bbc[dc]])
                        for dc in range(2):
                            if c == 0:
                                S.op(S.dve, lambda: nc.vector.tensor_copy(out=C32[a][:, dc, 0:257], in_=ps_c[dc]),
                                     reads=[bbc[dc]], writes=[b_C32[a]])
                            else:
                                S.op(S.dve, lambda: nc.vector.scalar_tensor_tensor(out=C32[a][:, dc, 0:257], in0=C32[a][:, dc, 0:257],
                                                                                   scalar=dcc, in1=ps_c[dc],
                                                                                   op0=ALU.mult, op1=ALU.add),
                                     reads=[bbc[dc], b_C32[a], b_dec], writes=[b_C32[a]])
                        S.op(S.pool, lambda: nc.gpsimd.tensor_copy(out=Cbf[a][par][:, :, 0:257], in_=C32[a][:, :, 0:257]),
                             reads=[b_C32[a]], writes=[b_Cbf[a][par]])
                        smt = sm[a]
                        S.op(S.act, lambda: nc.scalar.activation(out=smt[:, 5:6], in_=bank3[:, 256:257], func=AF.Abs),
                             reads=[bb3], writes=[b_sm[a][5]])
                        S.op(S.dve, lambda: nc.vector.tensor_tensor(out=smt[:, 0:1], in0=smt[:, 5:6], in1=flc, op=ALU.max),
                             reads=[b_sm[a][5], b_fl], writes=[b_sm[a][0]])
                        S.op(S.dve, lambda: nc.vector.reciprocal(out=smt[:, 1:2], in_=smt[:, 0:1]),
                             reads=[b_sm[a][0]], writes=[b_sm[a][1]])
                        S.op(S.act, lambda: nc.scalar.activation(out=hs[a][:], in_=bank3[:, 0:256], func=AF.Copy, scale=smt[:, 1:2]),
                             reads=[bb3, b_sm[a][1]], writes=[b_hs[a]])
                        S.op(S.act, lambda: nc.scalar.activation(out=jk[a][:], in_=hs[a][:], func=AF.Square, accum_out=smt[:, 2:3]),
                             reads=[b_hs[a]], writes=[b_jk[a], b_sm[a][2]])
                        S.op(S.act, lambda: nc.scalar.activation(out=smt[:, 3:4], in_=smt[:, 2:3], func=AF.Sqrt,
                                                                 bias=k.eps_t[:, 0:1], scale=1.0 / 256),
                             reads=[b_sm[a][2], k.b_eps], writes=[b_sm[a][3]])
                        S.op(S.dve, lambda: nc.vector.reciprocal(out=smt[:, 4:5], in_=smt[:, 3:4]),
                             reads=[b_sm[a][3]], writes=[b_sm[a][4]])
                        S.op(S.dve, lambda: nc.vector.scalar_tensor_tensor(out=t1[a][:], in0=hs[a][:], scalar=smt[:, 4:5],
                                                                           in1=mlng[:, h * 256:(h + 1) * 256],
                                                                           op0=ALU.mult, op1=ALU.mult),
                             reads=[b_hs[a], b_sm[a][4], k.b_pp], writes=[b_t1[a]])
                        S.op(S.pool, lambda: nc.gpsimd.tensor_tensor(out=yb[a][:], in0=t1[a][:], in1=otm[a][gi][:, j, :], op=ALU.mult),
                             reads=[b_t1[a], bin_], writes=[b_yb[a]])
                        pb, bpb = k.pb[a], k.b_pb[a]
                        for dc in range(2):
                            S.op(S.pe, lambda: nc.tensor.transpose(out=pb[:, dc * 128:(dc + 1) * 128], in_=yb[a][:, dc * 128:(dc + 1) * 128],
                                                                   identity=k.ident_bf),
                                 reads=[b_yb[a], k.b_cbf], writes=[bpb], inc=(dc == 1))
                        S.op(S.act, lambda: nc.scalar.copy(out=yTt[a][gi][:, :, tsl], in_=pb[:, 0:256].rearrange("p (c t) -> p c t", c=2)),
                             reads=[bpb], writes=[b_y[a][gi]])
                for a in range(NH):
                    h = heads[a]
                    S.dma(S.sp, sem_y[a][gi], k.yt_ml[2 * h:2 * h + 2, :, g * 512:(g + 1) * 512].rearrange("c p t -> p c t"),
                          yTt[a][gi][:], reads=[b_y[a][gi]])
        S.barrier()


def stage_xa(k):
    nc, S = k.nc, k.S
    with ExitStack() as st:
        mt = k.sb("xa_mt", [128, 2, D], F32, st)
        mn = k.sb("xa_mn", [128, D], BF16, st)
        jk = k.sb("xa_jk", [128, D], BF16, st)
        memT = k.sb("xa_memT", [128, 8, N_MEM], BF16, st)
        knT = k.sb("xa_knT", [128, 4, 2, N_MEM], BF16, st)
        vmem = k.sb("xa_vmem", [128, 2, D], BF16, st)
        kn = k.sb("xa_kn", [128, 256], BF16, st)
        sm = k.sb("xa_sm", [128, 64], F32, st)
        wkv = [k.sb(f"xa_w{i}", [128, 8, 512], BF16, st) for i in range(4)]
        b_mt, b_mn, b_jk, b_memT, b_knT, b_vmem, b_kn = Buf(), Buf(), Buf(), Buf(), Buf(), Buf(), Buf()
        b_w = [Buf() for _ in range(4)]
        sem = S.new_sem("d_xa0")
        sem_w = [S.new_sem(f"d_xaw{i}") for i in range(4)]
        S.dma(S.sp, sem, mt[:], k.mem.rearrange("(j p) d -> p j d", p=128), writes=[b_mt])
        for i in range(4):
            S.dma(S.pool, sem_w[i], wkv[i][:], k.w_mem_kv[:, i * 512:(i + 1) * 512].rearrange("(c p) n -> p c n", p=128),
                  writes=[b_w[i]])
        nsm = [0]

        def smcol():
            c = nsm[0]
            nsm[0] += 1
            return sm[:, c:c + 1], Buf()

        def rstd_of(src_ap, src_bufs, n):
            ss, bss = smcol()
            rm, brm = smcol()
            rs, brs = smcol()
            S.op(S.act, lambda: nc.scalar.activation(out=jk[:, 0:n], in_=src_ap, func=AF.Square, accum_out=ss),
                 reads=src_bufs, writes=[b_jk, bss])
            S.op(S.act, lambda: nc.scalar.activation(out=rm, in_=ss, func=AF.Sqrt, bias=k.eps_t[:, 0:1], scale=1.0 / n),
                 reads=[bss, k.b_eps], writes=[brm])
            S.op(S.dve, lambda: nc.vector.reciprocal(out=rs, in_=rm), reads=[brm], writes=[brs])
            return rs, brs

        gmem = k.ppv("gmem")
        for j in range(2):
            rs, brs = rstd_of(mt[:, j, :], [b_mt], D)
            S.op(S.dve, lambda: nc.vector.scalar_tensor_tensor(out=mn[:], in0=mt[:, j, :], scalar=rs, in1=gmem,
                                                               op0=ALU.mult, op1=ALU.mult),
                 reads=[b_mt, brs, k.b_pp], writes=[b_mn])
            pb, bpb = k.pb[j], k.b_pb[j]
            for c in range(8):
                S.op(S.pe, lambda: nc.tensor.transpose(out=pb[:, c * 128:(c + 1) * 128], in_=mn[:, c * 128:(c + 1) * 128],
                                                       identity=k.ident_bf),
                     reads=[b_mn, k.b_cbf], writes=[bpb], inc=(c == 7))
            S.op(S.act, lambda: nc.scalar.copy(out=memT[:, :, j * 128:(j + 1) * 128], in_=pb[:].rearrange("p (c t) -> p c t", c=8)),
                 reads=[bpb], writes=[b_memT])
        kng = k.ppv("kng")
        npf = [0]

        def next_pf():
            i = npf[0] % 6
            npf[0] += 1
            return k.pf[i], k.b_pf[i]
        for j in range(2):
            for t4 in range(4):
                p, bp = next_pf()
                for c in range(8):
                    S.op(S.pe, lambda: nc.tensor.matmul(p[:], lhsT=memT[:, c, j * 128:(j + 1) * 128], rhs=wkv[t4][:, c, :],
                                                        start=(c == 0), stop=(c == 7)),
                         reads=[b_memT, b_w[t4]], writes=[bp], inc=(c == 7))
                if t4 < 2:
                    for hh in range(2):
                        h = t4 * 2 + hh
                        rs, brs = rstd_of(p[:, hh * 256:(hh + 1) * 256], [bp], 256)
                        S.op(S.dve, lambda: nc.vector.scalar_tensor_tensor(out=kn[:], in0=p[:, hh * 256:(hh + 1) * 256], scalar=rs,
                                                                           in1=kng, op0=ALU.mult, op1=ALU.mult),
                             reads=[bp, brs, k.b_pp], writes=[b_kn])
                        pb, bpb = k.pb[hh], k.b_pb[hh]
                        for dc in range(2):
                            S.op(S.pe, lambda: nc.tensor.transpose(out=pb[:, dc * 128:(dc + 1) * 128], in_=kn[:, dc * 128:(dc + 1) * 128],
                                                                   identity=k.ident_bf),
                                 reads=[b_kn, k.b_cbf], writes=[bpb], inc=(dc == 1))
                        S.op(S.act, lambda: nc.scalar.copy(out=knT[:, h, :, j * 128:(j + 1) * 128],
                                                           in_=pb[:, 0:256].rearrange("p (c t) -> p c t", c=2)),
                             reads=[bpb], writes=[b_knT])
                else:
                    S.op(S.dve, lambda: nc.vector.tensor_copy(out=vmem[:, j, (t4 - 2) * 512:(t4 - 1) * 512], in_=p[:]),
                         reads=[bp], writes=[b_vmem])
        qx = [k.sb(f"xa_qx{i}", [128, 2, 512], BF16, st) for i in range(2)]
        b_qx = [Buf(), Buf()]
        sem_q = [S.new_sem("d_xaq0"), S.new_sem("d_xaq1")]
        yx = [k.sb(f"xa_yx{i}", [128, 2, 512], BF16, st) for i in range(2)]
        b_yx = [Buf(), Buf()]
        sem_y = [S.new_sem("d_xay0"), S.new_sem("d_xay1")]
        sq = k.sb("xa_sq", [128, 2, 512], BF16, st)
        rq = k.sb("xa_rq", [128, 512], F32, st)
        rq2 = k.sb("xa_rq2", [128, 512], F32, st)
        qn = k.sb("xa_qn", [128, 2, 512], BF16, st)
        pT = k.sb("xa_pT", [128, 2, 512], BF16, st)
        rden = k.sb("xa_rden", [128, 512], F32, st)
        b_sq, b_rq, b_rq2, b_qn, b_pT, b_rden = Buf(), Buf(), Buf(), Buf(), Buf(), Buf()
        qng = k.ppv("qng")
        it = 0
        for h in range(4):
            for tt in range(NTT):
                i = it % 2
                it += 1
                cols = slice(tt * 512, (tt + 1) * 512)
                S.dma(S.sp, sem_q[i], qx[i][:], k.qt_x[2 * h:2 * h + 2, :, cols].rearrange("c p t -> p c t"), writes=[b_qx[i]])
                S.op(S.act, lambda: nc.scalar.activation(out=sq[:], in_=qx[i][:], func=AF.Square), reads=[b_qx[i]], writes=[b_sq])
                p, bp = next_pf()
                for dc in range(2):
                    S.op(S.pe, lambda: nc.tensor.matmul(p[:], lhsT=k.ones_bf, rhs=sq[:, dc, :], start=(dc == 0), stop=(dc == 1)),
                         reads=[b_sq, k.b_cbf], writes=[bp], inc=(dc == 1))
                S.op(S.act, lambda: nc.scalar.activation(out=rq[:], in_=p[:], func=AF.Sqrt, bias=k.eps_t[:, 0:1], scale=1.0 / 256),
                     reads=[bp, k.b_eps], writes=[b_rq])
                S.op(S.dve, lambda: nc.vector.reciprocal(out=rq2[:], in_=rq[:]), reads=[b_rq], writes=[b_rq2])
                for dc in range(2):
                    S.op(S.dve, lambda: nc.vector.scalar_tensor_tensor(out=qn[:, dc, :], in0=qx[i][:, dc, :], scalar=qng[:, dc:dc + 1],
                                                                       in1=rq2[:], op0=ALU.mult, op1=ALU.mult),
                         reads=[b_qx[i], b_rq2, k.b_pp], writes=[b_qn])
                for mc in range(2):
                    p, bp = next_pf()
                    for dc in range(2):
                        S.op(S.pe, lambda: nc.tensor.matmul(p[:], lhsT=knT[:, h, dc, mc * 128:(mc + 1) * 128], rhs=qn[:, dc, :],
                                                            start=(dc == 0), stop=(dc == 1)),
                             reads=[b_knT, b_qn], writes=[bp], inc=(dc == 1))
                    S.op(S.act, lambda: nc.scalar.activation(out=pT[:, mc, :], in_=p[:], func=AF.Exp, scale=1.0 / 16),
                         reads=[bp], writes=[b_pT])
                p, bp = next_pf()
                for mc in range(2):
                    S.op(S.pe, lambda: nc.tensor.matmul(p[:], lhsT=k.ones_bf, rhs=pT[:, mc, :], start=(mc == 0), stop=(mc == 1)),
                         reads=[b_pT, k.b_cbf], writes=[bp], inc=(mc == 1))
                S.op(S.dve, lambda: nc.vector.reciprocal(out=rden[:], in_=p[:]), reads=[bp], writes=[b_rden])
                for dvc in range(2):
                    p, bp = next_pf()
                    for mc in range(2):
                        S.op(S.pe, lambda: nc.tensor.matmul(p[:], lhsT=vmem[:, mc, h * 256 + dvc * 128:h * 256 + (dvc + 1) * 128],
                                                            rhs=pT[:, mc, :], start=(mc == 0), stop=(mc == 1)),
                             reads=[b_vmem, b_pT], writes=[bp], inc=(mc == 1))
                    S.op(S.dve, lambda: nc.vector.tensor_tensor(out=yx[i][:, dvc, :], in0=p[:], in1=rden[:], op=ALU.mult),
                         reads=[bp, b_rden], writes=[b_yx[i]])
                S.dma(S.sp, sem_y[i], k.yt_x[2 * h:2 * h + 2, :, cols].rearrange("c p t -> p c t"), yx[i][:], reads=[b_yx[i]])
        S.barrier()


def stage_4a(k):
    nc, S = k.nc, k.S
    with ExitStack() as st:
        Wb = [k.sb(f"a_W{b}", [128, 8, D], BF16, st) for b in range(4)]
        b_W = [Buf() for _ in range(4)]
        sem_w = [S.new_sem(f"d_4aw{i}") for i in range(4)]
        for b, src in enumerate((k.w_sb, k.w_ml, k.w_x, k.w_out)):
            for hf in range(2):
                S.dma(S.pool, sem_w[b], Wb[b][:, :, hf * 512:(hf + 1) * 512],
                      src[:, hf * 512:(hf + 1) * 512].rearrange("(c p) n -> p c n", p=128), writes=[b_W[b]])
        yt = [k.sb(f"a_y{i}", [128, 8, 512], BF16, st) for i in range(2)]
        gt = [k.sb(f"a_g{i}", [128, 8, 512], BF16, st) for i in range(2)]
        b_yt = [Buf(), Buf()]
        b_gt = [Buf(), Buf()]
        sem_in = [S.new_sem("d_4ain0"), S.new_sem("d_4ain1")]
        mixed = k.sb("a_mixed", [128, 8, 512], F32, st)
        mixbf = k.sb("a_mixbf", [128, 8, 512], BF16, st)
        tmp = [k.sb(f"a_tmp{i}", [128, 512], F32, st) for i in range(2)]
        b_mixed = [Buf() for _ in range(8)]
        b_mixbf, b_tmp = Buf(), [Buf(), Buf()]
        xt = k.sb("a_xt", [128, 4, D], F32, st)
        b_xt = [Buf() for _ in range(4)]
        sem_x = S.new_sem("d_4ax")
        sem_x1 = S.new_sem("d_4ax1")
        h2 = k.sb("a_h2", [128, D], BF16, st)
        jk = k.sb("a_jk", [128, D], BF16, st)
        h2T = [k.sb(f"a_h2T{i}", [128, 8, 512], BF16, st) for i in range(2)]
        b_h2, b_jk, b_h2T = Buf(), Buf(), [Buf(), Buf()]
        sem_h = [S.new_sem("d_4ah0"), S.new_sem("d_4ah1")]
        sm = k.sb("a_sm", [128, 3 * NTB], F32, st)
        gmlp = k.ppv("gmlp")
        ysrc = (k.yt_sb, k.yt_ml, k.yt_x)
        npf = [0]
        ntmp = [0]
        itc = [0]
        h2b = [h2, k.sb("a_h2b", [128, D], BF16, st)]
        b_h2b = [b_h2, Buf()]

        def branch(tt, b):
            cols = slice(tt * 512, (tt + 1) * 512)
            i = itc[0] % 2
            itc[0] += 1
            S.dma(S.sp, sem_in[i], yt[i][:], ysrc[b][:, :, cols].rearrange("c p t -> p c t"), writes=[b_yt[i]])
            S.dma(S.sp, sem_in[i], gt[i][:], k.g_scr[b * 8:(b + 1) * 8, :, cols].rearrange("c p t -> p c t"), writes=[b_gt[i]])
            for n in range(8):
                p, bp = k.pf[npf[0] % 6], k.b_pf[npf[0] % 6]
                npf[0] += 1
                for fc in range(8):
                    S.op(S.pe, lambda: nc.tensor.matmul(p[:], lhsT=Wb[b][:, fc, n * 128:(n + 1) * 128], rhs=yt[i][:, fc, :],
                                                        start=(fc == 0), stop=(fc == 7)),
                         reads=[b_W[b], b_yt[i]], writes=[bp], inc=(fc == 7))
                if b == 0:
                    S.op(S.dve, lambda: nc.vector.tensor_tensor(out=mixed[:, n, :], in0=p[:], in1=gt[i][:, n, :], op=ALU.mult),
                         reads=[bp, b_gt[i]], writes=[b_mixed[n]])
                else:
                    ti = ntmp[0] % 2
                    ntmp[0] += 1
                    S.op(S.dve, lambda: nc.vector.tensor_tensor(out=tmp[ti][:], in0=p[:], in1=gt[i][:, n, :], op=ALU.mult),
                         reads=[bp, b_gt[i]], writes=[b_tmp[ti]])
                    S.op(S.pool, lambda: nc.gpsimd.tensor_tensor(out=mixed[:, n, :], in0=mixed[:, n, :], in1=tmp[ti][:], op=ALU.add),
                         reads=[b_tmp[ti], b_mixed[n]], writes=[b_mixed[n]])
                    if b == 2:
                        S.op(S.act, lambda: nc.scalar.copy(out=mixbf[:, n, :], in_=mixed[:, n, :]),
                             reads=[b_mixed[n]], writes=[b_mixbf])

        def transposes(tt, j):
            hi = tt % 2
            hb_, bhb = h2b[j % 2], b_h2b[j % 2]
            pb, bpb = k.pb[j % 2], k.b_pb[j % 2]
            for c in range(8):
                S.op(S.pe, lambda: nc.tensor.transpose(out=pb[:, c * 128:(c + 1) * 128], in_=hb_[:, c * 128:(c + 1) * 128],
                                                       identity=k.ident_bf),
                     reads=[bhb, k.b_cbf], writes=[bpb], inc=(c == 7))
            S.op(S.act, lambda: nc.scalar.copy(out=h2T[hi][:, :, j * 128:(j + 1) * 128], in_=pb[:].rearrange("p (c t) -> p c t", c=8)),
                 reads=[bpb], writes=[b_h2T[hi]])

        def outproj(tt):
            cols = slice(tt * 512, (tt + 1) * 512)
            hi = tt % 2
            S.dma(S.sp, sem_x, xt[:], k.x[cols, :].rearrange("(j p) d -> p j d", p=128), writes=b_xt)
            for j in range(4):
                tb = tt * 4 + j
                for hf in range(2):
                    p, bp = k.pf[npf[0] % 6], k.b_pf[npf[0] % 6]
                    npf[0] += 1
                    for fc in range(8):
                        S.op(S.pe, lambda: nc.tensor.matmul(p[:], lhsT=mixbf[:, fc, j * 128:(j + 1) * 128],
                                                            rhs=Wb[3][:, fc, hf * 512:(hf + 1) * 512],
                                                            start=(fc == 0), stop=(fc == 7)),
                             reads=[b_W[3], b_mixbf], writes=[bp], inc=(fc == 7))
                    S.op(S.dve, lambda: nc.vector.tensor_tensor(out=xt[:, j, hf * 512:(hf + 1) * 512], in0=p[:],
                                                                in1=xt[:, j, hf * 512:(hf + 1) * 512], op=ALU.add),
                         reads=[bp, b_xt[j]], writes=[b_xt[j]])
                if j > 0:
                    transposes(tt, j - 1)
                ss, rm, rs = sm[:, 3 * tb:3 * tb + 1], sm[:, 3 * tb + 1:3 * tb + 2], sm[:, 3 * tb + 2:3 * tb + 3]
                b_ss, b_rm, b_rs = Buf(), Buf(), Buf()
                hb_, bhb = h2b[j % 2], b_h2b[j % 2]
                S.op(S.act, lambda: nc.scalar.activation(out=jk[:], in_=xt[:, j, :], func=AF.Square, accum_out=ss),
                     reads=[b_xt[j]], writes=[b_jk, b_ss])
                S.op(S.act, lambda: nc.scalar.activation(out=rm, in_=ss, func=AF.Sqrt, bias=k.eps_t[:, 0:1], scale=1.0 / D),
                     reads=[b_ss, k.b_eps], writes=[b_rm])
                S.op(S.dve, lambda: nc.vector.reciprocal(out=rs, in_=rm), reads=[b_rm], writes=[b_rs])
                S.op(S.dve, lambda: nc.vector.scalar_tensor_tensor(out=hb_[:], in0=xt[:, j, :], scalar=rs, in1=gmlp,
                                                                   op0=ALU.mult, op1=ALU.mult),
                     reads=[b_xt[j], b_rs, k.b_pp], writes=[bhb])
            transposes(tt, 3)
            S.dma(S.pool, sem_x1, k.x1_scr[cols, :].rearrange("(j p) d -> p j d", p=128), xt[:], reads=b_xt)
            S.dma(S.pool, sem_h[hi], k.h2t_scr[:, :, cols].rearrange("c p t -> p c t"), h2T[hi][:], reads=[b_h2T[hi]])

        for b in range(3):
            branch(0, b)
        for tt in range(NTT):
            if tt + 1 < NTT:
                branch(tt + 1, 0)
            outproj(tt)
            if tt + 1 < NTT:
                branch(tt + 1, 1)
                branch(tt + 1, 2)
        S.barrier()


def stage_4b(k):
    nc, S = k.nc, k.S
    with ExitStack() as st:
        W1 = k.sb("b_W1", [128, 8, 4096], BF16, st)
        W2 = k.sb("b_W2", [128, 32, D], BF16, st)
        b_W1 = [Buf() for _ in range(4)]
        b_W2 = [Buf() for _ in range(4)]
        sem_w = [S.new_sem(f"d_4bw{i}") for i in range(8)]
        for q in range(4):
            S.dma(S.pool, sem_w[q], W1[:, :, q * 1024:(q + 1) * 1024],
                  k.w_ff1[:, q * 1024:(q + 1) * 1024].rearrange("(c p) n -> p c n", p=128), writes=[b_W1[q]])
        for q in range(4):
            S.dma(S.pool, sem_w[4 + q], W2[:, q * 8:(q + 1) * 8, :],
                  k.w_ff2[q * 1024:(q + 1) * 1024, :].rearrange("(c p) n -> p c n", p=128), writes=[b_W2[q]])
        h2T = [k.sb(f"b_h2T{i}", [128, 8, 512], BF16, st) for i in range(2)]
        b_h2T = [Buf(), Buf()]
        sem_h = [S.new_sem("d_4bh0"), S.new_sem("d_4bh1")]
        x1b = [k.sb(f"b_x1{i}", [128, D], F32, st) for i in range(2)]
        b_x1 = [Buf(), Buf()]
        sem_x = [S.new_sem("d_4bx0"), S.new_sem("d_4bx1")]
        sem_o = [S.new_sem("d_4bo0"), S.new_sem("d_4bo1")]
        aT = k.sb("b_aT", [128, 32, 512], BF16, st)
        b_aT = [Buf() for _ in range(32)]
        rr = [k.sb(f"b_r{i}", [128, 512], F32, st) for i in range(2)]
        b_rr = [Buf(), Buf()]
        npf = 0
        nx = 0
        for tt in range(NTT):
            cols = slice(tt * 512, (tt + 1) * 512)
            hi = tt % 2
            S.dma(S.sp, sem_h[hi], h2T[hi][:], k.h2t_scr[:, :, cols].rearrange("c p t -> p c t"), writes=[b_h2T[hi]])
            for fc in range(32):
                p, bp = k.pf[npf % 6], k.b_pf[npf % 6]
                npf += 1
                for c in range(8):
                    S.op(S.pe, lambda: nc.tensor.matmul(p[:], lhsT=W1[:, c, fc * 128:(fc + 1) * 128], rhs=h2T[hi][:, c, :],
                                                        start=(c == 0), stop=(c == 7)),
                         reads=[b_W1[fc // 8], b_h2T[hi]], writes=[bp], inc=(c == 7))
                ri = fc % 2
                S.op(S.act, lambda: nc.scalar.activation(out=rr[ri][:], in_=p[:], func=AF.Relu), reads=[bp], writes=[b_rr[ri]])
                S.op(S.dve, lambda: nc.vector.tensor_tensor(out=aT[:, fc, :], in0=p[:], in1=rr[ri][:], op=ALU.mult),
                     reads=[bp, b_rr[ri]], writes=[b_aT[fc]])
            for j in range(4):
                tb = tt * 4 + j
                xi = nx % 2
                nx += 1
                S.dma(S.sp, sem_x[xi], x1b[xi][:], k.x1_scr[tb * 128:(tb + 1) * 128, :], writes=[b_x1[xi]])
                for hf in range(2):
                    p, bp = k.pf[npf % 6], k.b_pf[npf % 6]
                    npf += 1
                    for fc in range(32):
                        S.op(S.pe, lambda: nc.tensor.matmul(p[:], lhsT=aT[:, fc, j * 128:(j + 1) * 128],
                                                            rhs=W2[:, fc, hf * 512:(hf + 1) * 512],
                                                            start=(fc == 0), stop=(fc == 31)),
                             reads=[b_W2[fc // 8], b_aT[fc]], writes=[bp], inc=(fc == 31))
                    S.op(S.dve, lambda: nc.vector.tensor_tensor(out=x1b[xi][:, hf * 512:(hf + 1) * 512], in0=p[:],
                                                                in1=x1b[xi][:, hf * 512:(hf + 1) * 512], op=ALU.add),
                         reads=[bp, b_x1[xi]], writes=[b_x1[xi]])
                S.dma(S.pool, sem_o[xi], k.out[tb * 128:(tb + 1) * 128, :], x1b[xi][:], reads=[b_x1[xi]], is_output=True)
        S.barrier()


def ml_prep(k, st, T=None, stp=None, phase=0):
    nc, S = k.nc, k.S
    if phase == 1:
        return _ml_prep_compute(k, T, stp)
    T = K()
    T.tab = k.sb("ml_tab", [128, 3, 32, 4], F32, st)
    T.Mb = k.sb("ml_Mb", [128, 33, 4], F32, st)
    T.uT = k.sb("ml_u", [128, 128], F32, st)
    T.wT = k.sb("ml_w", [128, 128], F32, st)
    T.flT = k.sb("ml_fl", [128, 128], F32, st)
    T.decT = k.sb("ml_dec", [128, 128], F32, st)
    nl16 = k.sb("ml_nl16", [128, 1], F32, st)
    tmp = k.sb("ml_tmp", [128, 128], F32, st)
    tab, Mb = T.tab, T.Mb
    b_tab, b_Mb, b_c, b_tmp = Buf(), Buf(), Buf(), Buf()
    T.b_u, T.b_w, T.b_fl, T.b_dec = Buf(), Buf(), Buf(), Buf()
    T.nl16, T.tmp, T.b_tab, T.b_Mb, T.b_c, T.b_tmp = nl16, tmp, b_tab, b_Mb, b_c, b_tmp
    return T


def _ml_prep_compute(k, T, st2):
    nc, S = k.nc, k.S
    tab, Mb, nl16, tmp = T.tab, T.Mb, T.nl16, T.tmp
    b_tab, b_Mb, b_c, b_tmp = T.b_tab, T.b_Mb, T.b_c, T.b_tmp
    S.op(S.dve, lambda: nc.vector.memset(nl16[:], -LN16), writes=[b_c])
    S.op(S.dve, lambda: nc.vector.memset(Mb[:, 0, :], 0.0), writes=[b_Mb])
    if True:
        fp = k.sb("ml_fp", [4, S_LEN], F32, st2)
        ip = k.sb("ml_ip", [4, S_LEN], F32, st2)
        Fn = k.sb("ml_Fn", [4, S_LEN], F32, st2)
        on = k.sb("ml_on", [4, S_LEN], F32, st2)
        b_fp, b_ip, b_Fn, b_on = Buf(), Buf(), Buf(), Buf()
        sg = S.new_sem("d_mlg")
        S.dma(S.sp, sg, fp[:], k.gif[4:8, :], writes=[b_fp])
        S.dma(S.sp, sg, ip[:], k.gif[0:4, :], writes=[b_ip])
        S.op(S.dve, lambda: nc.vector.memset(on[:], 1.0), writes=[b_on])
        S.op(S.act, lambda: nc.scalar.activation(out=fp[:], in_=fp[:], func=AF.Exp, scale=-1.0), reads=[b_fp], writes=[b_fp])
        S.op(S.act, lambda: nc.scalar.activation(out=fp[:], in_=fp[:], func=AF.Ln, bias=1.0), reads=[b_fp], writes=[b_fp])
        S.op(S.dve, lambda: nc.vector.tensor_tensor_scan(out=Fn[:], data0=on[:], data1=fp[:], initial=0.0,
                                                         op0=ALU.mult, op1=ALU.add),
             reads=[b_on, b_fp], writes=[b_Fn])
        S.op(S.dve, lambda: nc.vector.tensor_tensor(out=ip[:], in0=ip[:], in1=Fn[:], op=ALU.add),
             reads=[b_ip, b_Fn], writes=[b_ip])
        S.op(S.dve, lambda: nc.vector.tensor_tensor_scan(out=fp[:], data0=ip[:], data1=ip[:], initial=0.0,
                                                         op0=ALU.max, op1=ALU.max),
             reads=[b_ip], writes=[b_fp])
        pt, bpt = k.pf[0], k.b_pf[0]
        ptv = pt[:, 0:384].rearrange("p (q c h) -> p q c h", q=3, c=32)
        idf = k.ppv("ident")
        for q, (X, bX) in enumerate(((Fn, b_Fn), (ip, b_ip), (fp, b_fp))):
            for c in range(32):
                S.op(S.pe, lambda: nc.tensor.transpose(out=ptv[:, q, c, :], in_=X[0:4, c * 128:(c + 1) * 128],
                                                       identity=idf[0:4, 0:4]),
                     reads=[bX, k.b_pp], writes=[bpt], inc=(q == 2 and c == 31))
        S.op(S.dve, lambda: nc.vector.tensor_copy(out=tab[:], in_=ptv), reads=[bpt], writes=[b_tab])
    pm, bpm = k.pf[1], k.b_pf[1]
    o_sel = PP["sel127"][0]
    S.op(S.pe, lambda: nc.tensor.matmul(pm[:, 0:128], lhsT=k.ppt[:, o_sel:o_sel + 128],
                                        rhs=tab[:, 2].rearrange("p c h -> p (c h)"), start=True, stop=True),
         reads=[b_tab, k.b_pp], writes=[bpm])
    S.op(S.dve, lambda: nc.vector.tensor_copy(out=Mb[:, 1:33, :], in_=pm[:, 0:128].rearrange("p (c h) -> p c h", c=32)),
         reads=[bpm], writes=[b_Mb])

    def table(dst, bd, in0, in1, bias):
        S.op(S.dve, lambda: nc.vector.tensor_tensor(out=tmp[:].rearrange("p (c h) -> p c h", c=32), in0=in0, in1=in1, op=ALU.subtract),
             reads=[b_tab, b_Mb], writes=[b_tmp])
        if bias:
            S.op(S.act, lambda: nc.scalar.activation(out=dst[:], in_=tmp[:], func=AF.Exp, bias=nl16[:, 0:1]),
                 reads=[b_tmp, b_c], writes=[bd])
        else:
            S.op(S.act, lambda: nc.scalar.activation(out=dst[:], in_=tmp[:], func=AF.Exp), reads=[b_tmp], writes=[bd])
    table(T.uT, T.b_u, tab[:, 1], Mb[:, 0:32, :], True)
    table(T.wT, T.b_w, tab[:, 1], Mb[:, 1:33, :], True)
    table(T.flT, T.b_fl, tab[:, 0], Mb[:, 0:32, :], False)
    table(T.decT, T.b_dec, Mb[:, 0:32, :], Mb[:, 1:33, :], False)
    return T


def xa_prep(k, st, X=None, st2=None, phase=0):
    nc, S = k.nc, k.S
    if phase == 0:
        X = K()
        X.knT = k.sb("xa_knT", [128, 4, 2, N_MEM], BF16, st)
        X.vmem = k.sb("xa_vmem", [128, 2, D], BF16, st)
        X.b_knT, X.b_vmem = Buf(), Buf()
        return X
    if phase == 2:
        return X.compute()
    knT, vmem = X.knT, X.vmem
    if True:
        mt = k.sb("xa_mt", [128, 2, D], F32, st2)
        mn = k.sb("xa_mn", [128, D], BF16, st2)
        jk = k.sb("xa_jk", [128, D], BF16, st2)
        memT = k.sb("xa_memT", [128, 8, N_MEM], BF16, st2)
        kn = k.sb("xa_kn", [128, 256], BF16, st2)
        sm = k.sb("xa_sm", [128, 64], F32, st2)
        wkv = [k.sb(f"xa_w{i}", [128, 8, 512], BF16, st2) for i in range(4)]
        b_mt, b_mn, b_jk, b_memT, b_kn = Buf(), Buf(), Buf(), Buf(), Buf()
        b_w = [Buf() for _ in range(4)]
        sem = S.new_sem("d_xa0")
        sem_w = [S.new_sem(f"d_xaw{i}") for i in range(4)]
        S.dma(S.sp, sem, mt[:], k.mem.rearrange("(j p) d -> p j d", p=128), writes=[b_mt])
        for i in range(4):
            S.dma(S.pool, sem_w[i], wkv[i][:], k.w_mem_kv[:, i * 512:(i + 1) * 512].rearrange("(c p) n -> p c n", p=128),
                  writes=[b_w[i]])
        nsm = [0]

        def smcol():
            c = nsm[0]
            nsm[0] += 1
            return sm[:, c:c + 1], Buf()

        def compute():
            _xa_compute()
        X.compute = compute

    def _xa_compute():
        def rstd_of(src_ap, src_bufs, n):
            ss, bss = smcol()
            rm, brm = smcol()
            rs, brs = smcol()
            S.op(S.act, lambda: nc.scalar.activation(out=jk[:, 0:n], in_=src_ap, func=AF.Square, accum_out=ss),
                 reads=src_bufs, writes=[b_jk, bss])
            S.op(S.act, lambda: nc.scalar.activation(out=rm, in_=ss, func=AF.Ln, bias=k.eps_t[:, 0:1], scale=1.0 / n),
                 reads=[bss, k.b_eps], writes=[brm])
            S.op(S.act, lambda: nc.scalar.activation(out=rs, in_=rm, func=AF.Exp, scale=-0.5), reads=[brm], writes=[brs])
            return rs, brs

        gmem = k.ppv("gmem")
        for j in range(2):
            rs, brs = rstd_of(mt[:, j, :], [b_mt], D)
            S.op(S.dve, lambda: nc.vector.scalar_tensor_tensor(out=mn[:], in0=mt[:, j, :], scalar=rs, in1=gmem,
                                                               op0=ALU.mult, op1=ALU.mult),
                 reads=[b_mt, brs, k.b_pp], writes=[b_mn])
            pb, bpb = k.pb[j], k.b_pb[j]
            for c in range(8):
                S.op(S.pe, lambda: nc.tensor.transpose(out=pb[:, c * 128:(c + 1) * 128], in_=mn[:, c * 128:(c + 1) * 128],
                                                       identity=k.ident_bf),
                     reads=[b_mn, k.b_cbf], writes=[bpb], inc=(c == 7))
            S.op(S.act, lambda: nc.scalar.copy(out=memT[:, :, j * 128:(j + 1) * 128], in_=pb[:].rearrange("p (c t) -> p c t", c=8)),
                 reads=[bpb], writes=[b_memT])
        kng = k.ppv("kng")
        npf = [0]
        for j in range(2):
            for t4 in range(4):
                p, bp = k.pf[npf[0] % 6], k.b_pf[npf[0] % 6]
                npf[0] += 1
                for c in range(8):
                    S.op(S.pe, lambda: nc.tensor.matmul(p[:], lhsT=memT[:, c, j * 128:(j + 1) * 128], rhs=wkv[t4][:, c, :],
                                                        start=(c == 0), stop=(c == 7)),
                         reads=[b_memT, b_w[t4]], writes=[bp], inc=(c == 7))
                if t4 < 2:
                    for hh in range(2):
                        h = t4 * 2 + hh
                        rs, brs = rstd_of(p[:, hh * 256:(hh + 1) * 256], [bp], 256)
                        S.op(S.dve, lambda: nc.vector.scalar_tensor_tensor(out=kn[:], in0=p[:, hh * 256:(hh + 1) * 256], scalar=rs,
                                                                           in1=kng, op0=ALU.mult, op1=ALU.mult),
                             reads=[bp, brs, k.b_pp], writes=[b_kn])
                        pb, bpb = k.pb[hh], k.b_pb[hh]
                        for dc in range(2):
                            S.op(S.pe, lambda: nc.tensor.transpose(out=pb[:, dc * 128:(dc + 1) * 128], in_=kn[:, dc * 128:(dc + 1) * 128],
                                                                   identity=k.ident_bf),
                                 reads=[b_kn, k.b_cbf], writes=[bpb], inc=(dc == 1))
                        S.op(S.act, lambda: nc.scalar.copy(out=knT[:, h, :, j * 128:(j + 1) * 128],
                                                           in_=pb[:, 0:256].rearrange("p (c t) -> p c t", c=2)),
                             reads=[bpb], writes=[X.b_knT])
                else:
                    S.op(S.dve, lambda: nc.vector.tensor_copy(out=vmem[:, j, (t4 - 2) * 512:(t4 - 1) * 512], in_=p[:]),
                         reads=[bp], writes=[X.b_vmem])
    return X


def gen_sb(k, st, zz, b_z, oo, b_o):
    nc, S = k.nc, k.S
    NS = 2
    tiles_of = [[7, 4, 3, 0], [6, 5, 2, 1]]
    qT = [k.sb(f"sb_q{i}", [128, S_LEN], BF16, st) for i in range(2)]
    kT = [k.sb(f"sb_k{i}", [128, S_LEN], BF16, st) for i in range(2)]
    vv = [k.sb(f"sb_v{i}", [128, NTB, 128], BF16, st) for i in range(2)]
    b_q = [Buf() for _ in range(2)]
    b_k = [Buf() for _ in range(2)]
    b_v = [Buf() for _ in range(2)]
    sem_in = [S.new_sem(f"d_sbin{i}") for i in range(2)]
    yT = [k.sb(f"sb_y{i}", [128, S_LEN], BF16, st) for i in range(2)]
    b_y = [Buf() for _ in range(2)]
    sem_y = [S.new_sem(f"d_sby{i}") for i in range(2)]
    e_sb = k.sb("sb_e", [128, NS, 512], F32, st)
    b_e = [Buf() for s in range(NS)]
    nl = [k.sb(f"sb_nl{j}", [128, NS, 512], BF16, st) for j in range(2)]
    b_nl = [[Buf() for s in range(NS)] for j in range(2)]
    at = [k.sb(f"sb_at{j}", [128, NS, 512], BF16, st) for j in range(2)]
    b_at = [[Buf() for s in range(NS)] for j in range(2)]
    sacc = k.sb("sb_sacc", [128, NS, 512], BF16, st)
    b_sacc = [Buf() for s in range(NS)]

    def load_head(h):
        i = h % 2
        S.dma(S.sp, sem_in[i], qT[i][:], k.qt_sb[h], writes=[b_q[i]])
        S.dma(S.sp, sem_in[i], kT[i][:], k.kt_sb[h], writes=[b_k[i]])
        S.dma(S.sp, sem_in[i], vv[i][:], k.v_sb[:, h * 128:(h + 1) * 128].rearrange("(j p) n -> p j n", p=128),
              writes=[b_v[i]])

    def dummies(n):
        for _ in range(n):
            nc.tensor.matmul(k.dummy_bank[:], lhsT=k.ident_bf, rhs=k.sbmask_bf[:, 0:512], start=True, stop=True)

    def mm1(h, s, qi, kb):
        i = h % 2
        c0 = 128 * max(kb - 4 * qi, 0)
        S.op(S.pe, lambda: nc.tensor.matmul(zz[:, s, c0:512], lhsT=kT[i][:, kb * 128:(kb + 1) * 128],
                                            rhs=qT[i][:, qi * 512 + c0:(qi + 1) * 512], start=True, stop=True),
             reads=[b_k[i], b_q[i]], writes=[b_z[s]])

    load_head(0)
    for h in range(8):
        if h + 1 < 8:
            load_head(h + 1)
        i = h % 2
        steps = [[(qi, kb) for qi in tiles_of[s] for kb in range(4 * qi + 3, -1, -1)] for s in range(NS)]
        nsteps = len(steps[0])
        assert all(len(x) == nsteps for x in steps)
        for s in range(NS):
            mm1(h, s, *steps[s][0])
        for n in range(nsteps):
            par = n % 2
            info = []
            for s in range(NS):
                qi, kb = steps[s][n]
                info.append((qi, kb, kb - 4 * qi, kb == 4 * qi + 3, kb == 0))
            cs = [slice(128 * max(info[s][2], 0), 512) for s in range(NS)]
            for s in range(NS):
                qi, kb, r, first, last = info[s]
                S.op(S.act, lambda: nc.scalar.activation(out=e_sb[:, s, cs[s]], in_=zz[:, s, cs[s]], func=AF.Exp),
                     reads=[b_z[s]], writes=[b_e[s]])
                if not first:
                    S.op(S.pe, lambda: nc.tensor.matmul(zz[:, s, cs[s]], lhsT=k.nones_bf, rhs=sacc[:, s, cs[s]], start=False, stop=False,
                                                        skip_group_check=True),
                         reads=[b_sacc[s], k.b_cbf], writes=[b_z[s]], inc=False)
            for s in range(NS):
                S.op(S.act, lambda: nc.scalar.activation(out=nl[par][:, s, cs[s]], in_=e_sb[:, s, cs[s]], func=AF.Ln, bias=1.0),
                     reads=[b_e[s]], writes=[b_nl[par][s]])
            yield
            for s in range(NS):
                qi, kb, r, first, last = info[s]
                if r >= 0:
                    S.op(S.dve, lambda: nc.vector.tensor_tensor(out=nl[par][:, s, cs[s]], in0=nl[par][:, s, cs[s]],
                                                                in1=k.sbmask_bf[:, r * 512 + cs[s].start:(r + 1) * 512], op=ALU.mult),
                         reads=[b_nl[par][s], k.b_cbf], writes=[b_nl[par][s]])
            for s in range(NS):
                qi, kb, r, first, last = info[s]
                S.op(S.pe, lambda: nc.tensor.matmul(zz[:, s, cs[s]], lhsT=k.ntri_bf, rhs=nl[par][:, s, cs[s]], start=False, stop=True,
                                                    skip_group_check=True),
                     reads=[b_nl[par][s], k.b_cbf], writes=[b_z[s]])
                dummies(k.ndummy)
            yield
            for s in range(NS):
                S.op(S.act, lambda: nc.scalar.activation(out=at[par][:, s, cs[s]], in_=zz[:, s, cs[s]], func=AF.Exp),
                     reads=[b_z[s]], writes=[b_at[par][s]])
            for s in range(NS):
                qi, kb, r, first, last = info[s]
                if not last:
                    if first:
                        S.op(S.pool, lambda: nc.gpsimd.memset(sacc[:, s, 0:384], 0.0), writes=[b_sacc[s]])
                        S.op(S.pool, lambda: nc.gpsimd.tensor_copy(out=sacc[:, s, cs[s]], in_=nl[par][:, s, cs[s]]),
                             reads=[b_nl[par][s]], writes=[b_sacc[s]])
                    else:
                        S.op(S.pool, lambda: nc.gpsimd.tensor_tensor(out=sacc[:, s, cs[s]], in0=sacc[:, s, cs[s]], in1=nl[par][:, s, cs[s]],
                                                                    op=ALU.add),
                             reads=[b_nl[par][s], b_sacc[s]], writes=[b_sacc[s]])
            yield
            for s in range(NS):
                qi, kb, r, first, last = info[s]
                if n + 1 < nsteps:
                    mm1(h, s, *steps[s][n + 1])
                if r >= 0:
                    S.op(S.dve, lambda: nc.vector.tensor_tensor(out=at[par][:, s, cs[s]], in0=at[par][:, s, cs[s]],
                                                                in1=k.sbmask_bf[:, r * 512 + cs[s].start:(r + 1) * 512], op=ALU.mult),
                         reads=[b_at[par][s], k.b_cbf], writes=[b_at[par][s]])
                S.op(S.pe, lambda: nc.tensor.matmul(oo[:, s, cs[s]], lhsT=vv[i][:, kb, :], rhs=at[par][:, s, cs[s]],
                                                    start=first, stop=last, skip_group_check=True),
                     reads=[b_v[i], b_at[par][s]], writes=[b_o[s]])
                dummies(k.ndummy + 1)
                if last:
                    S.op(S.dve, lambda: nc.vector.tensor_copy(out=yT[i][:, qi * 512:(qi + 1) * 512], in_=oo[:, s, :]),
                         reads=[b_o[s]], writes=[b_y[i]])
        S.dma(S.sp, sem_y[i], k.yt_sb[h], yT[i][:], reads=[b_y[i]])


def gen_ml(k, st, T, bankA, bbA, bankB, bbB, bankN, bbN, pbT, bpbT):
    nc, S = k.nc, k.S
    qTt = [k.sb(f"ml_q{i}", [128, 2, 512], BF16, st) for i in range(2)]
    kTt = [k.sb(f"ml_k{i}", [128, 2, 512], BF16, st) for i in range(2)]
    ktm = [k.sb(f"ml_kt{i}", [128, 4, 256], BF16, st) for i in range(2)]
    vtm = [k.sb(f"ml_vt{i}", [128, 4, 256], BF16, st) for i in range(2)]
    otm = [k.sb(f"ml_ot{i}", [128, 4, 256], BF16, st) for i in range(2)]
    b_in = [Buf() for i in range(2)]
    sem_in = [S.new_sem(f"d_mlin{i}") for i in range(2)]
    yTt = [k.sb(f"ml_y{i}", [128, 2, 512], BF16, st) for i in range(2)]
    b_y = [Buf() for i in range(2)]
    sem_y = [S.new_sem(f"d_mly{i}") for i in range(2)]
    PT = [k.sb(f"ml_PT{i}", [128, 128], BF16, st) for i in range(2)]
    vu = [k.sb(f"ml_vu{i}", [128, 264], BF16, st) for i in range(2)]
    vw = [k.sb(f"ml_vw{i}", [128, 264], BF16, st) for i in range(2)]
    go = [k.sb(f"ml_go{i}", [128, 256], F32, st) for i in range(2)]
    b_PT = [Buf() for i in range(2)]
    b_vu = [Buf() for i in range(2)]
    b_vw = [Buf() for i in range(2)]
    b_go = [Buf() for i in range(2)]
    C32 = k.sb("ml_C32", [128, 2, 264], F32, st)
    Cbf = [k.sb(f"ml_Cbf{i}", [128, 2, 264], BF16, st) for i in range(2)]
    b_C32 = Buf()
    b_Cbf = [Buf() for i in range(2)]
    yb = [k.sb(f"ml_yb{i}", [128, 256], BF16, st) for i in range(2)]
    jk = k.sb("ml_jk", [128, 256], BF16, st)
    sm = [k.sb(f"ml_sm{i}", [128, 8], F32, st) for i in range(2)]
    b_yb = [Buf() for i in range(2)]
    b_jk = Buf()
    b_sm = [[Buf() for _ in range(8)] for i in range(2)]
    mlng = k.ppv("mlng")

    def load_group(h, g, gi):
        sm_, bb = sem_in[gi], [b_in[gi]]
        cols = slice(g * 512, (g + 1) * 512)
        S.dma(S.sp, sm_, qTt[gi][:], k.qt_ml[2 * h:2 * h + 2, :, cols].rearrange("c p t -> p c t"), writes=bb)
        S.dma(S.sp, sm_, kTt[gi][:], k.kt_ml[2 * h:2 * h + 2, :, cols].rearrange("c p t -> p c t"), writes=bb)
        for dst, src in ((ktm, k.k_ml), (vtm, k.v_ml), (otm, k.o_ml)):
            S.dma(S.sp, sm_, dst[gi][:], src[cols, h * 256:(h + 1) * 256].rearrange("(j p) n -> p j n", p=128), writes=bb)

    ng = 0
    load_group(0, 0, 0)
    for h in range(4):
        for g in range(8):
            gi = ng % 2
            ng += 1
            if g + 1 < 8:
                load_group(h, g + 1, ng % 2)
            elif h + 1 < 4:
                load_group(h + 1, 0, ng % 2)
            for j in range(4):
                c = g * 4 + j
                par = c % 2
                col = c * 4 + h
                bin_ = b_in[gi]
                tsl = slice(j * 128, (j + 1) * 128)
                uc = T.uT[:, col:col + 1]
                wc = T.wT[:, col:col + 1]
                flc = T.flT[:, col:col + 1]
                dcc = T.decT[:, col:col + 1]
                smt = sm[par]
                bsm = b_sm[par]
                ps_s = bankA[:, 264:392]
                ps_c = [bankA[:, 0:257], bankB[:, 0:257]]
                bbc = [bbA, bbB]
                ps_n = bankN[:, 0:257]
                for dc in range(2):
                    S.op(S.pe, lambda: nc.tensor.matmul(ps_s, lhsT=kTt[gi][:, dc, tsl], rhs=qTt[gi][:, dc, tsl],
                                                        start=(dc == 0), stop=(dc == 1)),
                         reads=[bin_], writes=[bbA], inc=(dc == 1))
                for (dst, bd, sc, bs) in ((vu[par], b_vu[par], uc, T.b_u), (vw[par], b_vw[par], wc, T.b_w)):
                    S.op(S.dve, lambda: nc.vector.tensor_scalar(out=dst[:, 0:256], in0=vtm[gi][:, j, :], scalar1=sc, scalar2=None,
                                                                op0=ALU.mult),
                         reads=[bin_, bs], writes=[bd])
                    S.op(S.dve, lambda: nc.vector.tensor_copy(out=dst[:, 256:257], in_=sc), reads=[bs], writes=[bd])
                S.op(S.pool, lambda: nc.gpsimd.tensor_tensor(out=go[par][:], in0=otm[gi][:, j, :], in1=mlng[:, h * 256:(h + 1) * 256],
                                                            op=ALU.mult),
                     reads=[bin_, k.b_pp], writes=[b_go[par]])
                yield
                S.op(S.dve, lambda: nc.vector.tensor_tensor(out=PT[par][:], in0=ps_s, in1=k.mlmask_bf, op=ALU.mult),
                     reads=[bbA, k.b_cbf], writes=[b_PT[par]])
                for dc in range(2):
                    S.op(S.pe, lambda: nc.tensor.matmul(ps_c[dc], lhsT=ktm[gi][:, j, dc * 128:(dc + 1) * 128],
                                                        rhs=vw[par][:, 0:257], start=True, stop=True),
                         reads=[bin_, b_vw[par]], writes=[bbc[dc]])
                yield
                S.op(S.pe, lambda: nc.tensor.matmul(ps_n, lhsT=PT[par][:], rhs=vu[par][:, 0:257], start=True, stop=(c == 0)),
                     reads=[b_PT[par], b_vu[par]], writes=[bbN], inc=(c == 0))
                if c > 0:
                    cb, bcb = Cbf[1 - par], b_Cbf[1 - par]
                    for dc in range(2):
                        S.op(S.pe, lambda: nc.tensor.matmul(ps_n, lhsT=qTt[gi][:, dc, tsl], rhs=cb[:, dc, 0:257],
                                                            start=False, stop=(dc == 1)),
                             reads=[bin_, bcb], writes=[bbN], inc=(dc == 1))
                for dc in range(2):
                    if c == 0:
                        S.op(S.dve, lambda: nc.vector.tensor_copy(out=C32[:, dc, 0:257], in_=ps_c[dc]),
                             reads=[bbc[dc]], writes=[b_C32])
                    else:
                        S.op(S.dve, lambda: nc.vector.scalar_tensor_tensor(out=C32[:, dc, 0:257], in0=C32[:, dc, 0:257],
                                                                           scalar=dcc, in1=ps_c[dc], op0=ALU.mult, op1=ALU.add),
                             reads=[bbc[dc], b_C32, T.b_dec], writes=[b_C32])
                yield
                S.op(S.pool, lambda: nc.gpsimd.tensor_copy(out=Cbf[par][:, :, 0:257], in_=C32[:, :, 0:257]),
                     reads=[b_C32], writes=[b_Cbf[par]])
                S.op(S.act, lambda: nc.scalar.activation(out=smt[:, 0:1], in_=bankN[:, 256:257], func=AF.Abs),
                     reads=[bbN], writes=[bsm[0]])
                S.op(S.act, lambda: nc.scalar.activation(out=jk[:], in_=bankN[:, 0:256], func=AF.Square, accum_out=smt[:, 1:2]),
                     reads=[bbN], writes=[b_jk, bsm[1]])
                yield
                S.op(S.dve, lambda: nc.vector.tensor_tensor(out=smt[:, 2:3], in0=smt[:, 0:1], in1=flc, op=ALU.max),
                     reads=[bsm[0], T.b_fl], writes=[bsm[2]])
                S.op(S.dve, lambda: nc.vector.tensor_scalar(out=smt[:, 3:4], in0=smt[:, 2:3], scalar1=smt[:, 2:3], scalar2=None,
                                                            op0=ALU.mult),
                     reads=[bsm[2]], writes=[bsm[3]])
                yield
                S.op(S.dve, lambda: nc.vector.tensor_scalar(out=smt[:, 3:4], in0=smt[:, 3:4], scalar1=EPS, scalar2=None, op0=ALU.mult),
                     reads=[bsm[3]], writes=[bsm[3]])
                S.op(S.dve, lambda: nc.vector.scalar_tensor_tensor(out=smt[:, 4:5], in0=smt[:, 1:2], scalar=1.0 / 256,
                                                                   in1=smt[:, 3:4], op0=ALU.mult, op1=ALU.add),
                     reads=[bsm[1], bsm[3]], writes=[bsm[4]])
                yield
                S.op(S.act, lambda: nc.scalar.activation(out=smt[:, 5:6], in_=smt[:, 4:5], func=AF.Ln), reads=[bsm[4]], writes=[bsm[5]])
                yield
                S.op(S.act, lambda: nc.scalar.activation(out=smt[:, 6:7], in_=smt[:, 5:6], func=AF.Exp, scale=-0.5),
                     reads=[bsm[5]], writes=[bsm[6]])
                yield
                S.op(S.dve, lambda: nc.vector.scalar_tensor_tensor(out=yb[par][:], in0=bankN[:, 0:256], scalar=smt[:, 6:7],
                                                                   in1=go[par][:], op0=ALU.mult, op1=ALU.mult),
                     reads=[bbN, bsm[6], b_go[par]], writes=[b_yb[par]])
                yield
                for dc in range(2):
                    S.op(S.pe, lambda: nc.tensor.transpose(out=pbT[:, dc * 128:(dc + 1) * 128], in_=yb[par][:, dc * 128:(dc + 1) * 128],
                                                           identity=k.ident_bf),
                         reads=[b_yb[par], k.b_cbf], writes=[bpbT], inc=(dc == 1))
                yield
                S.op(S.dve, lambda: nc.vector.tensor_copy(out=yTt[gi][:, :, tsl], in_=pbT[:, 0:256].rearrange("p (c t) -> p c t", c=2)),
                     reads=[bpbT], writes=[b_y[gi]])
                yield
            S.dma(S.sp, sem_y[gi], k.yt_ml[2 * h:2 * h + 2, :, g * 512:(g + 1) * 512].rearrange("c p t -> p c t"),
                  yTt[gi][:], reads=[b_y[gi]])


def gen_xa(k, st, X, banks, bbanks):
    nc, S = k.nc, k.S
    qx = [k.sb(f"xa_qx{i}", [128, 2, 512], BF16, st) for i in range(2)]
    b_qx = [Buf(), Buf()]
    sem_q = [S.new_sem("d_xaq0"), S.new_sem("d_xaq1")]
    yx = [k.sb(f"xa_yx{i}", [128, 2, 512], BF16, st) for i in range(2)]
    b_yx = [Buf(), Buf()]
    sem_y = [S.new_sem("d_xay0"), S.new_sem("d_xay1")]
    sq = k.sb("xa_sq", [128, 2, 512], BF16, st)
    rq = k.sb("xa_rq", [128, 512], F32, st)
    rq2 = k.sb("xa_rq2", [128, 512], F32, st)
    qn = k.sb("xa_qn", [128, 2, 512], BF16, st)
    pT = k.sb("xa_pT", [128, 2, 512], BF16, st)
    rden = k.sb("xa_rden", [128, 512], F32, st)
    b_sq, b_rq, b_rq2, b_qn, b_pT, b_rden = Buf(), Buf(), Buf(), Buf(), Buf(), Buf()
    qng = k.ppv("qng")
    knT, vmem = X.knT, X.vmem
    (pa, pb_, pc), (ba, bb_, bc) = banks, bbanks
    its = [(h, tt) for h in range(4) for tt in range(NTT)]

    def load(n):
        h, tt = its[n]
        S.dma(S.sp, sem_q[n % 2], qx[n % 2][:], k.qt_x[2 * h:2 * h + 2, :, tt * 512:(tt + 1) * 512].rearrange("c p t -> p c t"),
              writes=[b_qx[n % 2]])
    load(0)
    for n, (h, tt) in enumerate(its):
        i = n % 2
        cols = slice(tt * 512, (tt + 1) * 512)
        if n + 1 < len(its):
            load(n + 1)
        S.op(S.dve, lambda: nc.vector.tensor_tensor(out=sq[:], in0=qx[i][:], in1=qx[i][:], op=ALU.mult), reads=[b_qx[i]], writes=[b_sq])
        yield
        for dc in range(2):
            S.op(S.pe, lambda: nc.tensor.matmul(pa[:], lhsT=k.ones_bf, rhs=sq[:, dc, :], start=(dc == 0), stop=(dc == 1)),
                 reads=[b_sq, k.b_cbf], writes=[ba], inc=(dc == 1))
        yield
        S.op(S.act, lambda: nc.scalar.activation(out=rq[:], in_=pa[:], func=AF.Ln, bias=k.eps_t[:, 0:1], scale=1.0 / 256),
             reads=[ba, k.b_eps], writes=[b_rq])
        yield
        S.op(S.act, lambda: nc.scalar.activation(out=rq2[:], in_=rq[:], func=AF.Exp, scale=-0.5), reads=[b_rq], writes=[b_rq2])
        yield
        for dc in range(2):
            S.op(S.dve, lambda: nc.vector.scalar_tensor_tensor(out=qn[:, dc, :], in0=qx[i][:, dc, :], scalar=qng[:, dc:dc + 1],
                                                               in1=rq2[:], op0=ALU.mult, op1=ALU.mult),
                 reads=[b_qx[i], b_rq2, k.b_pp], writes=[b_qn])
        yield
        for mc, (p, bp) in enumerate(((pa, ba), (pb_, bb_))):
            for dc in range(2):
                S.op(S.pe, lambda: nc.tensor.matmul(p[:], lhsT=knT[:, h, dc, mc * 128:(mc + 1) * 128], rhs=qn[:, dc, :],
                                                    start=(dc == 0), stop=(dc == 1)),
                     reads=[X.b_knT, b_qn], writes=[bp], inc=(dc == 1))
        yield
        for mc, (p, bp) in enumerate(((pa, ba), (pb_, bb_))):
            S.op(S.act, lambda: nc.scalar.activation(out=pT[:, mc, :], in_=p[:], func=AF.Exp, scale=1.0 / 16),
                 reads=[bp], writes=[b_pT])
        yield
        for mc in range(2):
            S.op(S.pe, lambda: nc.tensor.matmul(pc[:], lhsT=k.ones_bf, rhs=pT[:, mc, :], start=(mc == 0), stop=(mc == 1)),
                 reads=[b_pT, k.b_cbf], writes=[bc], inc=(mc == 1))
        for dvc, (p, bp) in enumerate(((pa, ba), (pb_, bb_))):
            for mc in range(2):
                S.op(S.pe, lambda: nc.tensor.matmul(p[:], lhsT=vmem[:, mc, h * 256 + dvc * 128:h * 256 + (dvc + 1) * 128],
                                                    rhs=pT[:, mc, :], start=(mc == 0), stop=(mc == 1)),
                     reads=[X.b_vmem, b_pT], writes=[bp], inc=(mc == 1))
        yield
        S.op(S.dve, lambda: nc.vector.reciprocal(out=rden[:], in_=pc[:]), reads=[bc], writes=[b_rden])
        yield
        for dvc, (p, bp) in enumerate(((pa, ba), (pb_, bb_))):
            S.op(S.dve, lambda: nc.vector.tensor_tensor(out=yx[i][:, dvc, :], in0=p[:], in1=rden[:], op=ALU.mult),
                 reads=[bp, b_rden], writes=[b_yx[i]])
        S.dma(S.sp, sem_y[i], k.yt_x[2 * h:2 * h + 2, :, cols].rearrange("c p t -> p c t"), yx[i][:], reads=[b_yx[i]])
        yield


def stage_mid(k, bg_per_yield=1):
    nc, S = k.nc, k.S
    with ExitStack() as st:
        T = ml_prep(k, st)
        X = xa_prep(k, st)
        with ExitStack() as stp:
            xa_prep(k, st, X, stp, phase=1)
            ml_prep(k, st, T, stp, phase=1)
            xa_prep(k, st, X, stp, phase=2)
            S.barrier()
        fg = gen_sb(k, st, k.zz, k.b_zz, k.oo, k.b_oo)
        bankN = k.pb[1][:].bitcast(F32)
        pb0f = k.pb[0][:].bitcast(F32)
        bankB = pb0f[:, 128:512]
        bgs = [gen_ml(k, st, T, k.pf[0], k.b_pf[0], bankB, k.b_pb[0], bankN, k.b_pb[1], k.pb[0], k.b_pb[0]),
               gen_xa(k, st, X, (k.pf[0], pb0f, bankN), (k.b_pf[0], k.b_pb[0], k.b_pb[1]))]
        k.dummy_bank = k.pf[1]
        bi = 0
        for _ in fg:
            for _r in range(bg_per_yield):
                while bi < len(bgs):
                    try:
                        next(bgs[bi])
                        break
                    except StopIteration:
                        bi += 1
        while bi < len(bgs):
            for _ in bgs[bi]:
                pass
            bi += 1
        S.barrier()


ALL_STAGES = ("s1", "s2", "mid", "4a", "4b")
_NC_CACHE = {}


def kernel(**inputs):
    inp = {k_: np.asarray(v) for k_, v in inputs.items()}
    if "nc" not in _NC_CACHE:
        _NC_CACHE["nc"] = build(dbg=(), stages=ALL_STAGES)
    nc = _NC_CACHE["nc"]
    pp = make_pp(inp)
    shared = {"pp": pp, "w_in": np.ascontiguousarray(inp["w_in"][0]), "w_mem_kv": np.ascontiguousarray(inp["w_mem_kv"][0]),
              "w_sb_proj": np.ascontiguousarray(inp["w_sb_proj"][0]), "w_ml_proj": np.ascontiguousarray(inp["w_ml_proj"][0]),
              "w_x_proj": np.ascontiguousarray(inp["w_x_proj"][0]), "w_out": np.ascontiguousarray(inp["w_out"][0]),
              "w_ff1": np.ascontiguousarray(inp["w_ff1"][0]), "w_ff2": np.ascontiguousarray(inp["w_ff2"][0])}
    in_maps = []
    for b in range(8):
        m = dict(shared)
        m["x"] = np.ascontiguousarray(inp["x"][b])
        m["mem"] = np.ascontiguousarray(inp["mem"][b])
        in_maps.append(m)
    res = run_bass_kernel_spmd(nc, in_maps, core_ids=list(range(8)))
    return np.stack([np.asarray(r["out"]) for r in res.results], axis=0).astype(np.float32)
```

```python
from contextlib import ExitStack
import numpy as np
import concourse.bass as bass
import concourse.mybir as mybir
from concourse.bass_utils import run_bass_kernel_spmd

F32 = mybir.dt.float32
BF16 = mybir.dt.bfloat16
AF = mybir.ActivationFunctionType
ALU = mybir.AluOpType
AX = mybir.AxisListType

S_LEN = 4096
D = 1024
NTB = 32
NTT = 8
N_IN = 11272
N_MEM = 256
EPS = 1e-6
LN16 = float(np.log(16.0))
NDUMMY = 1


class SemObj:
    def __init__(self, h, name):
        self.h = h
        self.name = name
        self.count = 0
        self.is_dma = name.startswith("d_")


class Buf:
    __slots__ = ("name", "w", "r", "psum")

    def __init__(self, name="", psum=False):
        self.name = name
        self.w = None
        self.r = []
        self.psum = psum


class Eng:
    def __init__(self, name, e, sem, inorder_self):
        self.name = name
        self.e = e
        self.sem = sem
        self.seen = {}
        self.inorder_self = inorder_self
        self.dangling = False


class Sched:
    def __init__(self, nc, stack, self_sync=True):
        self.nc = nc
        self.stack = stack
        self.sems = []
        mk = self.new_sem
        self.pe = Eng("pe", nc.tensor, mk("s_pe"), True)
        self.act = Eng("act", nc.scalar, mk("s_act"), not self_sync)
        self.dve = Eng("dve", nc.vector, mk("s_dve"), not self_sync)
        self.pool = Eng("pool", nc.gpsimd, mk("s_pool"), not self_sync)
        self.sp = Eng("sp", nc.sync, mk("s_sp"), True)
        self.engs = [self.pe, self.act, self.dve, self.pool, self.sp]
        self.out_events = []
        self.nops = 0
        self.trace = {e.name: [] for e in self.engs}

    def new_sem(self, name):
        h = self.stack.enter_context(self.nc.semaphore(name))
        s = SemObj(h, name)
        self.sems.append(s)
        return s

    def _waits(self, eng, reads, writes):
        need = {}

        def add(ev):
            if ev is None:
                return
            s, v = ev
            if need.get(s, 0) < v:
                need[s] = v
        for b in reads:
            add(b.w)
            if b.psum:
                for ev in b.r:
                    if ev[0] is not eng.sem:
                        add(ev)
        for b in writes:
            add(b.w)
            for ev in b.r:
                add(ev)
        for s, v in need.items():
            if s.is_dma:
                v = s.count
            if s is eng.sem and eng.inorder_self:
                continue
            if eng.seen.get(s, 0) >= v:
                continue
            eng.e.wait_ge(s.h, v)
            eng.seen[s] = v
            self.trace[eng.name].append(("w", s.name, v))

    def op(self, eng, emit, reads=(), writes=(), inc=True):
        self._waits(eng, reads, writes)
        ins = emit()
        self.nops += 1
        if inc:
            eng.sem.count += 1
            ins.then_inc(eng.sem.h, 1)
            ev = (eng.sem, eng.sem.count)
            eng.dangling = False
            self.trace[eng.name].append(("i", eng.sem.name, 1))
        else:
            ev = (eng.sem, eng.sem.count + 1)
            eng.dangling = True
        for b in writes:
            b.w = ev
            b.r = []
        for b in reads:
            if b.w is not ev and (not b.r or b.r[-1] != ev):
                b.r.append(ev)
        return ins

    def dma(self, q, sem, out, in_, reads=(), writes=(), is_output=False, **kw):
        self._waits(q, reads, writes)
        ins = q.e.dma_start(out=out, in_=in_, **kw)
        ins.then_inc(sem.h, 16)
        sem.count += 16
        self.trace[q.name].append(("i", sem.name, 16))
        ev = (sem, sem.count)
        for b in writes:
            b.w = ev
            b.r = []
        for b in reads:
            b.r.append(ev)
        if is_output:
            self.out_events.append(ev)
        self.nops += 1
        return ins

    def barrier(self):
        for e in self.engs:
            assert not e.dangling
        for e in self.engs:
            for s in self.sems:
                if s.count == 0:
                    continue
                if s is e.sem:
                    continue
                if e.seen.get(s, 0) >= s.count:
                    continue
                e.e.wait_ge(s.h, s.count)
                e.seen[s] = s.count
                self.trace[e.name].append(("w", s.name, s.count))

    def check_deadlock(self):
        vals = {}
        pos = {n: 0 for n in self.trace}
        progress = True
        while progress:
            progress = False
            for n, tr in self.trace.items():
                while pos[n] < len(tr):
                    kind, sn, v = tr[pos[n]]
                    if kind == "w":
                        if vals.get(sn, 0) < v:
                            break
                    else:
                        vals[sn] = vals.get(sn, 0) + v
                    pos[n] += 1
                    progress = True
        stuck = {n: (pos[n], self.trace[n][pos[n]]) for n in self.trace if pos[n] < len(self.trace[n])}
        return stuck

    def finish(self):
        need = {}
        for s, v in self.out_events:
            need[s] = max(need.get(s, 0), v)
        for s, v in need.items():
            self.sp.e.wait_ge(s.h, v)


PP = {}
_off = 0


def _pp(name, n):
    global _off
    PP[name] = (_off, n)
    _off += n


_pp("ident", 128)
_pp("ntri", 128)
_pp("nones", 128)
_pp("ones", 128)
_pp("mlmask", 128)
_pp("sbmask", 4 * 512)
_pp("sel127", 128)
_pp("gmix", 1024)
_pp("bgate", 24)
_pp("convw", 64)
_pp("convb", 16)
_pp("mlng", 1024)
_pp("gmem", 1024)
_pp("qng", 2)
_pp("kng", 256)
_pp("gmlp", 1024)
_pp("bif", 2)
_pp("sel4", 12)
NPP = _off


def make_pp(inp):
    pp = np.zeros((128, NPP), np.float32)

    def put(name, arr):
        o, n = PP[name]
        pp[:, o:o + n] = np.asarray(arr, np.float32).reshape(128, n)
    p = np.arange(128)
    put("ident", np.eye(128))
    put("ntri", -(p[:, None] >= p[None, :]).astype(np.float32))
    put("nones", -np.ones((128, 128)))
    put("ones", np.ones((128, 128)))
    c = np.arange(512)
    put("sbmask", np.stack([((128 * r + p[:, None]) < c[None, :]) for r in range(4)], 1).astype(np.float32))
    put("mlmask", (p[:, None] <= p[None, :]).astype(np.float32))
    put("sel127", np.repeat((p == 127).astype(np.float32)[:, None], 128, 1))
    put("gmix", np.broadcast_to(inp["g_mix"][0][None, :], (128, 1024)))
    put("bgate", inp["b_gate"][0].reshape(24, 128).T)
    put("convw", inp["conv_w"][0].reshape(4, 16, 128).transpose(2, 1, 0))
    put("convb", inp["conv_b"][0].reshape(16, 128).T)
    put("mlng", np.broadcast_to(inp["ml_norm_g"][0][None, :], (128, 1024)))
    put("gmem", np.broadcast_to(inp["g_mem"][0][None, :], (128, 1024)))
    put("qng", inp["q_norm_g"][0].reshape(2, 128).T)
    put("kng", np.broadcast_to(inp["k_norm_g"][0][None, :], (128, 256)))
    put("gmlp", np.broadcast_to(inp["g_mlp"][0][None, :], (128, 1024)))
    bif = np.zeros((128, 2), np.float32)
    bif[:4, 0] = inp["b_if"][0, :4]
    bif[:4, 1] = inp["b_if"][0, 4:]
    put("bif", bif)
    sel = np.zeros((128, 12), np.float32)
    for q in range(3):
        for h in range(4):
            sel[32 * q + h, 4 * q + h] = 1.0
    put("sel4", sel)
    return pp


C_SBQ, C_SBK, C_SBV = 0, 1024, 2048
C_MLQ, C_MLK, C_MLV, C_MLO = 3072, 4096, 5120, 6144
C_MLI, C_MLF, C_XQ, C_GATE = 7168, 7172, 7176, 8200


class K:
    pass


def build(dbg=(), stages=("s1", "s2")):
    nc = bass.Bass("TRN2", target_bir_lowering=False)
    k = K()
    k.nc = nc
    k.dbg = dbg
    k.ndummy = NDUMMY

    def din(name, shape, dt=F32):
        return nc.dram_tensor(name, shape, dt, kind="ExternalInput").ap()

    def dscr(name, shape, dt):
        kind = "ExternalOutput" if name in dbg else "Internal"
        return nc.dram_tensor(name, shape, dt, kind=kind).ap()

    k.x = din("x", [S_LEN, D])
    k.mem = din("mem", [N_MEM, D])
    k.pp = din("pp", [128, NPP])
    k.w_in = din("w_in", [D, N_IN])
    k.w_mem_kv = din("w_mem_kv", [D, 2048])
    k.w_sb = din("w_sb_proj", [D, D])
    k.w_ml = din("w_ml_proj", [D, D])
    k.w_x = din("w_x_proj", [D, D])
    k.w_out = din("w_out", [D, D])
    k.w_ff1 = din("w_ff1", [D, 4096])
    k.w_ff2 = din("w_ff2", [4096, D])
    k.out = nc.dram_tensor("out", [S_LEN, D], F32, kind="ExternalOutput").ap()

    k.qt_sb = dscr("qt_sb", [8, 128, S_LEN], BF16)
    k.kt_sb = dscr("kt_sb", [8, 128, S_LEN], BF16)
    k.v_sb = dscr("v_sb", [S_LEN, D], BF16)
    k.qt_ml = dscr("qt_ml", [8, 128, S_LEN], BF16)
    k.kt_ml = dscr("kt_ml", [8, 128, S_LEN], BF16)
    k.k_ml = dscr("k_ml", [S_LEN, D], BF16)
    k.v_ml = dscr("v_ml", [S_LEN, D], BF16)
    k.o_ml = dscr("o_ml", [S_LEN, D], BF16)
    k.qt_x = dscr("qt_x", [8, 128, S_LEN], BF16)
    k.gif = dscr("gif", [8, S_LEN], F32)
    k.g_scr = dscr("g_scr", [24, 128, S_LEN], BF16)
    k.yt_sb = dscr("yt_sb", [8, 128, S_LEN], BF16)
    k.yt_ml = dscr("yt_ml", [8, 128, S_LEN], BF16)
    k.yt_x = dscr("yt_x", [8, 128, S_LEN], BF16)

    k.x1_scr = dscr("x1_scr", [S_LEN, D], F32)
    k.h2t_scr = dscr("h2t_scr", [8, 128, S_LEN], BF16)

    with ExitStack() as st:
        S = Sched(nc, st)
        k.S = S
        k.st = st

        def sb(name, shape, dt, stack):
            return stack.enter_context(nc.sbuf_tensor(name, shape, dt))

        def ps(name, shape, dt, stack=st):
            return stack.enter_context(nc.psum_tensor(name, shape, dt))
        k.sb = sb
        k.ps = ps

        k.pf = [ps(f"pf{i}", [128, 512], F32) for i in range(2)]
        k.zz = ps("pzz", [128, 2, 512], F32)
        k.oo = ps("poo", [128, 2, 512], F32)
        k.pf += [k.zz[:, 0, :], k.zz[:, 1, :], k.oo[:, 0, :], k.oo[:, 1, :]]
        k.b_pf = [Buf(f"pf{i}", psum=True) for i in range(6)]
        k.b_zz = [k.b_pf[2], k.b_pf[3]]
        k.b_oo = [k.b_pf[4], k.b_pf[5]]
        k.pb = [ps(f"pb{i}", [128, 1024], BF16) for i in range(2)]
        k.b_pb = [Buf(f"pb{i}", psum=True) for i in range(2)]

        with ExitStack() as stA:
            k.ppt = sb("ppt", [128, NPP], F32, stA)
            k.b_pp = Buf("pp")
            k.sem_c = S.new_sem("d_const")
            S.dma(S.sp, k.sem_c, k.ppt[:], k.pp[:, :], writes=[k.b_pp])

            def ppv(name):
                o, n = PP[name]
                return k.ppt[:, o:o + n]
            k.ppv = ppv
            NCB = 5 * 128 + 2048
            k.cbf = sb("cbf", [128, NCB], BF16, stA)
            k.b_cbf = Buf("cbf")
            o_id = PP["ident"][0]
            S.op(S.dve, lambda: nc.vector.tensor_copy(out=k.cbf[:, :], in_=k.ppt[:, o_id:o_id + NCB]),
                 reads=[k.b_pp], writes=[k.b_cbf])
            k.ident_bf = k.cbf[:, 0:128]
            k.ntri_bf = k.cbf[:, 128:256]
            k.nones_bf = k.cbf[:, 256:384]
            k.ones_bf = k.cbf[:, 384:512]
            k.mlmask_bf = k.cbf[:, 512:640]
            k.sbmask_bf = k.cbf[:, 640:640 + 2048]
            k.eps_t = sb("eps_t", [128, 1], F32, stA)
            k.b_eps = Buf("eps")
            S.op(S.dve, lambda: nc.vector.memset(k.eps_t[:], EPS), writes=[k.b_eps])

            with ExitStack() as st12:
                k.hT = sb("hT", [128, 8, S_LEN], BF16, st12)
                k.b_hT = [Buf(f"hT{i}") for i in range(NTB)]
                with ExitStack() as st1:
                    if "s1" in stages:
                        stage1(k, st1)
                    if "s2" in stages:
                        stage2(k)
                if "hT" in dbg:
                    hT_d = nc.dram_tensor("hT_dbg", [128, 8, S_LEN], BF16, kind="ExternalOutput").ap()
                    sd = S.new_sem("d_dbg")
                    S.dma(S.sp, sd, hT_d, k.hT[:], reads=k.b_hT, is_output=True)
                S.barrier()
            if "4a" in stages and "mid" in stages:
                k.Wsb = sb("a_W0pre", [128, 8, D], BF16, stA)
                k.b_Wsb = Buf()
                sem_pre = S.new_sem("d_4aw_pre")
                for hf in range(2):
                    S.dma(S.pool, sem_pre, k.Wsb[:, :, hf * 512:(hf + 1) * 512],
                          k.w_sb[:, hf * 512:(hf + 1) * 512].rearrange("(c p) n -> p c n", p=128), writes=[k.b_Wsb])
            if "mid" in stages:
                stage_mid(k)
            if "sb" in stages:
                stage_sb(k)
            if "ml" in stages:
                stage_ml(k)
            if "xa" in stages:
                stage_xa(k)
            if "4a" in stages:
                stage_4a(k)
            S.barrier()
        if "4b" in stages:
            stage_4b(k)
        S.out_events.extend((s_, s_.count) for s_ in S.sems if s_.name.startswith("d_") and s_.count)
        S.finish()
        stuck = S.check_deadlock()
        assert not stuck, f"deadlock: {stuck}"
        k.S = S
    nc._sched = None
    return nc


def stage1(k, st):
    nc, S = k.nc, k.S
    if True:
        xb = [k.sb(f"xb{i}", [128, D], F32, st) for i in range(3)]
        b_xb = [Buf() for _ in range(3)]
        sem_x = [S.new_sem(f"d_x{i}") for i in range(3)]
        hb = [k.sb(f"hb{i}", [128, D], BF16, st) for i in range(2)]
        b_hb = [Buf() for _ in range(2)]
        junk = k.sb("junk1", [128, D], BF16, st)
        b_junk = Buf()
        stat = k.sb("stat1", [128, 3 * NTB], F32, st)
        gm = k.ppv("gmix")
        for i in range(NTB):
            xi, bx = xb[i % 3], b_xb[i % 3]
            S.dma(S.sp, sem_x[i % 3], xi[:], k.x[i * 128:(i + 1) * 128, :], writes=[bx])
            b_ss, b_rms, b_rs = Buf(), Buf(), Buf()
            ss = stat[:, 3 * i:3 * i + 1]
            rms = stat[:, 3 * i + 1:3 * i + 2]
            rs = stat[:, 3 * i + 2:3 * i + 3]
            S.op(S.act, lambda: nc.scalar.activation(out=junk[:], in_=xi[:], func=AF.Square, accum_out=ss),
                 reads=[bx], writes=[b_junk, b_ss])
            S.op(S.act, lambda: nc.scalar.activation(out=rms, in_=ss, func=AF.Sqrt, bias=k.eps_t[:, 0:1], scale=1.0 / D),
                 reads=[b_ss, k.b_eps], writes=[b_rms])
            S.op(S.dve, lambda: nc.vector.reciprocal(out=rs, in_=rms), reads=[b_rms], writes=[b_rs])
            hi, bh = hb[i % 2], b_hb[i % 2]
            S.op(S.dve, lambda: nc.vector.scalar_tensor_tensor(out=hi[:], in0=xi[:], scalar=rs, in1=gm,
                                                               op0=ALU.mult, op1=ALU.mult),
                 reads=[bx, b_rs, k.b_pp], writes=[bh])
            pb, bpb = k.pb[i % 2], k.b_pb[i % 2]
            for c in range(8):
                S.op(S.pe, lambda: nc.tensor.transpose(out=pb[:, c * 128:(c + 1) * 128], in_=hi[:, c * 128:(c + 1) * 128],
                                                       identity=k.ident_bf),
                     reads=[bh, k.b_cbf], writes=[bpb], inc=(c == 7))
            ev_eng = S.act if i % 2 == 0 else S.dve
            if ev_eng is S.act:
                S.op(S.act, lambda: nc.scalar.copy(out=k.hT[:, :, i * 128:(i + 1) * 128],
                                                   in_=pb[:].rearrange("p (c t) -> p c t", c=8)),
                     reads=[bpb], writes=[k.b_hT[i]])
            else:
                S.op(S.dve, lambda: nc.vector.tensor_copy(out=k.hT[:, :, i * 128:(i + 1) * 128],
                                                          in_=pb[:].rearrange("p (c t) -> p c t", c=8)),
                     reads=[bpb], writes=[k.b_hT[i]])


def stage2(k):
    nc, S = k.nc, k.S
    with ExitStack() as st:
        NW = 3
        wt = [k.sb(f"wt{i}", [128, 8, 512], BF16, st) for i in range(NW)]
        b_wt = [Buf() for _ in range(NW)]
        sem_w = [S.new_sem(f"d_w{i}") for i in range(NW)]
        ob = [k.sb(f"ob{i}", [128, S_LEN], BF16, st) for i in range(2)]
        b_ob = [Buf() for _ in range(2)]
        sem_ob = [S.new_sem(f"d_ob{i}") for i in range(2)]
        ot = [k.sb(f"ot{i}", [128, 4, 512], BF16, st) for i in range(2)]
        b_ot = [Buf() for _ in range(2)]
        sem_ot = [S.new_sem(f"d_ot{i}") for i in range(2)]
        zc = k.sb("zc", [128, 8 + S_LEN], BF16, st)
        b_zc = Buf()
        dg = [k.sb(f"dg{i}", [128, 4, 128], BF16, st) for i in range(2)]
        b_dg = [Buf() for _ in range(2)]
        cnt = {"w": 0, "ob": 0, "ot": 0, "pf": 0, "dg": 0, "ev": 0}
        S.op(S.dve, lambda: nc.vector.memset(zc[:, 0:8], 0.0), writes=[b_zc])

        def load_w(col0, ncols=512):
            i = cnt["w"] % NW
            cnt["w"] += 1
            S.dma(S.pool, sem_w[i], wt[i][:, :, 0:ncols],
                  k.w_in[:, col0:col0 + ncols].rearrange("(c p) n -> p c n", p=128), writes=[b_wt[i]])
            return wt[i], b_wt[i]

        def next_pf():
            i = cnt["pf"] % 4
            cnt["pf"] += 1
            return k.pf[i], k.b_pf[i]

        def evac(out, in_, reads, writes, func=None, scale=1.0, bias=None):
            if func is None and scale == 1.0:
                cnt["ev"] += 1
                if cnt["ev"] % 2 == 0:
                    return S.op(S.dve, lambda: nc.vector.tensor_copy(out=out, in_=in_), reads=reads, writes=writes)
                return S.op(S.act, lambda: nc.scalar.copy(out=out, in_=in_), reads=reads, writes=writes)
            f = func if func is not None else AF.Copy
            if bias is not None:
                return S.op(S.act, lambda: nc.scalar.activation(out=out, in_=in_, func=f, bias=bias, scale=scale),
                            reads=reads, writes=writes)
            return S.op(S.act, lambda: nc.scalar.activation(out=out, in_=in_, func=f, scale=scale),
                        reads=reads, writes=writes)

        def fm_group(w, bw, g, dst_row, scale=1.0):
            i = cnt["ob"] % 2
            cnt["ob"] += 1
            o, bo = ob[i], b_ob[i]
            for tt in range(NTT):
                p, bp = next_pf()
                for c in range(8):
                    S.op(S.pe, lambda: nc.tensor.matmul(p[:], lhsT=w[:, c, g * 128:(g + 1) * 128],
                                                        rhs=k.hT[:, c, tt * 512:(tt + 1) * 512],
                                                        start=(c == 0), stop=(c == 7)),
                         reads=[bw] + k.b_hT[tt * 4:tt * 4 + 4], writes=[bp], inc=(c == 7))
                evac(o[:, tt * 512:(tt + 1) * 512], p[:], [bp], [bo], scale=scale)
            S.dma(S.sp, sem_ob[i], dst_row, o[:], reads=[bo])

        def tm_tile(w, bw, dst, col0, func=None):
            for tq in range(8):
                i = cnt["ot"] % 2
                cnt["ot"] += 1
                o, bo = ot[i], b_ot[i]
                for j in range(4):
                    tb = tq * 4 + j
                    p, bp = next_pf()
                    for c in range(8):
                        S.op(S.pe, lambda: nc.tensor.matmul(p[:], lhsT=k.hT[:, c, tb * 128:(tb + 1) * 128],
                                                            rhs=w[:, c, :], start=(c == 0), stop=(c == 7)),
                             reads=[bw, k.b_hT[tb]], writes=[bp], inc=(c == 7))
                    evac(o[:, j, :], p[:], [bp], [bo], func=func)
                S.dma(S.sp, sem_ot[i], dst[tq * 512:(tq + 1) * 512, col0:col0 + 512].rearrange("(j p) n -> p j n", p=128),
                      o[:], reads=[bo])

        zcs = [zc, k.sb("zc2", [128, 8 + S_LEN], BF16, st)]
        b_zcs = [b_zc, Buf()]
        S.op(S.dve, lambda: nc.vector.memset(zcs[1][:, 0:8], 0.0), writes=[b_zcs[1]])

        def conv_proj(it):
            w, bw, g, zi = it["w"], it["bw"], it["g"], it["zi"]
            for tt in range(NTT):
                p, bp = next_pf()
                for c in range(8):
                    S.op(S.pe, lambda: nc.tensor.matmul(p[:], lhsT=w[:, c, g * 128:(g + 1) * 128],
                                                        rhs=k.hT[:, c, tt * 512:(tt + 1) * 512],
                                                        start=(c == 0), stop=(c == 7)),
                         reads=[bw] + k.b_hT[tt * 4:tt * 4 + 4], writes=[bp], inc=(c == 7))
                evac(zcs[zi][:, 8 + tt * 512:8 + (tt + 1) * 512], p[:], [bp], [b_zcs[zi]])

        def conv_apply(it):
            cg, zi = it["cg"], it["zi"]
            di = cnt["dg"] % 2
            cnt["dg"] += 1
            d, bd = dg[di], b_dg[di]
            o_cw = PP["convw"][0]
            for j in range(4):
                S.op(S.dve, lambda: nc.vector.tensor_scalar(out=d[:, j, :], in0=k.ppv("ident"),
                                                            scalar1=k.ppt[:, o_cw + cg * 4 + j:o_cw + cg * 4 + j + 1],
                                                            scalar2=None, op0=ALU.mult),
                     reads=[k.b_pp], writes=[bd])
            i = cnt["ob"] % 2
            cnt["ob"] += 1
            o, bo = ob[i], b_ob[i]
            it["o"], it["bo"] = o, bo
            o_cb = PP["convb"][0]
            for tt in range(NTT):
                p, bp = next_pf()
                for j in range(4):
                    S.op(S.pe, lambda: nc.tensor.matmul(p[:], lhsT=d[:, j, :],
                                                        rhs=zcs[zi][:, 5 + j + tt * 512:5 + j + (tt + 1) * 512],
                                                        start=(j == 0), stop=(j == 3)),
                         reads=[bd, b_zcs[zi]], writes=[bp], inc=(j == 3))
                evac(o[:, tt * 512:(tt + 1) * 512], p[:], [bp], [bo], func=AF.Silu,
                     bias=k.ppt[:, o_cb + cg:o_cb + cg + 1])
            S.dma(S.sp, sem_ob[i], it["dst_row"], o[:], reads=[bo])

        def conv_ktrans(it):
            if it["kdst"] is None:
                return
            o, bo, kdst, kcol = it["o"], it["bo"], it["kdst"], it["kcol"]
            for tq in range(8):
                pb, bpb = k.pb[tq % 2], k.b_pb[tq % 2]
                for j in range(4):
                    tb = tq * 4 + j
                    S.op(S.pe, lambda: nc.tensor.transpose(out=pb[:, j * 128:(j + 1) * 128],
                                                           in_=o[:, tb * 128:(tb + 1) * 128], identity=k.ident_bf),
                         reads=[bo, k.b_cbf], writes=[bpb], inc=(j == 3))
                ii = cnt["ot"] % 2
                cnt["ot"] += 1
                t_, bt = ot[ii], b_ot[ii]
                evac(t_[:, :, 0:128], pb[:, 0:512].rearrange("p (j n) -> p j n", j=4), [bpb], [bt])
                S.dma(S.sp, sem_ot[ii],
                      kdst[tq * 512:(tq + 1) * 512, kcol:kcol + 128].rearrange("(j p) n -> p j n", p=128),
                      t_[:, :, 0:128], reads=[bt])

        for half in range(2):
            w, bw = load_w(C_SBV + half * 512)
            tm_tile(w, bw, k.v_sb, half * 512)
        for half in range(2):
            w, bw = load_w(C_MLV + half * 512)
            tm_tile(w, bw, k.v_ml, half * 512)
        for half in range(2):
            w, bw = load_w(C_SBQ + half * 512)
            for g in range(4):
                fm_group(w, bw, g, k.qt_sb[half * 4 + g], scale=128.0 ** -0.5)
        for half in range(2):
            w, bw = load_w(C_SBK + half * 512)
            for g in range(4):
                fm_group(w, bw, g, k.kt_sb[half * 4 + g])
        for half in range(2):
            w, bw = load_w(C_XQ + half * 512)
            for g in range(4):
                fm_group(w, bw, g, k.qt_x[half * 4 + g])
        for half in range(2):
            w, bw = load_w(C_MLO + half * 512)
            tm_tile(w, bw, k.o_ml, half * 512, func=AF.Sigmoid)
        items = []
        for which, c0 in ((0, C_MLQ), (1, C_MLK)):
            for half in range(2):
                for g in range(4):
                    cg = half * 4 + g
                    items.append({"col0": c0 + half * 512, "g": g, "cg": which * 8 + cg, "zi": len(items) % 2,
                                  "dst_row": (k.qt_ml if which == 0 else k.kt_ml)[cg],
                                  "kdst": k.k_ml if which == 1 else None, "kcol": cg * 128})
        wcur = {}

        def get_w(it):
            if it["col0"] not in wcur:
                wcur.clear()
                wcur[it["col0"]] = load_w(it["col0"])
            it["w"], it["bw"] = wcur[it["col0"]]
        get_w(items[0])
        conv_proj(items[0])
        for n, it in enumerate(items):
            if n + 1 < len(items):
                get_w(items[n + 1])
                conv_proj(items[n + 1])
            conv_apply(it)
            if n > 0:
                conv_ktrans(items[n - 1])
        conv_ktrans(items[-1])
        o_bg = PP["bgate"][0]
        for t6 in range(6):
            w, bw = load_w(C_GATE + t6 * 512)
            for g in range(4):
                gg = t6 * 4 + g
                i = cnt["ob"] % 2
                cnt["ob"] += 1
                o, bo = ob[i], b_ob[i]
                for tt in range(NTT):
                    p, bp = next_pf()
                    for c in range(8):
                        S.op(S.pe, lambda: nc.tensor.matmul(p[:], lhsT=w[:, c, g * 128:(g + 1) * 128],
                                                            rhs=k.hT[:, c, tt * 512:(tt + 1) * 512],
                                                            start=(c == 0), stop=(c == 7)),
                             reads=[bw] + k.b_hT[tt * 4:tt * 4 + 4], writes=[bp], inc=(c == 7))
                    evac(o[:, tt * 512:(tt + 1) * 512], p[:], [bp, k.b_pp], [bo], func=AF.Sigmoid,
                         bias=k.ppt[:, o_bg + gg:o_bg + gg + 1])
                S.dma(S.sp, sem_ob[i], k.g_scr[gg], o[:], reads=[bo])
        w, bw = load_w(C_MLI, 8)
        gi = [k.sb(f"gi_rows{i}", [4, 2, 512], F32, st) for i in range(2)]
        b_gi = [Buf(), Buf()]
        sem_g = [S.new_sem("d_gif0"), S.new_sem("d_gif1")]
        o_bif = PP["bif"][0]
        for tt in range(NTT):
            gt, bg = gi[tt % 2], b_gi[tt % 2]
            for which in range(2):
                p, bp = next_pf()
                for c in range(8):
                    S.op(S.pe, lambda: nc.tensor.matmul(p[0:4, :], lhsT=w[:, c, which * 4:which * 4 + 4],
                                                        rhs=k.hT[:, c, tt * 512:(tt + 1) * 512],
                                                        start=(c == 0), stop=(c == 7)),
                         reads=[bw] + k.b_hT[tt * 4:tt * 4 + 4], writes=[bp], inc=(c == 7))
                S.op(S.act, lambda: nc.scalar.activation(out=gt[:, which, :], in_=p[0:4, :],
                                                         func=AF.Identity,
                                                         bias=k.ppt[0:4, o_bif + which:o_bif + which + 1], scale=1.0),
                     reads=[bp, k.b_pp], writes=[bg])
            S.dma(S.sp, sem_g[tt % 2], k.gif[:, tt * 512:(tt + 1) * 512].rearrange("(w h) t -> h w t", w=2), gt[:], reads=[bg])
        S.barrier()


def stage_sb(k):
    nc, S = k.nc, k.S
    NS = 3
    tiles_of = [[7, 3], [6, 4], [5, 2, 1, 0]]
    with ExitStack() as st:
        qT = [k.sb(f"sb_q{i}", [128, S_LEN], BF16, st) for i in range(2)]
        kT = [k.sb(f"sb_k{i}", [128, S_LEN], BF16, st) for i in range(2)]
        vv = [k.sb(f"sb_v{i}", [128, NTB, 128], BF16, st) for i in range(2)]
        b_q = [Buf() for _ in range(2)]
        b_k = [Buf() for _ in range(2)]
        b_v = [Buf() for _ in range(2)]
        sem_in = [S.new_sem(f"d_sbin{i}") for i in range(2)]
        yT = [k.sb(f"sb_y{i}", [128, S_LEN], BF16, st) for i in range(2)]
        b_y = [Buf() for _ in range(2)]
        sem_y = [S.new_sem(f"d_sby{i}") for i in range(2)]
        e_sb = [k.sb(f"sb_e{s}", [128, 512], F32, st) for s in range(NS)]
        b_e = [Buf() for _ in range(NS)]
        nl = [[k.sb(f"sb_nl{s}_{j}", [128, 512], BF16, st) for j in range(2)] for s in range(NS)]
        b_nl = [[Buf() for _ in range(2)] for _ in range(NS)]
        at = [[k.sb(f"sb_at{s}_{j}", [128, 512], BF16, st) for j in range(2)] for s in range(NS)]
        b_at = [[Buf() for _ in range(2)] for _ in range(NS)]
        sacc = [k.sb(f"sb_sacc{s}", [128, 512], BF16, st) for s in range(NS)]
        b_sacc = [Buf() for _ in range(NS)]
        zb = [k.pf[s] for s in range(NS)]
        b_zb = [k.b_pf[s] for s in range(NS)]
        ob = [k.pf[NS + s] for s in range(NS)]
        b_ob = [k.b_pf[NS + s] for s in range(NS)]

        def load_head(h):
            i = h % 2
            S.dma(S.sp, sem_in[i], qT[i][:], k.qt_sb[h], writes=[b_q[i]])
            S.dma(S.sp, sem_in[i], kT[i][:], k.kt_sb[h], writes=[b_k[i]])
            S.dma(S.sp, sem_in[i], vv[i][:], k.v_sb[:, h * 128:(h + 1) * 128].rearrange("(j p) n -> p j n", p=128),
                  writes=[b_v[i]])

        def mm1(h, s, qi, kb):
            i = h % 2
            S.op(S.pe, lambda: nc.tensor.matmul(zb[s][:], lhsT=kT[i][:, kb * 128:(kb + 1) * 128],
                                                rhs=qT[i][:, qi * 512:(qi + 1) * 512], start=True, stop=False),
                 reads=[b_k[i], b_q[i]], writes=[b_zb[s]])

        load_head(0)
        for h in range(8):
            if h + 1 < 8:
                load_head(h + 1)
            i = h % 2
            steps = [[(qi, kb) for qi in tiles_of[s] for kb in range(4 * qi + 3, -1, -1)] for s in range(NS)]
            nsteps = len(steps[0])
            assert all(len(x) == nsteps for x in steps)
            for s in range(NS):
                mm1(h, s, *steps[s][0])
            for n in range(nsteps):
                par = n % 2
                info = []
                for s in range(NS):
                    qi, kb = steps[s][n]
                    r = kb - 4 * qi
                    info.append((qi, kb, r, kb == 4 * qi + 3, kb == 0))
                for s in range(NS):
                    S.op(S.act, lambda: nc.scalar.activation(out=e_sb[s][:], in_=zb[s][:], func=AF.Exp),
                         reads=[b_zb[s]], writes=[b_e[s]])
                for s in range(NS):
                    S.op(S.act, lambda: nc.scalar.activation(out=nl[s][par][:], in_=e_sb[s][:], func=AF.Ln, bias=1.0),
                         reads=[b_e[s]], writes=[b_nl[s][par]])
                for s in range(NS):
                    qi, kb, r, first, last = info[s]
                    if r >= 0:
                        S.op(S.dve, lambda: nc.vector.tensor_tensor(out=nl[s][par][:], in0=nl[s][par][:],
                                                                    in1=k.sbmask_bf[:, r * 512:(r + 1) * 512], op=ALU.mult),
                             reads=[b_nl[s][par], k.b_cbf], writes=[b_nl[s][par]])
                for s in range(NS):
                    qi, kb, r, first, last = info[s]
                    S.op(S.pe, lambda: nc.tensor.matmul(zb[s][:], lhsT=k.ntri_bf, rhs=nl[s][par][:], start=False, stop=first),
                         reads=[b_nl[s][par], k.b_cbf], writes=[b_zb[s]], inc=first)
                    if not first:
                        S.op(S.pe, lambda: nc.tensor.matmul(zb[s][:], lhsT=k.nones_bf, rhs=sacc[s][:], start=False, stop=True),
                             reads=[b_sacc[s], k.b_cbf], writes=[b_zb[s]])
                for s in range(NS):
                    qi, kb, r, first, last = info[s]
                    S.op(S.act, lambda: nc.scalar.activation(out=at[s][par][:], in_=zb[s][:], func=AF.Exp),
                         reads=[b_zb[s]], writes=[b_at[s][par]])
                    if not last:
                        if first:
                            S.op(S.pool, lambda: nc.gpsimd.tensor_copy(out=sacc[s][:], in_=nl[s][par][:]),
                                 reads=[b_nl[s][par]], writes=[b_sacc[s]])
                        else:
                            S.op(S.pool, lambda: nc.gpsimd.tensor_tensor(out=sacc[s][:], in0=sacc[s][:], in1=nl[s][par][:],
                                                                        op=ALU.add),
                                 reads=[b_nl[s][par], b_sacc[s]], writes=[b_sacc[s]])
                for s in range(NS):
                    qi, kb, r, first, last = info[s]
                    if r >= 0:
                        S.op(S.dve, lambda: nc.vector.tensor_tensor(out=at[s][par][:], in0=at[s][par][:],
                                                                    in1=k.sbmask_bf[:, r * 512:(r + 1) * 512], op=ALU.mult),
                             reads=[b_at[s][par], k.b_cbf], writes=[b_at[s][par]])
                    S.op(S.pe, lambda: nc.tensor.matmul(ob[s][:], lhsT=vv[i][:, kb, :], rhs=at[s][par][:],
                                                        start=first, stop=last),
                         reads=[b_v[i], b_at[s][par]], writes=[b_ob[s]])
                    if n + 1 < nsteps:
                        mm1(h, s, *steps[s][n + 1])
                    if last:
                        S.op(S.dve, lambda: nc.vector.tensor_copy(out=yT[i][:, qi * 512:(qi + 1) * 512], in_=ob[s][:]),
                             reads=[b_ob[s]], writes=[b_y[i]])
            S.dma(S.sp, sem_y[i], k.yt_sb[h], yT[i][:], reads=[b_y[i]])
        S.barrier()


def stage_ml(k):
    nc, S = k.nc, k.S
    with ExitStack() as st:
        tab = k.sb("ml_tab", [128, 3, 32, 4], F32, st)
        Mb = k.sb("ml_Mb", [128, 33, 4], F32, st)
        uT = k.sb("ml_u", [128, 32, 4], F32, st)
        wT = k.sb("ml_w", [128, 32, 4], F32, st)
        flT = k.sb("ml_fl", [128, 32, 4], F32, st)
        decT = k.sb("ml_dec", [128, 32, 4], F32, st)
        nl16 = k.sb("ml_nl16", [128, 1], F32, st)
        b_tab, b_Mb, b_u, b_w, b_fl, b_dec, b_c = Buf(), Buf(), Buf(), Buf(), Buf(), Buf(), Buf()
        S.op(S.dve, lambda: nc.vector.memset(nl16[:], -LN16), writes=[b_c])
        S.op(S.dve, lambda: nc.vector.memset(Mb[:, 0, :], 0.0), writes=[b_Mb])
        with ExitStack() as st2:
            fp = k.sb("ml_fp", [4, S_LEN], F32, st2)
            ip = k.sb("ml_ip", [4, S_LEN], F32, st2)
            Fn = k.sb("ml_Fn", [4, S_LEN], F32, st2)
            on = k.sb("ml_on", [4, S_LEN], F32, st2)
            b_fp, b_ip, b_Fn, b_on = Buf(), Buf(), Buf(), Buf()
            sg = S.new_sem("d_mlg")
            S.dma(S.sp, sg, fp[:], k.gif[4:8, :], writes=[b_fp])
            S.dma(S.sp, sg, ip[:], k.gif[0:4, :], writes=[b_ip])
            S.op(S.dve, lambda: nc.vector.memset(on[:], 1.0), writes=[b_on])
            S.op(S.act, lambda: nc.scalar.activation(out=fp[:], in_=fp[:], func=AF.Exp, scale=-1.0),
                 reads=[b_fp], writes=[b_fp])
            S.op(S.act, lambda: nc.scalar.activation(out=fp[:], in_=fp[:], func=AF.Ln, bias=1.0),
                 reads=[b_fp], writes=[b_fp])
            S.op(S.dve, lambda: nc.vector.tensor_tensor_scan(out=Fn[:], data0=on[:], data1=fp[:], initial=0.0,
                                                             op0=ALU.mult, op1=ALU.add),
                 reads=[b_on, b_fp], writes=[b_Fn])
            S.op(S.dve, lambda: nc.vector.tensor_tensor(out=ip[:], in0=ip[:], in1=Fn[:], op=ALU.add),
                 reads=[b_ip, b_Fn], writes=[b_ip])
            S.op(S.dve, lambda: nc.vector.tensor_tensor_scan(out=fp[:], data0=ip[:], data1=ip[:], initial=0.0,
                                                             op0=ALU.max, op1=ALU.max),
                 reads=[b_ip], writes=[b_fp])
            pt, bpt = k.pf[0], k.b_pf[0]
            ptv = pt[:, 0:384].rearrange("p (q c h) -> p q c h", q=3, c=32)
            idf = k.ppv("ident")
            for q, (X, bX) in enumerate(((Fn, b_Fn), (ip, b_ip), (fp, b_fp))):
                for c in range(32):
                    S.op(S.pe, lambda: nc.tensor.transpose(out=ptv[:, q, c, :], in_=X[0:4, c * 128:(c + 1) * 128],
                                                           identity=idf[0:4, 0:4]),
                         reads=[bX, k.b_pp], writes=[bpt], inc=(q == 2 and c == 31))
            S.op(S.dve, lambda: nc.vector.tensor_copy(out=tab[:], in_=ptv), reads=[bpt], writes=[b_tab])
            S.barrier()
        pm, bpm = k.pf[1], k.b_pf[1]
        o_sel = PP["sel127"][0]
        S.op(S.pe, lambda: nc.tensor.matmul(pm[:, 0:128], lhsT=k.ppt[:, o_sel:o_sel + 128],
                                            rhs=tab[:, 2].rearrange("p c h -> p (c h)"), start=True, stop=True),
             reads=[b_tab, k.b_pp], writes=[bpm])
        S.op(S.dve, lambda: nc.vector.tensor_copy(out=Mb[:, 1:33, :], in_=pm[:, 0:128].rearrange("p (c h) -> p c h", c=32)),
             reads=[bpm], writes=[b_Mb])
        tmp = k.sb("ml_tmp", [128, 32, 4], F32, st)
        b_tmp = Buf()

        def table(dst, bd, in0, in1, bias):
            S.op(S.dve, lambda: nc.vector.tensor_tensor(out=tmp[:], in0=in0, in1=in1, op=ALU.subtract),
                 reads=[b_tab, b_Mb], writes=[b_tmp])
            if bias:
                S.op(S.act, lambda: nc.scalar.activation(out=dst[:], in_=tmp[:], func=AF.Exp, bias=nl16[:, 0:1]),
                     reads=[b_tmp, b_c], writes=[bd])
            else:
                S.op(S.act, lambda: nc.scalar.activation(out=dst[:], in_=tmp[:], func=AF.Exp),
                     reads=[b_tmp], writes=[bd])
        table(uT, b_u, tab[:, 1], Mb[:, 0:32, :], True)
        table(wT, b_w, tab[:, 1], Mb[:, 1:33, :], True)
        table(flT, b_fl, tab[:, 0], Mb[:, 0:32, :], False)
        table(decT, b_dec, Mb[:, 0:32, :], Mb[:, 1:33, :], False)

        if "ml_tabs" in k.dbg:
            td = nc.dram_tensor("ml_tabs", [128, 7, 128], F32, kind="ExternalOutput").ap()
            sdd = S.new_sem("d_mltab")
            S.dma(S.sp, sdd, td[:, 0:3, :], tab[:].rearrange("p q c h -> p q (c h)"), reads=[b_tab])
            for qi_, (t_, b_) in enumerate(((uT, b_u), (wT, b_w), (flT, b_fl), (decT, b_dec))):
                S.dma(S.sp, sdd, td[:, 3 + qi_, :], t_[:].rearrange("p c h -> p (c h)"), reads=[b_])
        NH = 2
        qTt = [[k.sb(f"ml_q{a}_{i}", [128, 2, 512], BF16, st) for i in range(2)] for a in range(NH)]
        kTt = [[k.sb(f"ml_k{a}_{i}", [128, 2, 512], BF16, st) for i in range(2)] for a in range(NH)]
        ktm = [[k.sb(f"ml_kt{a}_{i}", [128, 4, 256], BF16, st) for i in range(2)] for a in range(NH)]
        vtm = [[k.sb(f"ml_vt{a}_{i}", [128, 4, 256], BF16, st) for i in range(2)] for a in range(NH)]
        otm = [[k.sb(f"ml_ot{a}_{i}", [128, 4, 256], BF16, st) for i in range(2)] for a in range(NH)]
        b_in = [[Buf() for i in range(2)] for a in range(NH)]
        sem_in = [[S.new_sem(f"d_mlin{a}_{i}") for i in range(2)] for a in range(NH)]
        yTt = [[k.sb(f"ml_y{a}_{i}", [128, 2, 512], BF16, st) for i in range(2)] for a in range(NH)]
        b_y = [[Buf() for i in range(2)] for a in range(NH)]
        sem_y = [[S.new_sem(f"d_mly{a}_{i}") for i in range(2)] for a in range(NH)]
        PT = [[k.sb(f"ml_PT{a}_{i}", [128, 128], BF16, st) for i in range(2)] for a in range(NH)]
        vu = [[k.sb(f"ml_vu{a}_{i}", [128, 264], BF16, st) for i in range(2)] for a in range(NH)]
        vw = [[k.sb(f"ml_vw{a}_{i}", [128, 264], BF16, st) for i in range(2)] for a in range(NH)]
        b_PT = [[Buf() for i in range(2)] for a in range(NH)]
        b_vu = [[Buf() for i in range(2)] for a in range(NH)]
        b_vw = [[Buf() for i in range(2)] for a in range(NH)]
        C32 = [k.sb(f"ml_C32_{a}", [128, 2, 264], F32, st) for a in range(NH)]
        Cbf = [[k.sb(f"ml_Cbf{a}_{i}", [128, 2, 264], BF16, st) for i in range(2)] for a in range(NH)]
        b_C32 = [Buf() for a in range(NH)]
        b_Cbf = [[Buf() for i in range(2)] for a in range(NH)]
        hs = [k.sb(f"ml_hs{a}", [128, 256], F32, st) for a in range(NH)]
        t1 = [k.sb(f"ml_t1{a}", [128, 256], F32, st) for a in range(NH)]
        yb = [k.sb(f"ml_yb{a}", [128, 256], BF16, st) for a in range(NH)]
        jk = [k.sb(f"ml_jk{a}", [128, 256], BF16, st) for a in range(NH)]
        sm = [k.sb(f"ml_sm{a}", [128, 8], F32, st) for a in range(NH)]
        b_hs = [Buf() for a in range(NH)]
        b_t1 = [Buf() for a in range(NH)]
        b_yb = [Buf() for a in range(NH)]
        b_jk = [Buf() for a in range(NH)]
        b_sm = [[Buf() for _ in range(8)] for a in range(NH)]
        mlng = k.ppv("mlng")

        def load_group(a, h, g):
            i = g % 2
            sm_, bb = sem_in[a][i], [b_in[a][i]]
            cols = slice(g * 512, (g + 1) * 512)
            S.dma(S.sp, sm_, qTt[a][i][:], k.qt_ml[2 * h:2 * h + 2, :, cols].rearrange("c p t -> p c t"), writes=bb)
            S.dma(S.sp, sm_, kTt[a][i][:], k.kt_ml[2 * h:2 * h + 2, :, cols].rearrange("c p t -> p c t"), writes=bb)
            for dst, src in ((ktm, k.k_ml), (vtm, k.v_ml), (otm, k.o_ml)):
                S.dma(S.sp, sm_, dst[a][i][:], src[cols, h * 256:(h + 1) * 256].rearrange("(j p) n -> p j n", p=128), writes=bb)

        for hp in range(2):
            heads = [2 * hp, 2 * hp + 1]
            for a in range(NH):
                load_group(a, heads[a], 0)
            for g in range(8):
                if g + 1 < 8:
                    for a in range(NH):
                        load_group(a, heads[a], g + 1)
                gi = g % 2
                for j in range(4):
                    c = g * 4 + j
                    par = c % 2
                    for a in range(NH):
                        h = heads[a]
                        col = c * 4 + h
                        bank1, bb1 = k.pf[3 * a], k.b_pf[3 * a]
                        bank2, bb2 = k.pf[3 * a + 1], k.b_pf[3 * a + 1]
                        bank3, bb3 = k.pf[3 * a + 2], k.b_pf[3 * a + 2]
                        bin_ = b_in[a][gi]
                        tsl = slice(j * 128, (j + 1) * 128)
                        uc = uT[:].rearrange("p c h -> p (c h)")[:, col:col + 1]
                        wc = wT[:].rearrange("p c h -> p (c h)")[:, col:col + 1]
                        flc = flT[:].rearrange("p c h -> p (c h)")[:, col:col + 1]
                        dcc = decT[:].rearrange("p c h -> p (c h)")[:, col:col + 1]
                        ps_s = bank1[:, 264:392]
                        for dc in range(2):
                            S.op(S.pe, lambda: nc.tensor.matmul(ps_s, lhsT=kTt[a][gi][:, dc, tsl], rhs=qTt[a][gi][:, dc, tsl],
                                                                start=(dc == 0), stop=(dc == 1)),
                                 reads=[bin_], writes=[bb1], inc=(dc == 1))
                        S.op(S.dve, lambda: nc.vector.tensor_tensor(out=PT[a][par][:], in0=ps_s, in1=k.mlmask_bf, op=ALU.mult),
                             reads=[bb1, k.b_cbf], writes=[b_PT[a][par]])
                        for (dst, bd, sc, bs) in ((vu[a][par], b_vu[a][par], uc, b_u), (vw[a][par], b_vw[a][par], wc, b_w)):
                            S.op(S.act, lambda: nc.scalar.activation(out=dst[:, 0:256], in_=vtm[a][gi][:, j, :], func=AF.Copy, scale=sc),
                                 reads=[bin_, bs], writes=[bd])
                            S.op(S.act, lambda: nc.scalar.copy(out=dst[:, 256:257], in_=sc), reads=[bs], writes=[bd])
                        ps_n = bank3[:, 0:257]
                        S.op(S.pe, lambda: nc.tensor.matmul(ps_n, lhsT=PT[a][par][:], rhs=vu[a][par][:, 0:257],
                                                            start=True, stop=(c == 0)),
                             reads=[b_PT[a][par], b_vu[a][par]], writes=[bb3], inc=(c == 0))
                        if c > 0:
                            cb, bcb = Cbf[a][1 - par], b_Cbf[a][1 - par]
                            for dc in range(2):
                                S.op(S.pe, lambda: nc.tensor.matmul(ps_n, lhsT=qTt[a][gi][:, dc, tsl], rhs=cb[:, dc, 0:257],
                                                                    start=False, stop=(dc == 1)),
                                     reads=[bin_, bcb], writes=[bb3], inc=(dc == 1))
                        ps_c = [bank1[:, 0:257], bank2[:, 0:257]]
                        bbc = [bb1, bb2]
                        for dc in range(2):
                            S.op(S.pe, lambda: nc.tensor.matmul(ps_c[dc], lhsT=ktm[a][gi][:, j, dc * 128:(dc + 1) * 128],
                                                                rhs=vw[a][par][:, 0:257], start=True, stop=True),
                                 reads=[bin_, b_vw[a][par]], writes=[bbc[dc]])
                        for dc in range(2):
                            if c == 0:
                                S.op(S.dve, lambda: nc.vector.tensor_copy(out=C32[a][:, dc, 0:257], in_=ps_c[dc]),
                                     reads=[bbc[dc]], writes=[b_C32[a]])
                            else:
                                S.op(S.dve, lambda: nc.vector.scalar_tensor_tensor(out=C32[a][:, dc, 0:257], in0=C32[a][:, dc, 0:257],
                                                                                   scalar=dcc, in1=ps_c[dc],
                                                                                   op0=ALU.mult, op1=ALU.add),
                                     reads=[bbc[dc], b_C32[a], b_dec], writes=[b_C32[a]])
                        S.op(S.pool, lambda: nc.gpsimd.tensor_copy(out=Cbf[a][par][:, :, 0:257], in_=C32[a][:, :, 0:257]),
                             reads=[b_C32[a]], writes=[b_Cbf[a][par]])
                        smt = sm[a]
                        S.op(S.act, lambda: nc.scalar.activation(out=smt[:, 5:6], in_=bank3[:, 256:257], func=AF.Abs),
                             reads=[bb3], writes=[b_sm[a][5]])
                        S.op(S.dve, lambda: nc.vector.tensor_tensor(out=smt[:, 0:1], in0=smt[:, 5:6], in1=flc, op=ALU.max),
                             reads=[b_sm[a][5], b_fl], writes=[b_sm[a][0]])
                        S.op(S.dve, lambda: nc.vector.reciprocal(out=smt[:, 1:2], in_=smt[:, 0:1]),
                             reads=[b_sm[a][0]], writes=[b_sm[a][1]])
                        S.op(S.act, lambda: nc.scalar.activation(out=hs[a][:], in_=bank3[:, 0:256], func=AF.Copy, scale=smt[:, 1:2]),
                             reads=[bb3, b_sm[a][1]], writes=[b_hs[a]])
                        S.op(S.act, lambda: nc.scalar.activation(out=jk[a][:], in_=hs[a][:], func=AF.Square, accum_out=smt[:, 2:3]),
                             reads=[b_hs[a]], writes=[b_jk[a], b_sm[a][2]])
                        S.op(S.act, lambda: nc.scalar.activation(out=smt[:, 3:4], in_=smt[:, 2:3], func=AF.Sqrt,
                                                                 bias=k.eps_t[:, 0:1], scale=1.0 / 256),
                             reads=[b_sm[a][2], k.b_eps], writes=[b_sm[a][3]])
                        S.op(S.dve, lambda: nc.vector.reciprocal(out=smt[:, 4:5], in_=smt[:, 3:4]),
                             reads=[b_sm[a][3]], writes=[b_sm[a][4]])
                        S.op(S.dve, lambda: nc.vector.scalar_tensor_tensor(out=t1[a][:], in0=hs[a][:], scalar=smt[:, 4:5],
                                                                           in1=mlng[:, h * 256:(h + 1) * 256],
                                                                           op0=ALU.mult, op1=ALU.mult),
                             reads=[b_hs[a], b_sm[a][4], k.b_pp], writes=[b_t1[a]])
                        S.op(S.pool, lambda: nc.gpsimd.tensor_tensor(out=yb[a][:], in0=t1[a][:], in1=otm[a][gi][:, j, :], op=ALU.mult),
                             reads=[b_t1[a], bin_], writes=[b_yb[a]])
                        pb, bpb = k.pb[a], k.b_pb[a]
                        for dc in range(2):
                            S.op(S.pe, lambda: nc.tensor.transpose(out=pb[:, dc * 128:(dc + 1) * 128], in_=yb[a][:, dc * 128:(dc + 1) * 128],
                                                                   identity=k.ident_bf),
                                 reads=[b_yb[a], k.b_cbf], writes=[bpb], inc=(dc == 1))
                        S.op(S.act, lambda: nc.scalar.copy(out=yTt[a][gi][:, :, tsl], in_=pb[:, 0:256].rearrange("p (c t) -> p c t", c=2)),
                             reads=[bpb], writes=[b_y[a][gi]])
                for a in range(NH):
                    h = heads[a]
                    S.dma(S.sp, sem_y[a][gi], k.yt_ml[2 * h:2 * h + 2, :, g * 512:(g + 1) * 512].rearrange("c p t -> p c t"),
                          yTt[a][gi][:], reads=[b_y[a][gi]])
        S.barrier()


def stage_xa(k):
    nc, S = k.nc, k.S
    with ExitStack() as st:
        mt = k.sb("xa_mt", [128, 2, D], F32, st)
        mn = k.sb("xa_mn", [128, D], BF16, st)
        jk = k.sb("xa_jk", [128, D], BF16, st)
        memT = k.sb("xa_memT", [128, 8, N_MEM], BF16, st)
        knT = k.sb("xa_knT", [128, 4, 2, N_MEM], BF16, st)
        vmem = k.sb("xa_vmem", [128, 2, D], BF16, st)
        kn = k.sb("xa_kn", [128, 256], BF16, st)
        sm = k.sb("xa_sm", [128, 64], F32, st)
        wkv = [k.sb(f"xa_w{i}", [128, 8, 512], BF16, st) for i in range(4)]
        b_mt, b_mn, b_jk, b_memT, b_knT, b_vmem, b_kn = Buf(), Buf(), Buf(), Buf(), Buf(), Buf(), Buf()
        b_w = [Buf() for _ in range(4)]
        sem = S.new_sem("d_xa0")
        sem_w = [S.new_sem(f"d_xaw{i}") for i in range(4)]
        S.dma(S.sp, sem, mt[:], k.mem.rearrange("(j p) d -> p j d", p=128), writes=[b_mt])
        for i in range(4):
            S.dma(S.pool, sem_w[i], wkv[i][:], k.w_mem_kv[:, i * 512:(i + 1) * 512].rearrange("(c p) n -> p c n", p=128),
                  writes=[b_w[i]])
        nsm = [0]

        def smcol():
            c = nsm[0]
            nsm[0] += 1
            return sm[:, c:c + 1], Buf()

        def rstd_of(src_ap, src_bufs, n):
            ss, bss = smcol()
            rm, brm = smcol()
            rs, brs = smcol()
            S.op(S.act, lambda: nc.scalar.activation(out=jk[:, 0:n], in_=src_ap, func=AF.Square, accum_out=ss),
                 reads=src_bufs, writes=[b_jk, bss])
            S.op(S.act, lambda: nc.scalar.activation(out=rm, in_=ss, func=AF.Sqrt, bias=k.eps_t[:, 0:1], scale=1.0 / n),
                 reads=[bss, k.b_eps], writes=[brm])
            S.op(S.dve, lambda: nc.vector.reciprocal(out=rs, in_=rm), reads=[brm], writes=[brs])
            return rs, brs

        gmem = k.ppv("gmem")
        for j in range(2):
            rs, brs = rstd_of(mt[:, j, :], [b_mt], D)
            S.op(S.dve, lambda: nc.vector.scalar_tensor_tensor(out=mn[:], in0=mt[:, j, :], scalar=rs, in1=gmem,
                                                               op0=ALU.mult, op1=ALU.mult),
                 reads=[b_mt, brs, k.b_pp], writes=[b_mn])
            pb, bpb = k.pb[j], k.b_pb[j]
            for c in range(8):
                S.op(S.pe, lambda: nc.tensor.transpose(out=pb[:, c * 128:(c + 1) * 128], in_=mn[:, c * 128:(c + 1) * 128],
                                                       identity=k.ident_bf),
                     reads=[b_mn, k.b_cbf], writes=[bpb], inc=(c == 7))
            S.op(S.act, lambda: nc.scalar.copy(out=memT[:, :, j * 128:(j + 1) * 128], in_=pb[:].rearrange("p (c t) -> p c t", c=8)),
                 reads=[bpb], writes=[b_memT])
        kng = k.ppv("kng")
        npf = [0]

        def next_pf():
            i = npf[0] % 6
            npf[0] += 1
            return k.pf[i], k.b_pf[i]
        for j in range(2):
            for t4 in range(4):
                p, bp = next_pf()
                for c in range(8):
                    S.op(S.pe, lambda: nc.tensor.matmul(p[:], lhsT=memT[:, c, j * 128:(j + 1) * 128], rhs=wkv[t4][:, c, :],
                                                        start=(c == 0), stop=(c == 7)),
                         reads=[b_memT, b_w[t4]], writes=[bp], inc=(c == 7))
                if t4 < 2:
                    for hh in range(2):
                        h = t4 * 2 + hh
                        rs, brs = rstd_of(p[:, hh * 256:(hh + 1) * 256], [bp], 256)
                        S.op(S.dve, lambda: nc.vector.scalar_tensor_tensor(out=kn[:], in0=p[:, hh * 256:(hh + 1) * 256], scalar=rs,
                                                                           in1=kng, op0=ALU.mult, op1=ALU.mult),
                             reads=[bp, brs, k.b_pp], writes=[b_kn])
                        pb, bpb = k.pb[hh], k.b_pb[hh]
                        for dc in range(2):
                            S.op(S.pe, lambda: nc.tensor.transpose(out=pb[:, dc * 128:(dc + 1) * 128], in_=kn[:, dc * 128:(dc + 1) * 128],
                                                                   identity=k.ident_bf),
                                 reads=[b_kn, k.b_cbf], writes=[bpb], inc=(dc == 1))
                        S.op(S.act, lambda: nc.scalar.copy(out=knT[:, h, :, j * 128:(j + 1) * 128],
                                                           in_=pb[:, 0:256].rearrange("p (c t) -> p c t", c=2)),
                             reads=[bpb], writes=[b_knT])
                else:
                    S.op(S.dve, lambda: nc.vector.tensor_copy(out=vmem[:, j, (t4 - 2) * 512:(t4 - 1) * 512], in_=p[:]),
                         reads=[bp], writes=[b_vmem])
        qx = [k.sb(f"xa_qx{i}", [128, 2, 512], BF16, st) for i in range(2)]
        b_qx = [Buf(), Buf()]
        sem_q = [S.new_sem("d_xaq0"), S.new_sem("d_xaq1")]
        yx = [k.sb(f"xa_yx{i}", [128, 2, 512], BF16, st) for i in range(2)]
        b_yx = [Buf(), Buf()]
        sem_y = [S.new_sem("d_xay0"), S.new_sem("d_xay1")]
        sq = k.sb("xa_sq", [128, 2, 512], BF16, st)
        rq = k.sb("xa_rq", [128, 512], F32, st)
        rq2 = k.sb("xa_rq2", [128, 512], F32, st)
        qn = k.sb("xa_qn", [128, 2, 512], BF16, st)
        pT = k.sb("xa_pT", [128, 2, 512], BF16, st)
        rden = k.sb("xa_rden", [128, 512], F32, st)
        b_sq, b_rq, b_rq2, b_qn, b_pT, b_rden = Buf(), Buf(), Buf(), Buf(), Buf(), Buf()
        qng = k.ppv("qng")
        it = 0
        for h in range(4):
            for tt in range(NTT):
                i = it % 2
                it += 1
                cols = slice(tt * 512, (tt + 1) * 512)
                S.dma(S.sp, sem_q[i], qx[i][:], k.qt_x[2 * h:2 * h + 2, :, cols].rearrange("c p t -> p c t"), writes=[b_qx[i]])
                S.op(S.act, lambda: nc.scalar.activation(out=sq[:], in_=qx[i][:], func=AF.Square), reads=[b_qx[i]], writes=[b_sq])
                p, bp = next_pf()
                for dc in range(2):
                    S.op(S.pe, lambda: nc.tensor.matmul(p[:], lhsT=k.ones_bf, rhs=sq[:, dc, :], start=(dc == 0), stop=(dc == 1)),
                         reads=[b_sq, k.b_cbf], writes=[bp], inc=(dc == 1))
                S.op(S.act, lambda: nc.scalar.activation(out=rq[:], in_=p[:], func=AF.Sqrt, bias=k.eps_t[:, 0:1], scale=1.0 / 256),
                     reads=[bp, k.b_eps], writes=[b_rq])
                S.op(S.dve, lambda: nc.vector.reciprocal(out=rq2[:], in_=rq[:]), reads=[b_rq], writes=[b_rq2])
                for dc in range(2):
                    S.op(S.dve, lambda: nc.vector.scalar_tensor_tensor(out=qn[:, dc, :], in0=qx[i][:, dc, :], scalar=qng[:, dc:dc + 1],
                                                                       in1=rq2[:], op0=ALU.mult, op1=ALU.mult),
                         reads=[b_qx[i], b_rq2, k.b_pp], writes=[b_qn])
                for mc in range(2):
                    p, bp = next_pf()
                    for dc in range(2):
                        S.op(S.pe, lambda: nc.tensor.matmul(p[:], lhsT=knT[:, h, dc, mc * 128:(mc + 1) * 128], rhs=qn[:, dc, :],
                                                            start=(dc == 0), stop=(dc == 1)),
                             reads=[b_knT, b_qn], writes=[bp], inc=(dc == 1))
                    S.op(S.act, lambda: nc.scalar.activation(out=pT[:, mc, :], in_=p[:], func=AF.Exp, scale=1.0 / 16),
                         reads=[bp], writes=[b_pT])
                p, bp = next_pf()
                for mc in range(2):
                    S.op(S.pe, lambda: nc.tensor.matmul(p[:], lhsT=k.ones_bf, rhs=pT[:, mc, :], start=(mc == 0), stop=(mc == 1)),
                         reads=[b_pT, k.b_cbf], writes=[bp], inc=(mc == 1))
                S.op(S.dve, lambda: nc.vector.reciprocal(out=rden[:], in_=p[:]), reads=[bp], writes=[b_rden])
                for dvc in range(2):
                    p, bp = next_pf()
                    for mc in range(2):
                        S.op(S.pe, lambda: nc.tensor.matmul(p[:], lhsT=vmem[:, mc, h * 256 + dvc * 128:h * 256 + (dvc + 1) * 128],
                                                            rhs=pT[:, mc, :], start=(mc == 0), stop=(mc == 1)),
                             reads=[b_vmem, b_pT], writes=[bp], inc=(mc == 1))
                    S.op(S.dve, lambda: nc.vector.tensor_tensor(out=yx[i][:, dvc, :], in0=p[:], in1=rden[:], op=ALU.mult),
                         reads=[bp, b_rden], writes=[b_yx[i]])
                S.dma(S.sp, sem_y[i], k.yt_x[2 * h:2 * h + 2, :, cols].rearrange("c p t -> p c t"), yx[i][:], reads=[b_yx[i]])
        S.barrier()


def stage_4a(k):
    nc, S = k.nc, k.S
    with ExitStack() as st:
        pre = hasattr(k, "Wsb")
        Wb = [k.Wsb if (pre and b == 0) else k.sb(f"a_W{b}", [128, 8, D], BF16, st) for b in range(4)]
        b_W = [k.b_Wsb if (pre and b == 0) else Buf() for b in range(4)]
        sem_w = [S.new_sem(f"d_4aw{i}") for i in range(4)]
        for b, src in enumerate((k.w_sb, k.w_ml, k.w_x, k.w_out)):
            if pre and b == 0:
                continue
            for hf in range(2):
                S.dma(S.pool, sem_w[b], Wb[b][:, :, hf * 512:(hf + 1) * 512],
                      src[:, hf * 512:(hf + 1) * 512].rearrange("(c p) n -> p c n", p=128), writes=[b_W[b]])
        yt = [k.sb(f"a_y{i}", [128, 8, 512], BF16, st) for i in range(2)]
        gt = [k.sb(f"a_g{i}", [128, 8, 512], BF16, st) for i in range(2)]
        b_yt = [Buf(), Buf()]
        b_gt = [Buf(), Buf()]
        sem_in = [S.new_sem("d_4ain0"), S.new_sem("d_4ain1")]
        mixed = k.sb("a_mixed", [128, 8, 512], F32, st)
        mixbf = k.sb("a_mixbf", [128, 8, 512], BF16, st)
        tmp = [k.sb(f"a_tmp{i}", [128, 512], F32, st) for i in range(2)]
        b_mixed = [Buf() for _ in range(8)]
        b_mixbf, b_tmp = Buf(), [Buf(), Buf()]
        xt = k.sb("a_xt", [128, 4, D], F32, st)
        b_xt = [Buf() for _ in range(4)]
        sem_x = S.new_sem("d_4ax")
        sem_x1 = S.new_sem("d_4ax1")
        h2 = k.sb("a_h2", [128, D], BF16, st)
        jk = k.sb("a_jk", [128, D], BF16, st)
        h2T = [k.sb(f"a_h2T{i}", [128, 8, 512], BF16, st) for i in range(2)]
        b_h2, b_jk, b_h2T = Buf(), Buf(), [Buf(), Buf()]
        sem_h = [S.new_sem("d_4ah0"), S.new_sem("d_4ah1")]
        sm = k.sb("a_sm", [128, 3 * NTB], F32, st)
        gmlp = k.ppv("gmlp")
        ysrc = (k.yt_sb, k.yt_ml, k.yt_x)
        npf = [0]
        ntmp = [0]
        itc = [0]
        h2b = [h2, k.sb("a_h2b", [128, D], BF16, st)]
        b_h2b = [b_h2, Buf()]

        def branch(tt, b):
            cols = slice(tt * 512, (tt + 1) * 512)
            i = itc[0] % 2
            itc[0] += 1
            S.dma(S.sp, sem_in[i], yt[i][:], ysrc[b][:, :, cols].rearrange("c p t -> p c t"), writes=[b_yt[i]])
            S.dma(S.sp, sem_in[i], gt[i][:], k.g_scr[b * 8:(b + 1) * 8, :, cols].rearrange("c p t -> p c t"), writes=[b_gt[i]])
            for n in range(8):
                p, bp = k.pf[npf[0] % 6], k.b_pf[npf[0] % 6]
                npf[0] += 1
                for fc in range(8):
                    S.op(S.pe, lambda: nc.tensor.matmul(p[:], lhsT=Wb[b][:, fc, n * 128:(n + 1) * 128], rhs=yt[i][:, fc, :],
                                                        start=(fc == 0), stop=(fc == 7)),
                         reads=[b_W[b], b_yt[i]], writes=[bp], inc=(fc == 7))
                if b == 0:
                    S.op(S.dve, lambda: nc.vector.tensor_tensor(out=mixed[:, n, :], in0=p[:], in1=gt[i][:, n, :], op=ALU.mult),
                         reads=[bp, b_gt[i]], writes=[b_mixed[n]])
                else:
                    ti = ntmp[0] % 2
                    ntmp[0] += 1
                    S.op(S.dve, lambda: nc.vector.tensor_tensor(out=tmp[ti][:], in0=p[:], in1=gt[i][:, n, :], op=ALU.mult),
                         reads=[bp, b_gt[i]], writes=[b_tmp[ti]])
                    S.op(S.pool, lambda: nc.gpsimd.tensor_tensor(out=mixed[:, n, :], in0=mixed[:, n, :], in1=tmp[ti][:], op=ALU.add),
                         reads=[b_tmp[ti], b_mixed[n]], writes=[b_mixed[n]])
                    if b == 2:
                        S.op(S.act, lambda: nc.scalar.copy(out=mixbf[:, n, :], in_=mixed[:, n, :]),
                             reads=[b_mixed[n]], writes=[b_mixbf])

        def transposes(tt, j):
            hi = tt % 2
            hb_, bhb = h2b[j % 2], b_h2b[j % 2]
            pb, bpb = k.pb[j % 2], k.b_pb[j % 2]
            for c in range(8):
                S.op(S.pe, lambda: nc.tensor.transpose(out=pb[:, c * 128:(c + 1) * 128], in_=hb_[:, c * 128:(c + 1) * 128],
                                                       identity=k.ident_bf),
                     reads=[bhb, k.b_cbf], writes=[bpb], inc=(c == 7))
            S.op(S.act, lambda: nc.scalar.copy(out=h2T[hi][:, :, j * 128:(j + 1) * 128], in_=pb[:].rearrange("p (c t) -> p c t", c=8)),
                 reads=[bpb], writes=[b_h2T[hi]])

        def outproj(tt):
            cols = slice(tt * 512, (tt + 1) * 512)
            hi = tt % 2
            S.dma(S.sp, sem_x, xt[:], k.x[cols, :].rearrange("(j p) d -> p j d", p=128), writes=b_xt)
            for j in range(4):
                tb = tt * 4 + j
                for hf in range(2):
                    p, bp = k.pf[npf[0] % 6], k.b_pf[npf[0] % 6]
                    npf[0] += 1
                    for fc in range(8):
                        S.op(S.pe, lambda: nc.tensor.matmul(p[:], lhsT=mixbf[:, fc, j * 128:(j + 1) * 128],
                                                            rhs=Wb[3][:, fc, hf * 512:(hf + 1) * 512],
                                                            start=(fc == 0), stop=(fc == 7)),
                             reads=[b_W[3], b_mixbf], writes=[bp], inc=(fc == 7))
                    S.op(S.dve, lambda: nc.vector.tensor_tensor(out=xt[:, j, hf * 512:(hf + 1) * 512], in0=p[:],
                                                                in1=xt[:, j, hf * 512:(hf + 1) * 512], op=ALU.add),
                         reads=[bp, b_xt[j]], writes=[b_xt[j]])
                if j > 0:
                    transposes(tt, j - 1)
                ss, rm, rs = sm[:, 3 * tb:3 * tb + 1], sm[:, 3 * tb + 1:3 * tb + 2], sm[:, 3 * tb + 2:3 * tb + 3]
                b_ss, b_rm, b_rs = Buf(), Buf(), Buf()
                hb_, bhb = h2b[j % 2], b_h2b[j % 2]
                S.op(S.act, lambda: nc.scalar.activation(out=jk[:], in_=xt[:, j, :], func=AF.Square, accum_out=ss),
                     reads=[b_xt[j]], writes=[b_jk, b_ss])
                S.op(S.act, lambda: nc.scalar.activation(out=rm, in_=ss, func=AF.Sqrt, bias=k.eps_t[:, 0:1], scale=1.0 / D),
                     reads=[b_ss, k.b_eps], writes=[b_rm])
                S.op(S.dve, lambda: nc.vector.reciprocal(out=rs, in_=rm), reads=[b_rm], writes=[b_rs])
                S.op(S.dve, lambda: nc.vector.scalar_tensor_tensor(out=hb_[:], in0=xt[:, j, :], scalar=rs, in1=gmlp,
                                                                   op0=ALU.mult, op1=ALU.mult),
                     reads=[b_xt[j], b_rs, k.b_pp], writes=[bhb])
            transposes(tt, 3)
            S.dma(S.pool, sem_x1, k.x1_scr[cols, :].rearrange("(j p) d -> p j d", p=128), xt[:], reads=b_xt)
            S.dma(S.pool, sem_h[hi], k.h2t_scr[:, :, cols].rearrange("c p t -> p c t"), h2T[hi][:], reads=[b_h2T[hi]])

        for b in range(3):
            branch(0, b)
        for tt in range(NTT):
            if tt + 1 < NTT:
                branch(tt + 1, 0)
            outproj(tt)
            if tt + 1 < NTT:
                branch(tt + 1, 1)
                branch(tt + 1, 2)
        S.barrier()


def stage_4b(k):
    nc, S = k.nc, k.S
    with ExitStack() as st:
        W1 = k.sb("b_W1", [128, 8, 4096], BF16, st)
        W2 = k.sb("b_W2", [128, 32, D], BF16, st)
        b_W1 = [Buf() for _ in range(4)]
        b_W2 = [Buf() for _ in range(4)]
        sem_w = [S.new_sem(f"d_4bw{i}") for i in range(8)]
        for q in range(4):
            S.dma(S.pool, sem_w[q], W1[:, :, q * 1024:(q + 1) * 1024],
                  k.w_ff1[:, q * 1024:(q + 1) * 1024].rearrange("(c p) n -> p c n", p=128), writes=[b_W1[q]])
        for q in range(4):
            S.dma(S.pool, sem_w[4 + q], W2[:, q * 8:(q + 1) * 8, :],
                  k.w_ff2[q * 1024:(q + 1) * 1024, :].rearrange("(c p) n -> p c n", p=128), writes=[b_W2[q]])
        h2T = [k.sb(f"b_h2T{i}", [128, 8, 512], BF16, st) for i in range(2)]
        b_h2T = [Buf(), Buf()]
        sem_h = [S.new_sem("d_4bh0"), S.new_sem("d_4bh1")]
        x1b = [k.sb(f"b_x1{i}", [128, D], F32, st) for i in range(2)]
        b_x1 = [Buf(), Buf()]
        sem_x = [S.new_sem("d_4bx0"), S.new_sem("d_4bx1")]
        sem_o = [S.new_sem("d_4bo0"), S.new_sem("d_4bo1")]
        aT = k.sb("b_aT", [128, 32, 512], BF16, st)
        b_aT = [Buf() for _ in range(32)]
        rr = [k.sb(f"b_r{i}", [128, 512], F32, st) for i in range(2)]
        b_rr = [Buf(), Buf()]
        npf = 0
        nx = 0
        for tt in range(NTT):
            cols = slice(tt * 512, (tt + 1) * 512)
            hi = tt % 2
            S.dma(S.sp, sem_h[hi], h2T[hi][:], k.h2t_scr[:, :, cols].rearrange("c p t -> p c t"), writes=[b_h2T[hi]])
            for fc in range(32):
                p, bp = k.pf[npf % 6], k.b_pf[npf % 6]
                npf += 1
                for c in range(8):
                    S.op(S.pe, lambda: nc.tensor.matmul(p[:], lhsT=W1[:, c, fc * 128:(fc + 1) * 128], rhs=h2T[hi][:, c, :],
                                                        start=(c == 0), stop=(c == 7)),
                         reads=[b_W1[fc // 8], b_h2T[hi]], writes=[bp], inc=(c == 7))
                ri = fc % 2
                S.op(S.act, lambda: nc.scalar.activation(out=rr[ri][:], in_=p[:], func=AF.Relu), reads=[bp], writes=[b_rr[ri]])
                S.op(S.dve, lambda: nc.vector.tensor_tensor(out=aT[:, fc, :], in0=p[:], in1=rr[ri][:], op=ALU.mult),
                     reads=[bp, b_rr[ri]], writes=[b_aT[fc]])
            for j in range(4):
                tb = tt * 4 + j
                xi = nx % 2
                nx += 1
                S.dma(S.sp, sem_x[xi], x1b[xi][:], k.x1_scr[tb * 128:(tb + 1) * 128, :], writes=[b_x1[xi]])
                for hf in range(2):
                    p, bp = k.pf[npf % 6], k.b_pf[npf % 6]
                    npf += 1
                    for fc in range(32):
                        S.op(S.pe, lambda: nc.tensor.matmul(p[:], lhsT=aT[:, fc, j * 128:(j + 1) * 128],
                                                            rhs=W2[:, fc, hf * 512:(hf + 1) * 512],
                                                            start=(fc == 0), stop=(fc == 31)),
                             reads=[b_W2[fc // 8], b_aT[fc]], writes=[bp], inc=(fc == 31))
                    S.op(S.dve, lambda: nc.vector.tensor_tensor(out=x1b[xi][:, hf * 512:(hf + 1) * 512], in0=p[:],
                                                                in1=x1b[xi][:, hf * 512:(hf + 1) * 512], op=ALU.add),
                         reads=[bp, b_x1[xi]], writes=[b_x1[xi]])
                S.dma(S.pool, sem_o[xi], k.out[tb * 128:(tb + 1) * 128, :], x1b[xi][:], reads=[b_x1[xi]], is_output=True)
        S.barrier()


def ml_prep(k, st, T=None, stp=None, phase=0):
    nc, S = k.nc, k.S
    if phase == 1:
        return _ml_prep_compute(k, T, stp)
    T = K()
    T.tab = k.sb("ml_tab", [128, 3, 32, 4], F32, st)
    T.Mb = k.sb("ml_Mb", [128, 33, 4], F32, st)
    T.uT = k.sb("ml_u", [128, 128], F32, st)
    T.wT = k.sb("ml_w", [128, 128], F32, st)
    T.flT = k.sb("ml_fl", [128, 128], F32, st)
    T.decT = k.sb("ml_dec", [128, 128], F32, st)
    nl16 = k.sb("ml_nl16", [128, 1], F32, st)
    tmp = k.sb("ml_tmp", [128, 128], F32, st)
    tab, Mb = T.tab, T.Mb
    b_tab, b_Mb, b_c, b_tmp = Buf(), Buf(), Buf(), Buf()
    T.b_u, T.b_w, T.b_fl, T.b_dec = Buf(), Buf(), Buf(), Buf()
    T.nl16, T.tmp, T.b_tab, T.b_Mb, T.b_c, T.b_tmp = nl16, tmp, b_tab, b_Mb, b_c, b_tmp
    return T


def _ml_prep_compute(k, T, st2):
    nc, S = k.nc, k.S
    tab, Mb, nl16, tmp = T.tab, T.Mb, T.nl16, T.tmp
    b_tab, b_Mb, b_c, b_tmp = T.b_tab, T.b_Mb, T.b_c, T.b_tmp
    S.op(S.dve, lambda: nc.vector.memset(nl16[:], -LN16), writes=[b_c])
    S.op(S.dve, lambda: nc.vector.memset(Mb[:, 0, :], 0.0), writes=[b_Mb])
    if True:
        fp = k.sb("ml_fp", [4, S_LEN], F32, st2)
        ip = k.sb("ml_ip", [4, S_LEN], F32, st2)
        Fn = k.sb("ml_Fn", [4, S_LEN], F32, st2)
        on = k.sb("ml_on", [4, S_LEN], F32, st2)
        b_fp, b_ip, b_Fn, b_on = Buf(), Buf(), Buf(), Buf()
        sg = S.new_sem("d_mlg")
        S.dma(S.sp, sg, fp[:], k.gif[4:8, :], writes=[b_fp])
        S.dma(S.sp, sg, ip[:], k.gif[0:4, :], writes=[b_ip])
        S.op(S.dve, lambda: nc.vector.memset(on[:], 1.0), writes=[b_on])
        S.op(S.act, lambda: nc.scalar.activation(out=fp[:], in_=fp[:], func=AF.Exp, scale=-1.0), reads=[b_fp], writes=[b_fp])
        S.op(S.act, lambda: nc.scalar.activation(out=fp[:], in_=fp[:], func=AF.Ln, bias=1.0), reads=[b_fp], writes=[b_fp])
        S.op(S.dve, lambda: nc.vector.tensor_tensor_scan(out=Fn[:], data0=on[:], data1=fp[:], initial=0.0,
                                                         op0=ALU.mult, op1=ALU.add),
             reads=[b_on, b_fp], writes=[b_Fn])
        S.op(S.dve, lambda: nc.vector.tensor_tensor(out=ip[:], in0=ip[:], in1=Fn[:], op=ALU.add),
             reads=[b_ip, b_Fn], writes=[b_ip])
        S.op(S.dve, lambda: nc.vector.tensor_tensor_scan(out=fp[:], data0=ip[:], data1=ip[:], initial=0.0,
                                                         op0=ALU.max, op1=ALU.max),
             reads=[b_ip], writes=[b_fp])
        pt, bpt = k.pf[0], k.b_pf[0]
        ptv = pt[:, 0:384].rearrange("p (q c h) -> p q c h", q=3, c=32)
        idf = k.ppv("ident")
        for q, (X, bX) in enumerate(((Fn, b_Fn), (ip, b_ip), (fp, b_fp))):
            for c in range(32):
                S.op(S.pe, lambda: nc.tensor.transpose(out=ptv[:, q, c, :], in_=X[0:4, c * 128:(c + 1) * 128],
                                                       identity=idf[0:4, 0:4]),
                     reads=[bX, k.b_pp], writes=[bpt], inc=(q == 2 and c == 31))
        S.op(S.dve, lambda: nc.vector.tensor_copy(out=tab[:], in_=ptv), reads=[bpt], writes=[b_tab])
    pm, bpm = k.pf[1], k.b_pf[1]
    o_sel = PP["sel127"][0]
    S.op(S.pe, lambda: nc.tensor.matmul(pm[:, 0:128], lhsT=k.ppt[:, o_sel:o_sel + 128],
                                        rhs=tab[:, 2].rearrange("p c h -> p (c h)"), start=True, stop=True),
         reads=[b_tab, k.b_pp], writes=[bpm])
    S.op(S.dve, lambda: nc.vector.tensor_copy(out=Mb[:, 1:33, :], in_=pm[:, 0:128].rearrange("p (c h) -> p c h", c=32)),
         reads=[bpm], writes=[b_Mb])

    def table(dst, bd, in0, in1, bias):
        S.op(S.dve, lambda: nc.vector.tensor_tensor(out=tmp[:].rearrange("p (c h) -> p c h", c=32), in0=in0, in1=in1, op=ALU.subtract),
             reads=[b_tab, b_Mb], writes=[b_tmp])
        if bias:
            S.op(S.act, lambda: nc.scalar.activation(out=dst[:], in_=tmp[:], func=AF.Exp, bias=nl16[:, 0:1]),
                 reads=[b_tmp, b_c], writes=[bd])
        else:
            S.op(S.act, lambda: nc.scalar.activation(out=dst[:], in_=tmp[:], func=AF.Exp), reads=[b_tmp], writes=[bd])
    table(T.uT, T.b_u, tab[:, 1], Mb[:, 0:32, :], True)
    table(T.wT, T.b_w, tab[:, 1], Mb[:, 1:33, :], True)
    table(T.flT, T.b_fl, tab[:, 0], Mb[:, 0:32, :], False)
    table(T.decT, T.b_dec, Mb[:, 0:32, :], Mb[:, 1:33, :], False)
    return T


def xa_prep(k, st, X=None, st2=None, phase=0):
    nc, S = k.nc, k.S
    if phase == 0:
        X = K()
        X.knT = k.sb("xa_knT", [128, 4, 2, N_MEM], BF16, st)
        X.vmem = k.sb("xa_vmem", [128, 2, D], BF16, st)
        X.b_knT, X.b_vmem = Buf(), Buf()
        return X
    if phase == 2:
        return X.compute()
    knT, vmem = X.knT, X.vmem
    if True:
        mt = k.sb("xa_mt", [128, 2, D], F32, st2)
        mn = k.sb("xa_mn", [128, D], BF16, st2)
        jk = k.sb("xa_jk", [128, D], BF16, st2)
        memT = k.sb("xa_memT", [128, 8, N_MEM], BF16, st2)
        kn = k.sb("xa_kn", [128, 256], BF16, st2)
        sm = k.sb("xa_sm", [128, 64], F32, st2)
        wkv = [k.sb(f"xa_w{i}", [128, 8, 512], BF16, st2) for i in range(4)]
        b_mt, b_mn, b_jk, b_memT, b_kn = Buf(), Buf(), Buf(), Buf(), Buf()
        b_w = [Buf() for _ in range(4)]
        sem = S.new_sem("d_xa0")
        sem_w = [S.new_sem(f"d_xaw{i}") for i in range(4)]
        S.dma(S.sp, sem, mt[:], k.mem.rearrange("(j p) d -> p j d", p=128), writes=[b_mt])
        for i in range(4):
            S.dma(S.pool, sem_w[i], wkv[i][:], k.w_mem_kv[:, i * 512:(i + 1) * 512].rearrange("(c p) n -> p c n", p=128),
                  writes=[b_w[i]])
        nsm = [0]

        def smcol():
            c = nsm[0]
            nsm[0] += 1
            return sm[:, c:c + 1], Buf()

        def compute():
            _xa_compute()
        X.compute = compute

    def _xa_compute():
        def rstd_of(src_ap, src_bufs, n):
            ss, bss = smcol()
            rm, brm = smcol()
            rs, brs = smcol()
            S.op(S.act, lambda: nc.scalar.activation(out=jk[:, 0:n], in_=src_ap, func=AF.Square, accum_out=ss),
                 reads=src_bufs, writes=[b_jk, bss])
            S.op(S.act, lambda: nc.scalar.activation(out=rm, in_=ss, func=AF.Ln, bias=k.eps_t[:, 0:1], scale=1.0 / n),
                 reads=[bss, k.b_eps], writes=[brm])
            S.op(S.act, lambda: nc.scalar.activation(out=rs, in_=rm, func=AF.Exp, scale=-0.5), reads=[brm], writes=[brs])
            return rs, brs

        gmem = k.ppv("gmem")
        for j in range(2):
            rs, brs = rstd_of(mt[:, j, :], [b_mt], D)
            S.op(S.dve, lambda: nc.vector.scalar_tensor_tensor(out=mn[:], in0=mt[:, j, :], scalar=rs, in1=gmem,
                                                               op0=ALU.mult, op1=ALU.mult),
                 reads=[b_mt, brs, k.b_pp], writes=[b_mn])
            pb, bpb = k.pb[j], k.b_pb[j]
            for c in range(8):
                S.op(S.pe, lambda: nc.tensor.transpose(out=pb[:, c * 128:(c + 1) * 128], in_=mn[:, c * 128:(c + 1) * 128],
                                                       identity=k.ident_bf),
                     reads=[b_mn, k.b_cbf], writes=[bpb], inc=(c == 7))
            S.op(S.act, lambda: nc.scalar.copy(out=memT[:, :, j * 128:(j + 1) * 128], in_=pb[:].rearrange("p (c t) -> p c t", c=8)),
                 reads=[bpb], writes=[b_memT])
        kng = k.ppv("kng")
        npf = [0]
        for j in range(2):
            for t4 in range(4):
                p, bp = k.pf[npf[0] % 6], k.b_pf[npf[0] % 6]
                npf[0] += 1
                for c in range(8):
                    S.op(S.pe, lambda: nc.tensor.matmul(p[:], lhsT=memT[:, c, j * 128:(j + 1) * 128], rhs=wkv[t4][:, c, :],
                                                        start=(c == 0), stop=(c == 7)),
                         reads=[b_memT, b_w[t4]], writes=[bp], inc=(c == 7))
                if t4 < 2:
                    for hh in range(2):
                        h = t4 * 2 + hh
                        rs, brs = rstd_of(p[:, hh * 256:(hh + 1) * 256], [bp], 256)
                        S.op(S.dve, lambda: nc.vector.scalar_tensor_tensor(out=kn[:], in0=p[:, hh * 256:(hh + 1) * 256], scalar=rs,
                                                                           in1=kng, op0=ALU.mult, op1=ALU.mult),
                             reads=[bp, brs, k.b_pp], writes=[b_kn])
                        pb, bpb = k.pb[hh], k.b_pb[hh]
                        for dc in range(2):
                            S.op(S.pe, lambda: nc.tensor.transpose(out=pb[:, dc * 128:(dc + 1) * 128], in_=kn[:, dc * 128:(dc + 1) * 128],
                                                                   identity=k.ident_bf),
                                 reads=[b_kn, k.b_cbf], writes=[bpb], inc=(dc == 1))
                        S.op(S.act, lambda: nc.scalar.copy(out=knT[:, h, :, j * 128:(j + 1) * 128],
                                                           in_=pb[:, 0:256].rearrange("p (c t) -> p c t", c=2)),
                             reads=[bpb], writes=[X.b_knT])
                else:
                    S.op(S.dve, lambda: nc.vector.tensor_copy(out=vmem[:, j, (t4 - 2) * 512:(t4 - 1) * 512], in_=p[:]),
                         reads=[bp], writes=[X.b_vmem])
    return X


def gen_sb(k, st, zz, b_z, oo, b_o):
    nc, S = k.nc, k.S
    NS = 2
    tiles_of = [[7, 4, 3, 0], [6, 5, 2, 1]]
    qT = [k.sb(f"sb_q{i}", [128, S_LEN], BF16, st) for i in range(2)]
    kT = [k.sb(f"sb_k{i}", [128, S_LEN], BF16, st) for i in range(2)]
    vv = [k.sb(f"sb_v{i}", [128, NTB, 128], BF16, st) for i in range(2)]
    b_q = [Buf() for _ in range(2)]
    b_k = [Buf() for _ in range(2)]
    b_v = [Buf() for _ in range(2)]
    sem_in = [S.new_sem(f"d_sbin{i}") for i in range(2)]
    yT = [k.sb(f"sb_y{i}", [128, S_LEN], BF16, st) for i in range(2)]
    b_y = [Buf() for _ in range(2)]
    sem_y = [S.new_sem(f"d_sby{i}") for i in range(2)]
    e_sb = k.sb("sb_e", [128, NS, 512], F32, st)
    b_e = [Buf() for s in range(NS)]
    nl = [k.sb(f"sb_nl{j}", [128, NS, 512], BF16, st) for j in range(2)]
    b_nl = [[Buf() for s in range(NS)] for j in range(2)]
    at = [k.sb(f"sb_at{j}", [128, NS, 512], BF16, st) for j in range(2)]
    b_at = [[Buf() for s in range(NS)] for j in range(2)]
    sacc = k.sb("sb_sacc", [128, NS, 512], BF16, st)
    b_sacc = [Buf() for s in range(NS)]

    def load_head(h):
        i = h % 2
        S.dma(S.sp, sem_in[i], qT[i][:], k.qt_sb[h], writes=[b_q[i]])
        S.dma(S.sp, sem_in[i], kT[i][:], k.kt_sb[h], writes=[b_k[i]])
        S.dma(S.sp, sem_in[i], vv[i][:], k.v_sb[:, h * 128:(h + 1) * 128].rearrange("(j p) n -> p j n", p=128),
              writes=[b_v[i]])

    def dummies(n):
        for _ in range(n):
            nc.tensor.matmul(k.dummy_bank[:], lhsT=k.ident_bf, rhs=k.sbmask_bf[:, 0:512], start=True, stop=True)

    def mm1(h, s, qi, kb):
        i = h % 2
        c0 = 128 * max(kb - 4 * qi, 0)
        S.op(S.pe, lambda: nc.tensor.matmul(zz[:, s, c0:512], lhsT=kT[i][:, kb * 128:(kb + 1) * 128],
                                            rhs=qT[i][:, qi * 512 + c0:(qi + 1) * 512], start=True, stop=True),
             reads=[b_k[i], b_q[i]], writes=[b_z[s]])

    load_head(0)
    for h in range(8):
        if h + 1 < 8:
            load_head(h + 1)
        i = h % 2
        steps = [[(qi, kb) for qi in tiles_of[s] for kb in range(4 * qi + 3, -1, -1)] for s in range(NS)]
        nsteps = len(steps[0])
        assert all(len(x) == nsteps for x in steps)
        for s in range(NS):
            mm1(h, s, *steps[s][0])
        for n in range(nsteps):
            par = n % 2
            info = []
            for s in range(NS):
                qi, kb = steps[s][n]
                info.append((qi, kb, kb - 4 * qi, kb == 4 * qi + 3, kb == 0))
            cs = [slice(128 * max(info[s][2], 0), 512) for s in range(NS)]
            for s in range(NS):
                qi, kb, r, first, last = info[s]
                S.op(S.act, lambda: nc.scalar.activation(out=e_sb[:, s, cs[s]], in_=zz[:, s, cs[s]], func=AF.Exp),
                     reads=[b_z[s]], writes=[b_e[s]])
                if not first:
                    S.op(S.pe, lambda: nc.tensor.matmul(zz[:, s, cs[s]], lhsT=k.nones_bf, rhs=sacc[:, s, cs[s]], start=False, stop=False,
                                                        skip_group_check=True),
                         reads=[b_sacc[s], k.b_cbf], writes=[b_z[s]], inc=False)
            for s in range(NS):
                S.op(S.act, lambda: nc.scalar.activation(out=nl[par][:, s, cs[s]], in_=e_sb[:, s, cs[s]], func=AF.Ln, bias=1.0),
                     reads=[b_e[s]], writes=[b_nl[par][s]])
            yield
            for s in range(NS):
                qi, kb, r, first, last = info[s]
                if r >= 0:
                    S.op(S.dve, lambda: nc.vector.tensor_tensor(out=nl[par][:, s, cs[s]], in0=nl[par][:, s, cs[s]],
                                                                in1=k.sbmask_bf[:, r * 512 + cs[s].start:(r + 1) * 512], op=ALU.mult),
                         reads=[b_nl[par][s], k.b_cbf], writes=[b_nl[par][s]])
            for s in range(NS):
                qi, kb, r, first, last = info[s]
                S.op(S.pe, lambda: nc.tensor.matmul(zz[:, s, cs[s]], lhsT=k.ntri_bf, rhs=nl[par][:, s, cs[s]], start=False, stop=True,
                                                    skip_group_check=True),
                     reads=[b_nl[par][s], k.b_cbf], writes=[b_z[s]])
                dummies(k.ndummy)
            yield
            for s in range(NS):
                S.op(S.act, lambda: nc.scalar.activation(out=at[par][:, s, cs[s]], in_=zz[:, s, cs[s]], func=AF.Exp),
                     reads=[b_z[s]], writes=[b_at[par][s]])
            for s in range(NS):
                qi, kb, r, first, last = info[s]
                if not last:
                    if first:
                        S.op(S.pool, lambda: nc.gpsimd.memset(sacc[:, s, 0:384], 0.0), writes=[b_sacc[s]])
                        S.op(S.pool, lambda: nc.gpsimd.tensor_copy(out=sacc[:, s, cs[s]], in_=nl[par][:, s, cs[s]]),
                             reads=[b_nl[par][s]], writes=[b_sacc[s]])
                    else:
                        S.op(S.pool, lambda: nc.gpsimd.tensor_tensor(out=sacc[:, s, cs[s]], in0=sacc[:, s, cs[s]], in1=nl[par][:, s, cs[s]],
                                                                    op=ALU.add),
                             reads=[b_nl[par][s], b_sacc[s]], writes=[b_sacc[s]])
            yield
            for s in range(NS):
                qi, kb, r, first, last = info[s]
                if n + 1 < nsteps:
                    mm1(h, s, *steps[s][n + 1])
                if r >= 0:
                    S.op(S.dve, lambda: nc.vector.tensor_tensor(out=at[par][:, s, cs[s]], in0=at[par][:, s, cs[s]],
                                                                in1=k.sbmask_bf[:, r * 512 + cs[s].start:(r + 1) * 512], op=ALU.mult),
                         reads=[b_at[par][s], k.b_cbf], writes=[b_at[par][s]])
                S.op(S.pe, lambda: nc.tensor.matmul(oo[:, s, cs[s]], lhsT=vv[i][:, kb, :], rhs=at[par][:, s, cs[s]],
                                                    start=first, stop=last, skip_group_check=True),
                     reads=[b_v[i], b_at[par][s]], writes=[b_o[s]])
                dummies(k.ndummy + 1)
                if last:
                    S.op(S.dve, lambda: nc.vector.tensor_copy(out=yT[i][:, qi * 512:(qi + 1) * 512], in_=oo[:, s, :]),
                         reads=[b_o[s]], writes=[b_y[i]])
        S.dma(S.sp, sem_y[i], k.yt_sb[h], yT[i][:], reads=[b_y[i]])


def gen_ml(k, st, T, bankA, bbA, bankB, bbB, bankN, bbN, pbT, bpbT):
    nc, S = k.nc, k.S
    qTt = [k.sb(f"ml_q{i}", [128, 2, 512], BF16, st) for i in range(2)]
    kTt = [k.sb(f"ml_k{i}", [128, 2, 512], BF16, st) for i in range(2)]
    ktm = [k.sb(f"ml_kt{i}", [128, 4, 256], BF16, st) for i in range(2)]
    vtm = [k.sb(f"ml_vt{i}", [128, 4, 256], BF16, st) for i in range(2)]
    otm = [k.sb(f"ml_ot{i}", [128, 4, 256], BF16, st) for i in range(2)]
    b_in = [Buf() for i in range(2)]
    sem_in = [S.new_sem(f"d_mlin{i}") for i in range(2)]
    yTt = [k.sb(f"ml_y{i}", [128, 2, 512], BF16, st) for i in range(2)]
    b_y = [Buf() for i in range(2)]
    sem_y = [S.new_sem(f"d_mly{i}") for i in range(2)]
    PT = [k.sb(f"ml_PT{i}", [128, 128], BF16, st) for i in range(2)]
    vu = [k.sb(f"ml_vu{i}", [128, 264], BF16, st) for i in range(2)]
    vw = [k.sb(f"ml_vw{i}", [128, 264], BF16, st) for i in range(2)]
    go = [k.sb(f"ml_go{i}", [128, 256], F32, st) for i in range(2)]
    b_PT = [Buf() for i in range(2)]
    b_vu = [Buf() for i in range(2)]
    b_vw = [Buf() for i in range(2)]
    b_go = [Buf() for i in range(2)]
    C32 = k.sb("ml_C32", [128, 2, 264], F32, st)
    Cbf = [k.sb(f"ml_Cbf{i}", [128, 2, 264], BF16, st) for i in range(2)]
    b_C32 = Buf()
    b_Cbf = [Buf() for i in range(2)]
    yb = [k.sb(f"ml_yb{i}", [128, 256], BF16, st) for i in range(2)]
    jk = k.sb("ml_jk", [128, 256], BF16, st)
    sm = [k.sb(f"ml_sm{i}", [128, 8], F32, st) for i in range(2)]
    b_yb = [Buf() for i in range(2)]
    b_jk = Buf()
    b_sm = [[Buf() for _ in range(8)] for i in range(2)]
    mlng = k.ppv("mlng")

    def load_group(h, g, gi):
        sm_, bb = sem_in[gi], [b_in[gi]]
        cols = slice(g * 512, (g + 1) * 512)
        S.dma(S.sp, sm_, qTt[gi][:], k.qt_ml[2 * h:2 * h + 2, :, cols].rearrange("c p t -> p c t"), writes=bb)
        S.dma(S.sp, sm_, kTt[gi][:], k.kt_ml[2 * h:2 * h + 2, :, cols].rearrange("c p t -> p c t"), writes=bb)
        for dst, src in ((ktm, k.k_ml), (vtm, k.v_ml), (otm, k.o_ml)):
            S.dma(S.sp, sm_, dst[gi][:], src[cols, h * 256:(h + 1) * 256].rearrange("(j p) n -> p j n", p=128), writes=bb)

    ng = 0
    load_group(0, 0, 0)
    for h in range(4):
        for g in range(8):
            gi = ng % 2
            ng += 1
            if g + 1 < 8:
                load_group(h, g + 1, ng % 2)
            elif h + 1 < 4:
                load_group(h + 1, 0, ng % 2)
            for j in range(4):
                c = g * 4 + j
                par = c % 2
                col = c * 4 + h
                bin_ = b_in[gi]
                tsl = slice(j * 128, (j + 1) * 128)
                uc = T.uT[:, col:col + 1]
                wc = T.wT[:, col:col + 1]
                flc = T.flT[:, col:col + 1]
                dcc = T.decT[:, col:col + 1]
                smt = sm[par]
                bsm = b_sm[par]
                ps_s = bankA[:, 264:392]
                ps_c = [bankA[:, 0:257], bankB[:, 0:257]]
                bbc = [bbA, bbB]
                ps_n = bankN[:, 0:257]
                for dc in range(2):
                    S.op(S.pe, lambda: nc.tensor.matmul(ps_s, lhsT=kTt[gi][:, dc, tsl], rhs=qTt[gi][:, dc, tsl],
                                                        start=(dc == 0), stop=(dc == 1)),
                         reads=[bin_], writes=[bbA], inc=(dc == 1))
                for (dst, bd, sc, bs) in ((vu[par], b_vu[par], uc, T.b_u), (vw[par], b_vw[par], wc, T.b_w)):
                    S.op(S.dve, lambda: nc.vector.tensor_scalar(out=dst[:, 0:256], in0=vtm[gi][:, j, :], scalar1=sc, scalar2=None,
                                                                op0=ALU.mult),
                         reads=[bin_, bs], writes=[bd])
                    S.op(S.dve, lambda: nc.vector.tensor_copy(out=dst[:, 256:257], in_=sc), reads=[bs], writes=[bd])
                S.op(S.pool, lambda: nc.gpsimd.tensor_tensor(out=go[par][:], in0=otm[gi][:, j, :], in1=mlng[:, h * 256:(h + 1) * 256],
                                                            op=ALU.mult),
                     reads=[bin_, k.b_pp], writes=[b_go[par]])
                yield
                S.op(S.dve, lambda: nc.vector.tensor_tensor(out=PT[par][:], in0=ps_s, in1=k.mlmask_bf, op=ALU.mult),
                     reads=[bbA, k.b_cbf], writes=[b_PT[par]])
                for dc in range(2):
                    S.op(S.pe, lambda: nc.tensor.matmul(ps_c[dc], lhsT=ktm[gi][:, j, dc * 128:(dc + 1) * 128],
                                                        rhs=vw[par][:, 0:257], start=True, stop=True),
                         reads=[bin_, b_vw[par]], writes=[bbc[dc]])
                yield
                S.op(S.pe, lambda: nc.tensor.matmul(ps_n, lhsT=PT[par][:], rhs=vu[par][:, 0:257], start=True, stop=(c == 0)),
                     reads=[b_PT[par], b_vu[par]], writes=[bbN], inc=(c == 0))
                if c > 0:
                    cb, bcb = Cbf[1 - par], b_Cbf[1 - par]
                    for dc in range(2):
                        S.op(S.pe, lambda: nc.tensor.matmul(ps_n, lhsT=qTt[gi][:, dc, tsl], rhs=cb[:, dc, 0:257],
                                                            start=False, stop=(dc == 1)),
                             reads=[bin_, bcb], writes=[bbN], inc=(dc == 1))
                for dc in range(2):
                    if c == 0:
                        S.op(S.dve, lambda: nc.vector.tensor_copy(out=C32[:, dc, 0:257], in_=ps_c[dc]),
                             reads=[bbc[dc]], writes=[b_C32])
                    else:
                        S.op(S.dve, lambda: nc.vector.scalar_tensor_tensor(out=C32[:, dc, 0:257], in0=C32[:, dc, 0:257],
                                                                           scalar=dcc, in1=ps_c[dc], op0=ALU.mult, op1=ALU.add),
                             reads=[bbc[dc], b_C32, T.b_dec], writes=[b_C32])
                yield
                S.op(S.pool, lambda: nc.gpsimd.tensor_copy(out=Cbf[par][:, :, 0:257], in_=C32[:, :, 0:257]),
                     reads=[b_C32], writes=[b_Cbf[par]])
                S.op(S.act, lambda: nc.scalar.activation(out=smt[:, 0:1], in_=bankN[:, 256:257], func=AF.Abs),
                     reads=[bbN], writes=[bsm[0]])
                S.op(S.act, lambda: nc.scalar.activation(out=jk[:], in_=bankN[:, 0:256], func=AF.Square, accum_out=smt[:, 1:2]),
                     reads=[bbN], writes=[b_jk, bsm[1]])
                yield
                S.op(S.dve, lambda: nc.vector.tensor_tensor(out=smt[:, 2:3], in0=smt[:, 0:1], in1=flc, op=ALU.max),
                     reads=[bsm[0], T.b_fl], writes=[bsm[2]])
                S.op(S.dve, lambda: nc.vector.tensor_scalar(out=smt[:, 3:4], in0=smt[:, 2:3], scalar1=smt[:, 2:3], scalar2=None,
                                                            op0=ALU.mult),
                     reads=[bsm[2]], writes=[bsm[3]])
                yield
                S.op(S.dve, lambda: nc.vector.tensor_scalar(out=smt[:, 3:4], in0=smt[:, 3:4], scalar1=EPS, scalar2=None, op0=ALU.mult),
                     reads=[bsm[3]], writes=[bsm[3]])
                S.op(S.dve, lambda: nc.vector.scalar_tensor_tensor(out=smt[:, 4:5], in0=smt[:, 1:2], scalar=1.0 / 256,
                                                                   in1=smt[:, 3:4], op0=ALU.mult, op1=ALU.add),
                     reads=[bsm[1], bsm[3]], writes=[bsm[4]])
                yield
                S.op(S.act, lambda: nc.scalar.activation(out=smt[:, 5:6], in_=smt[:, 4:5], func=AF.Ln), reads=[bsm[4]], writes=[bsm[5]])
                S.op(S.act, lambda: nc.scalar.activation(out=smt[:, 6:7], in_=smt[:, 5:6], func=AF.Exp, scale=-0.5),
                     reads=[bsm[5]], writes=[bsm[6]])
                yield
                S.op(S.dve, lambda: nc.vector.scalar_tensor_tensor(out=yb[par][:], in0=bankN[:, 0:256], scalar=smt[:, 6:7],
                                                                   in1=go[par][:], op0=ALU.mult, op1=ALU.mult),
                     reads=[bbN, bsm[6], b_go[par]], writes=[b_yb[par]])
                yield
                for dc in range(2):
                    S.op(S.pe, lambda: nc.tensor.transpose(out=pbT[:, dc * 128:(dc + 1) * 128], in_=yb[par][:, dc * 128:(dc + 1) * 128],
                                                           identity=k.ident_bf),
                         reads=[b_yb[par], k.b_cbf], writes=[bpbT], inc=(dc == 1))
                yield
                S.op(S.dve, lambda: nc.vector.tensor_copy(out=yTt[gi][:, :, tsl], in_=pbT[:, 0:256].rearrange("p (c t) -> p c t", c=2)),
                     reads=[bpbT], writes=[b_y[gi]])
                yield
            S.dma(S.sp, sem_y[gi], k.yt_ml[2 * h:2 * h + 2, :, g * 512:(g + 1) * 512].rearrange("c p t -> p c t"),
                  yTt[gi][:], reads=[b_y[gi]])


def gen_xa(k, st, X, banks, bbanks):
    nc, S = k.nc, k.S
    qx = [k.sb(f"xa_qx{i}", [128, 2, 512], BF16, st) for i in range(2)]
    b_qx = [Buf(), Buf()]
    sem_q = [S.new_sem("d_xaq0"), S.new_sem("d_xaq1")]
    yx = [k.sb(f"xa_yx{i}", [128, 2, 512], BF16, st) for i in range(2)]
    b_yx = [Buf(), Buf()]
    sem_y = [S.new_sem("d_xay0"), S.new_sem("d_xay1")]
    sq = k.sb("xa_sq", [128, 2, 512], BF16, st)
    rq = k.sb("xa_rq", [128, 512], F32, st)
    rq2 = k.sb("xa_rq2", [128, 512], F32, st)
    qn = k.sb("xa_qn", [128, 2, 512], BF16, st)
    pT = k.sb("xa_pT", [128, 2, 512], BF16, st)
    rden = k.sb("xa_rden", [128, 512], F32, st)
    b_sq, b_rq, b_rq2, b_qn, b_pT, b_rden = Buf(), Buf(), Buf(), Buf(), Buf(), Buf()
    qng = k.ppv("qng")
    knT, vmem = X.knT, X.vmem
    (pa, pb_, pc), (ba, bb_, bc) = banks, bbanks
    its = [(h, tt) for h in range(4) for tt in range(NTT)]

    def load(n):
        h, tt = its[n]
        S.dma(S.sp, sem_q[n % 2], qx[n % 2][:], k.qt_x[2 * h:2 * h + 2, :, tt * 512:(tt + 1) * 512].rearrange("c p t -> p c t"),
              writes=[b_qx[n % 2]])
    load(0)
    for n, (h, tt) in enumerate(its):
        i = n % 2
        cols = slice(tt * 512, (tt + 1) * 512)
        if n + 1 < len(its):
            load(n + 1)
        S.op(S.dve, lambda: nc.vector.tensor_tensor(out=sq[:], in0=qx[i][:], in1=qx[i][:], op=ALU.mult), reads=[b_qx[i]], writes=[b_sq])
        yield
        for dc in range(2):
            S.op(S.pe, lambda: nc.tensor.matmul(pa[:], lhsT=k.ones_bf, rhs=sq[:, dc, :], start=(dc == 0), stop=(dc == 1)),
                 reads=[b_sq, k.b_cbf], writes=[ba], inc=(dc == 1))
        yield
        S.op(S.act, lambda: nc.scalar.activation(out=rq[:], in_=pa[:], func=AF.Ln, bias=k.eps_t[:, 0:1], scale=1.0 / 256),
             reads=[ba, k.b_eps], writes=[b_rq])
        yield
        S.op(S.act, lambda: nc.scalar.activation(out=rq2[:], in_=rq[:], func=AF.Exp, scale=-0.5), reads=[b_rq], writes=[b_rq2])
        yield
        for dc in range(2):
            S.op(S.dve, lambda: nc.vector.scalar_tensor_tensor(out=qn[:, dc, :], in0=qx[i][:, dc, :], scalar=qng[:, dc:dc + 1],
                                                               in1=rq2[:], op0=ALU.mult, op1=ALU.mult),
                 reads=[b_qx[i], b_rq2, k.b_pp], writes=[b_qn])
        yield
        for mc, (p, bp) in enumerate(((pa, ba), (pb_, bb_))):
            for dc in range(2):
                S.op(S.pe, lambda: nc.tensor.matmul(p[:], lhsT=knT[:, h, dc, mc * 128:(mc + 1) * 128], rhs=qn[:, dc, :],
                                                    start=(dc == 0), stop=(dc == 1)),
                     reads=[X.b_knT, b_qn], writes=[bp], inc=(dc == 1))
        yield
        for mc, (p, bp) in enumerate(((pa, ba), (pb_, bb_))):
            S.op(S.act, lambda: nc.scalar.activation(out=pT[:, mc, :], in_=p[:], func=AF.Exp, scale=1.0 / 16),
                 reads=[bp], writes=[b_pT])
        yield
        for mc in range(2):
            S.op(S.pe, lambda: nc.tensor.matmul(pc[:], lhsT=k.ones_bf, rhs=pT[:, mc, :], start=(mc == 0), stop=(mc == 1)),
                 reads=[b_pT, k.b_cbf], writes=[bc], inc=(mc == 1))
        for dvc, (p, bp) in enumerate(((pa, ba), (pb_, bb_))):
            for mc in range(2):
                S.op(S.pe, lambda: nc.tensor.matmul(p[:], lhsT=vmem[:, mc, h * 256 + dvc * 128:h * 256 + (dvc + 1) * 128],
                                                    rhs=pT[:, mc, :], start=(mc == 0), stop=(mc == 1)),
                     reads=[X.b_vmem, b_pT], writes=[bp], inc=(mc == 1))
        yield
        S.op(S.dve, lambda: nc.vector.reciprocal(out=rden[:], in_=pc[:]), reads=[bc], writes=[b_rden])
        yield
        for dvc, (p, bp) in enumerate(((pa, ba), (pb_, bb_))):
            S.op(S.dve, lambda: nc.vector.tensor_tensor(out=yx[i][:, dvc, :], in0=p[:], in1=rden[:], op=ALU.mult),
                 reads=[bp, b_rden], writes=[b_yx[i]])
        S.dma(S.sp, sem_y[i], k.yt_x[2 * h:2 * h + 2, :, cols].rearrange("c p t -> p c t"), yx[i][:], reads=[b_yx[i]])
        yield


def stage_mid(k, bg_per_yield=1):
    nc, S = k.nc, k.S
    with ExitStack() as st:
        T = ml_prep(k, st)
        X = xa_prep(k, st)
        with ExitStack() as stp:
            xa_prep(k, st, X, stp, phase=1)
            ml_prep(k, st, T, stp, phase=1)
            xa_prep(k, st, X, stp, phase=2)
            S.barrier()
        fg = gen_sb(k, st, k.zz, k.b_zz, k.oo, k.b_oo)
        bankN = k.pb[1][:].bitcast(F32)
        pb0f = k.pb[0][:].bitcast(F32)
        bankB = pb0f[:, 128:512]
        bgs = [gen_ml(k, st, T, k.pf[0], k.b_pf[0], bankB, k.b_pb[0], bankN, k.b_pb[1], k.pb[0], k.b_pb[0]),
               gen_xa(k, st, X, (k.pf[0], pb0f, bankN), (k.b_pf[0], k.b_pb[0], k.b_pb[1]))]
        k.dummy_bank = k.pf[1]
        bi = 0
        for _ in fg:
            for _r in range(bg_per_yield):
                while bi < len(bgs):
                    try:
                        next(bgs[bi])
                        break
                    except StopIteration:
                        bi += 1
        while bi < len(bgs):
            for _ in bgs[bi]:
                pass
            bi += 1
        S.barrier()


ALL_STAGES = ("s1", "s2", "mid", "4a", "4b")
_NC_CACHE = {}


def kernel(**inputs):
    inp = {k_: np.asarray(v) for k_, v in inputs.items()}
    if "nc" not in _NC_CACHE:
        _NC_CACHE["nc"] = build(dbg=(), stages=ALL_STAGES)
    nc = _NC_CACHE["nc"]
    pp = make_pp(inp)
    shared = {"pp": pp, "w_in": np.ascontiguousarray(inp["w_in"][0]), "w_mem_kv": np.ascontiguousarray(inp["w_mem_kv"][0]),
              "w_sb_proj": np.ascontiguousarray(inp["w_sb_proj"][0]), "w_ml_proj": np.ascontiguousarray(inp["w_ml_proj"][0]),
              "w_x_proj": np.ascontiguousarray(inp["w_x_proj"][0]), "w_out": np.ascontiguousarray(inp["w_out"][0]),
              "w_ff1": np.ascontiguousarray(inp["w_ff1"][0]), "w_ff2": np.ascontiguousarray(inp["w_ff2"][0])}
    in_maps = []
    for b in range(8):
        m = dict(shared)
        m["x"] = np.ascontiguousarray(inp["x"][b])
        m["mem"] = np.ascontiguousarray(inp["mem"][b])
        in_maps.append(m)
    res = run_bass_kernel_spmd(nc, in_maps, core_ids=list(range(8)))
    return np.stack([np.asarray(r["out"]) for r in res.results], axis=0).astype(np.float32)
```
